# Optimizing a Trainium2 kernel written in Bass

```python
import jax, jax.numpy as jnp
from jax import lax
import numpy as np

D_MODEL = 1024
BATCH = 2
SEQ = 8192
DEPTH = 4
DEC_BATCH = 128
DEC_SEQ = 1
PAST_LEN = 2048
PAGE_SIZE = 128

N_A = DEPTH // 2
N_B = DEPTH - N_A
MIX_HALF = D_MODEL // 2
HG_HEADS = 4
HG_DK = MIX_HALF // HG_HEADS
HG_DV = MIX_HALF // HG_HEADS
HG_CHUNK = 64
SB_HEADS = 4
SB_DH = MIX_HALF // SB_HEADS
SB_BLOCK = 128
SB_BIAS_INIT = -8.0
MEM_HEADS = 4
MEM_DH = MIX_HALF // MEM_HEADS
N_MEM = 256
D_FF = ((8 * D_MODEL + 3 * 256 - 1) // (3 * 256)) * 256
EPS = 1e-6

kernel_name = 'yoco_hgrn2_stickbreak_mem_decoder_step'


def rmsnorm(x, g):
    xf = x.astype(jnp.float32)
    y = xf * lax.rsqrt(jnp.mean(xf * xf, axis=-1, keepdims=True) + EPS)
    return (y * g.astype(jnp.float32)).astype(x.dtype)


def swiglu(h, w_gu, w_down):
    gate, up = jnp.split(h @ w_gu, 2, axis=-1)
    return (jax.nn.silu(gate) * up) @ w_down


def hgrn_lower_bounds(lb_param):
    c = jnp.cumsum(jax.nn.softmax(lb_param.astype(jnp.float32), axis=0), axis=0)
    return c - c[0:1]


def hgrn2_scan(q, k, v, logf, s0):
    B, L, H, DK = q.shape
    DV = v.shape[-1]
    C = min(HG_CHUNK, L)
    pad = (-L) % C
    q, k, v, logf = (a.astype(jnp.float32) for a in (q, k, v, logf))
    if pad:
        pw = ((0, 0), (0, pad), (0, 0), (0, 0))
        q, k, v, logf = (jnp.pad(a, pw) for a in (q, k, v, logf))
    n = (L + pad) // C

    def to_chunks(a):
        return a.reshape(B, n, C, H, a.shape[-1]).transpose(1, 0, 3, 2, 4)

    causal = jnp.tril(jnp.ones((C, C), bool))[None, None, :, :, None]

    def step(S, inp):
        qi, ki, vi, fi = inp
        b = jnp.cumsum(fi, axis=2)
        o_inter = jnp.einsum('bhtk,bhkv->bhtv', qi * jnp.exp(b), S)
        rel = b[:, :, :, None, :] - b[:, :, None, :, :]
        dec = jnp.exp(jnp.where(causal, rel, -jnp.inf))
        att = jnp.einsum('bhtk,bhsk,bhtsk->bhts', qi, ki, dec)
        o = o_inter + jnp.einsum('bhts,bhsv->bhtv', att, vi)
        b_last = b[:, :, -1:, :]
        S_new = jnp.exp(b_last[:, :, 0, :, None]) * S + jnp.einsum('bhsk,bhsv->bhkv', ki * jnp.exp(b_last - b), vi)
        return S_new, o

    S, o = lax.scan(step, s0.astype(jnp.float32), tuple(to_chunks(a) for a in (q, k, v, logf)))
    o = o.transpose(1, 0, 3, 2, 4).reshape(B, n * C, H, DV)[:, :L]
    return o, S


def stick_breaking(q, k, v, bias, q_pos, k_pos):
    B, Lq, H, dh = q.shape
    blk = min(SB_BLOCK, Lq)
    pad = (-Lq) % blk
    qq = q
    if pad:
        qq = jnp.pad(qq, ((0, 0), (0, pad), (0, 0), (0, 0)))
        q_pos = jnp.pad(q_pos, (0, pad))
    nb = (Lq + pad) // blk
    qb = qq.reshape(B, nb, blk, H, dh).transpose(1, 0, 2, 3, 4)
    pb = q_pos.reshape(nb, blk)
    kf = k.astype(jnp.float32)
    vf = v.astype(jnp.float32)
    bf = bias.astype(jnp.float32)[None, :, None, None]
    scale = dh ** -0.5

    def one_block(args):
        qi, pi = args
        z = jnp.einsum('bqhd,bkhd->bhqk', qi.astype(jnp.float32), kf) * scale + bf
        mask = (k_pos[None, :] < pi[:, None])[None, None]
        log_keep = jnp.where(mask, jax.nn.log_sigmoid(-z), 0.0)
        tail = lax.cumsum(log_keep, axis=3, reverse=True) - log_keep
        w = jnp.where(mask, jnp.exp(jax.nn.log_sigmoid(z) + tail), 0.0)
        return jnp.einsum('bhqk,bkhd->bqhd', w, vf)

    o = lax.map(one_block, (qb, pb))
    o = o.transpose(1, 0, 2, 3, 4).reshape(B, nb * blk, H, dh)[:, :Lq]
    return o.astype(q.dtype)


def mem_attend(q, mk, mv):
    s = jnp.einsum('blhd,bmhd->bhlm', q.astype(jnp.float32), mk.astype(jnp.float32)) * (q.shape[-1] ** -0.5)
    p = jax.nn.softmax(s, axis=-1)
    return jnp.einsum('bhlm,bmhd->blhd', p, mv.astype(jnp.float32)).astype(q.dtype)


def a_mixer(hn, w_in, lb, hg_gain, mem_k, mem_v, w_o, s0):
    B, L, _ = hn.shape
    q_h, f_h, i_h, g_h, q_m = jnp.split(hn @ w_in, 5, axis=-1)
    q = jax.nn.silu(q_h).reshape(B, L, HG_HEADS, HG_DK)
    logf = jnp.logaddexp(jnp.log(lb), jnp.log1p(-lb) + jax.nn.log_sigmoid(f_h.astype(jnp.float32)))
    logf = logf.reshape(B, L, HG_HEADS, HG_DK)
    k = -jnp.expm1(logf)
    v = i_h.reshape(B, L, HG_HEADS, HG_DV)
    o, S = hgrn2_scan(q, k, v, logf, s0)
    o = rmsnorm(o.astype(hn.dtype), hg_gain) * jax.nn.silu(g_h).reshape(B, L, HG_HEADS, HG_DV)
    om = mem_attend(q_m.reshape(B, L, MEM_HEADS, MEM_DH), mem_k, mem_v)
    out = jnp.concatenate([o.reshape(B, L, MIX_HALF), om.reshape(B, L, MIX_HALF)], axis=-1) @ w_o
    return out, S


def b_mixer(hn, w_in, bias, k_all, v_all, q_pos, k_pos, mem_k, mem_v, w_o):
    B, L, _ = hn.shape
    q_s, q_m = jnp.split(hn @ w_in, 2, axis=-1)
    os_ = stick_breaking(q_s.reshape(B, L, SB_HEADS, SB_DH), k_all, v_all, bias, q_pos, k_pos)
    om = mem_attend(q_m.reshape(B, L, MEM_HEADS, MEM_DH), mem_k, mem_v)
    return jnp.concatenate([os_.reshape(B, L, MIX_HALF), om.reshape(B, L, MIX_HALF)], axis=-1) @ w_o


def run_trunk(x, mem_k, mem_v, hg_s0, past_k, past_v,
              g_norm, w_in_a, hg_lb, hg_norm, w_in_b, sb_bias, g_kv, w_kv, w_o, w_gu, w_down):
    B, L, _ = x.shape
    past = 0 if past_k is None else past_k.shape[1]
    lbs = hgrn_lower_bounds(hg_lb)
    q_pos = past + jnp.arange(L, dtype=jnp.int32)
    k_pos = jnp.arange(past + L, dtype=jnp.int32)
    h = x
    hg_states = []
    sb_k = sb_v = k_all = v_all = None
    for l in range(DEPTH):
        g = g_norm[l]
        hn = rmsnorm(h, g[0])
        if l < N_A:
            mix, S = a_mixer(hn, w_in_a[l], lbs[l], hg_norm[l], mem_k[l], mem_v[l], w_o[l], hg_s0[l])
            hg_states.append(S.astype(x.dtype))
        else:
            mix = b_mixer(hn, w_in_b[l - N_A], sb_bias[l - N_A], k_all, v_all, q_pos, k_pos,
                          mem_k[l], mem_v[l], w_o[l])
        h = h + rmsnorm(mix, g[1])
        h = h + rmsnorm(swiglu(rmsnorm(h, g[2]), w_gu[l], w_down[l]), g[3])
        if l == N_A - 1:
            sb_k, sb_v = jnp.split(rmsnorm(h, g_kv) @ w_kv, 2, axis=-1)
            sb_k = sb_k.reshape(B, L, SB_HEADS, SB_DH)
            sb_v = sb_v.reshape(B, L, SB_HEADS, SB_DH)
            if past_k is None:
                k_all, v_all = sb_k, sb_v
            else:
                k_all = jnp.concatenate([past_k.astype(sb_k.dtype), sb_k], axis=1)
                v_all = jnp.concatenate([past_v.astype(sb_v.dtype), sb_v], axis=1)
    return h, jnp.stack(hg_states), sb_k, sb_v


def setup_inputs(seed: int = 0) -> dict:
    key = jax.random.key(seed)
    ks = jax.random.split(key, 24)
    f32 = jnp.float32
    n_pages = PAST_LEN // PAGE_SIZE
    n_used = DEC_BATCH * n_pages
    n_pool = n_used + max(1, n_used // 4)

    def nrm(k, shape, s=1.0):
        return jax.random.normal(k, shape, f32) * s

    page_table = jax.random.permutation(ks[8], n_pool)[:n_used].reshape(DEC_BATCH, n_pages).astype(jnp.int32)
    return {
        'x_prompt': nrm(ks[0], (BATCH, SEQ, D_MODEL)),
        'x_sample': nrm(ks[1], (DEC_BATCH, DEC_SEQ, D_MODEL)),
        'mem_prompt': nrm(ks[2], (BATCH, N_MEM, D_MODEL)),
        'cache_sb_k': nrm(ks[3], (n_pool, PAGE_SIZE, SB_HEADS, SB_DH)),
        'cache_sb_v': nrm(ks[4], (n_pool, PAGE_SIZE, SB_HEADS, SB_DH)),
        'cache_mem_k': nrm(ks[5], (DEPTH, DEC_BATCH, N_MEM, MEM_HEADS, MEM_DH)),
        'cache_mem_v': nrm(ks[6], (DEPTH, DEC_BATCH, N_MEM, MEM_HEADS, MEM_DH)),
        'state_hgrn': nrm(ks[7], (N_A, DEC_BATCH, HG_HEADS, HG_DK, HG_DV), 0.5),
        'page_table': page_table,
        'g_norm': 1.0 + nrm(ks[9], (DEPTH, 4, D_MODEL), 0.05),
        'w_in_a': nrm(ks[10], (N_A, D_MODEL, 5 * MIX_HALF), D_MODEL ** -0.5),
        'hg_lb': nrm(ks[11], (N_A, HG_HEADS * HG_DK), 0.5),
        'hg_norm': 1.0 + nrm(ks[12], (N_A, HG_DV), 0.05),
        'w_in_b': nrm(ks[13], (N_B, D_MODEL, 2 * MIX_HALF), D_MODEL ** -0.5),
        'sb_bias': SB_BIAS_INIT + nrm(ks[20], (N_B, SB_HEADS), 0.3),
        'g_kv': 1.0 + nrm(ks[14], (D_MODEL,), 0.05),
        'w_kv': nrm(ks[15], (D_MODEL, 2 * SB_HEADS * SB_DH), D_MODEL ** -0.5),
        'w_mem_kv': nrm(ks[16], (DEPTH, D_MODEL, 2 * MEM_HEADS * MEM_DH), D_MODEL ** -0.5),
        'w_o': nrm(ks[17], (DEPTH, 2 * MIX_HALF, D_MODEL), (2 * MIX_HALF) ** -0.5),
        'w_gu': nrm(ks[18], (DEPTH, D_MODEL, 2 * D_FF), D_MODEL ** -0.5),
        'w_down': nrm(ks[19], (DEPTH, D_FF, D_MODEL), D_FF ** -0.5),
    }


def reference(x_prompt, x_sample, mem_prompt, cache_sb_k, cache_sb_v, cache_mem_k, cache_mem_v, state_hgrn,
              page_table, g_norm, w_in_a, hg_lb, hg_norm, w_in_b, sb_bias, g_kv, w_kv, w_mem_kv, w_o, w_gu, w_down):
    weights = (g_norm, w_in_a, hg_lb, hg_norm, w_in_b, sb_bias, g_kv, w_kv, w_o, w_gu, w_down)
    bp = mem_prompt.shape[0]
    mk_p, mv_p = jnp.split(jnp.einsum('bmd,lde->lbme', mem_prompt, w_mem_kv), 2, axis=-1)
    mem_k_prompt = mk_p.reshape(DEPTH, bp, N_MEM, MEM_HEADS, MEM_DH)
    mem_v_prompt = mv_p.reshape(DEPTH, bp, N_MEM, MEM_HEADS, MEM_DH)
    s0_p = jnp.zeros((N_A, bp, HG_HEADS, HG_DK, HG_DV), jnp.float32)
    y_prompt, state_hgrn_prompt, sb_k_prompt, sb_v_prompt = run_trunk(
        x_prompt, mem_k_prompt, mem_v_prompt, s0_p, None, None, *weights)
    ds, n_pages = page_table.shape
    past_k = cache_sb_k[page_table].reshape(ds, n_pages * PAGE_SIZE, SB_HEADS, SB_DH)
    past_v = cache_sb_v[page_table].reshape(ds, n_pages * PAGE_SIZE, SB_HEADS, SB_DH)
    y_sample, state_hgrn_sample, sb_k_sample, sb_v_sample = run_trunk(
        x_sample, cache_mem_k, cache_mem_v, state_hgrn, past_k, past_v, *weights)
    return (y_prompt, y_sample, state_hgrn_prompt, state_hgrn_sample.astype(state_hgrn.dtype),
            sb_k_prompt, sb_v_prompt, sb_k_sample, sb_v_sample, mem_k_prompt, mem_v_prompt)
```

```python
from contextlib import ExitStack
import os
import numpy as np
import concourse.bass as bass
import concourse.mybir as mybir
from concourse.bass_utils import run_bass_kernel_spmd

F32 = mybir.dt.float32
BF16 = mybir.dt.bfloat16
I32 = mybir.dt.int32
AF = mybir.ActivationFunctionType
ALU = mybir.AluOpType

D = 1024
SEQ = 8192
T = 1024
NG = SEQ // T
DFF = 2816
NS = 16
EPS = 1e-6
SCALE = 128 ** -0.5
STOP = 100
SAMPLE = True
NCORES = 8

CE = ['pe', 'act', 'dve', 'pool']
QE = ['sp', 'act', 'pool']
NSLOT = 8


class Sched:
    def __init__(self, nc, stack):
        self.nc = nc
        self.ops = {e: [] for e in ['pe', 'act', 'dve', 'pool', 'sp']}
        self.count = {e: 0 for e in CE}
        self.known = {e: {} for e in self.ops}
        self.last_w = {}
        self.readers = {}
        self.sem = {}
        for e in CE:
            self.sem[e] = stack.enter_context(nc.semaphore("c_" + e))
        self.dma_n = {q: 0 for q in QE}
        for q in QE:
            for s in range(NSLOT):
                self.sem[(q, s)] = stack.enter_context(nc.semaphore("d_%s%d" % (q, s)))

    def _need(self, eng, toks):
        best = {}
        for (k, v) in toks:
            if v <= 0:
                continue
            if best.get(k, 0) < v:
                best[k] = v
        out = []
        kn = self.known[eng]
        for k, v in best.items():
            if kn.get(k, 0) >= v:
                continue
            kn[k] = v
            out.append((k, v))
        return out

    def add(self, eng, fn, reads=(), writes=(), dma=False):
        toks = []
        for k in reads:
            t = self.last_w.get(k)
            if t is not None:
                toks.append(t)
            if k.startswith('ps'):
                toks.extend(t2 for t2 in self.readers.get(k, ()) if t2[0] != eng)
        for k in writes:
            t = self.last_w.get(k)
            if t is not None:
                toks.append(t)
            toks.extend(self.readers.get(k, ()))
        if eng == 'pe' and not dma:
            toks = [t for t in toks if t[0] != 'pe']
        if dma:
            n = self.dma_n[eng]
            self.dma_n[eng] = n + 1
            slot = (eng, n % NSLOT)
            val = 16 * (n // NSLOT + 1)
            toks.append((slot, val - 16))
            tok = (slot, val)
            inc = 16
        else:
            self.count[eng] += 1
            tok = (eng, self.count[eng])
            inc = 1
        waits = self._need(eng, toks)
        for k in writes:
            self.last_w[k] = tok
            self.readers[k] = []
        for k in reads:
            if k in writes:
                continue
            lst = self.readers.setdefault(k, [])
            lst[:] = [t for t in lst if t[0] != tok[0]] + [tok]
        self.ops[eng].append((waits, fn, tok, inc))
        return tok

    def barrier(self):
        toks = [(e, self.count[e]) for e in CE]
        for q in QE:
            n = self.dma_n[q]
            for s in range(NSLOT):
                cnt = (n - s + NSLOT - 1) // NSLOT if n > s else 0
                toks.append(((q, s), 16 * cnt))
        for e in self.ops:
            waits = self._need(e, list(toks))
            if waits:
                self.ops[e].append((waits, None, None, 0))

    def emit(self):
        nc = self.nc
        self.barrier()
        names = {'pe': 'tensor', 'act': 'scalar', 'dve': 'vector', 'pool': 'gpsimd', 'sp': 'sync'}
        with nc.Block() as block:
            for e, bn in names.items():
                ops = self.ops[e]

                def body(eng, ops=ops):
                    for (waits, fn, tok, inc) in ops:
                        for (k, v) in waits:
                            eng.wait_ge(self.sem[k], v)
                        if fn is not None:
                            ins = fn(eng)
                            ins.then_inc(self.sem[tok[0]], inc)
                getattr(block, bn)(body)


class Arena:
    def __init__(self, ap, nwords):
        self.ap = ap
        self.n = nwords
        self.top = 0

    def f32(self, words):
        a = self.top
        self.top += words
        assert self.top <= self.n, ("arena overflow", self.top, self.n)
        return self.ap[:, a:a + words]

    def bf(self, elems):
        w = (elems + 1) // 2
        return self.f32(w).bitcast(BF16)

    def i32(self, words):
        return self.f32(words).bitcast(I32)


NCONST = 128 + 128 + 128 + 128 + 4 * 512 + T


def host_consts():
    c = np.zeros((128, NCONST), np.float32)
    o = 0
    c[:, o:o + 128] = np.eye(128); o += 128
    c[:, o:o + 128] = 1.0; o += 128
    j = np.arange(128)[:, None]; k = np.arange(128)[None, :]
    c[:, o:o + 128] = (j > k); o += 128
    c[:, o:o + 128] = ((j // 64 == k // 64) & (j <= k)); o += 128
    q = np.arange(512)[None, :]
    for r in range(4):
        c[:, o:o + 512] = ((r * 128 + j) < q); o += 512
    rm = np.ones(T, np.float32); rm[0::64] = 0.0
    c[:, o:o + T] = rm[None, :]; o += T
    assert o == NCONST
    return c


def build():
    nc = bass.Bass("TRN2", target_bir_lowering=False)

    def din(name, shape, dt=F32):
        return nc.dram_tensor(name, list(shape), dt, kind="ExternalInput").ap()

    def dout(name, shape):
        return nc.dram_tensor(name, list(shape), F32, kind="ExternalOutput").ap()

    xp = din("xp", [SEQ, D]); xs = din("xs", [NS, D]); memp = din("memp", [256, D])
    consts = din("consts", [128, NCONST]); params = din("params", [128, 160])
    w_in_a = din("w_in_a", [2, D, 2560]); w_in_b = din("w_in_b", [2, D, D]); w_kv = din("w_kv", [D, D])
    w_mem = din("w_mem", [4, D, D]); w_o = din("w_o", [4, D, D]); w_gu = din("w_gu", [4, D, 2 * DFF])
    w_down = din("w_down", [4, DFF, D])
    yp = dout("yp", [SEQ, D]); stp = dout("stp", [2, 4, 128, 128])
    sbk_p = dout("sbk_p", [SEQ, 512]); sbv_p = dout("sbv_p", [SEQ, 512])
    mk_p = dout("mk_p", [4, 256, 512]); mv_p = dout("mv_p", [4, 256, 512])
    NPOOL = 2560
    pool_k = din("pool_k", [NPOOL * 128, 512]); pool_v = din("pool_v", [NPOOL * 128, 512])
    cmk = din("cmk", [4, NS, 256, 512]); cmv = din("cmv", [4, NS, 256, 512])
    st0 = din("st0", [2, NS, 4, 128, 128]); ptab = din("ptab", [1, NS * 16], I32)
    ys = dout("ys", [NS, D]); sts = dout("sts", [2, NS, 4, 128, 128])
    sbk_s = dout("sbk_s", [NS, 512]); sbv_s = dout("sbv_s", [NS, 512])
    ktc = nc.dram_tensor("ktc", [128, 4, SEQ], BF16).ap()
    vc = nc.dram_tensor("vc", [SEQ, 512], BF16).ap()

    st = ExitStack()
    with st:
        S = Sched(nc, st)
        NW = 52000
        arena_t = st.enter_context(nc.sbuf_tensor("arena", [128, NW], F32))
        A = Arena(arena_t, NW)
        PS = [st.enter_context(nc.psum_tensor("ps%d" % i, [128, 512], F32)) for i in range(8)]

        def PSB(i, n=512, dt=None):
            ap = PS[i][:, 0:n]
            return ap

        def mm(out, lhsT, rhs, start, stop, r, w):
            S.add('pe', lambda e: e.matmul(out, lhsT=lhsT, rhs=rhs, start=start, stop=stop), reads=r, writes=w)

        def tr(out, in_, ident, r, w):
            S.add('pe', lambda e: e.transpose(out, in_, ident), reads=r, writes=w)

        def act(out, in_, func, r, w, bias=None, scale=None):
            kw = {}
            if bias is not None:
                kw['bias'] = bias
            if scale is not None:
                kw['scale'] = scale
            S.add('act', lambda e: e.activation(out=out, in_=in_, func=func, **kw), reads=r, writes=w)

        def tt(eng, out, in0, in1, op, r, w):
            S.add(eng, lambda e: e.tensor_tensor(out=out, in0=in0, in1=in1, op=op), reads=r, writes=w)

        def ts(eng, out, in0, s1, s2, op0, op1, r, w):
            if s2 is None:
                S.add(eng, lambda e: e.tensor_scalar(out=out, in0=in0, scalar1=s1, scalar2=None, op0=op0), reads=r, writes=w)
            else:
                S.add(eng, lambda e: e.tensor_scalar(out=out, in0=in0, scalar1=s1, scalar2=s2, op0=op0, op1=op1), reads=r, writes=w)

        def stt(out, in0, scalar, in1, op0, op1, r, w):
            S.add('dve', lambda e: e.scalar_tensor_tensor(out=out, in0=in0, scalar=scalar, in1=in1, op0=op0, op1=op1), reads=r, writes=w)

        def cp(eng, out, in_, r, w):
            if eng == 'act':
                S.add('act', lambda e: e.copy(out=out, in_=in_), reads=r, writes=w)
            else:
                S.add(eng, lambda e: e.tensor_copy(out=out, in_=in_), reads=r, writes=w)

        def dma(q, out, in_, r, w):
            S.add(q, lambda e: e.dma_start(out=out, in_=in_), reads=r, writes=w, dma=True)

        ident = A.f32(128)
        resetm = A.f32(T)
        prm = A.f32(160)
        gn = prm[:, 0:128]
        lbp = prm[:, 128:136]
        hgn = prm[:, 136:138]
        sbb = prm[:, 138:146]
        gkv = prm[:, 146:154]
        lb1 = A.f32(4); oml1 = A.f32(4)
        ones_ff = A.f32(128); U_ff = A.f32(128)
        iota_c = prm[:, 154:155]
        identb = A.bf(128); onesb = A.bf(128); Ub = A.bf(128); hmaskb = A.bf(128)
        sbmaskb = [A.bf(512) for r in range(4)]
        h = A.f32(8 * T).rearrange("p (c t) -> p c t", c=8)
        MKT = [A.bf(4 * 256).rearrange("p (h m) -> p h m", h=4) for l in range(4)]
        MV = [A.bf(2 * 512).rearrange("p (t c) -> p t c", t=2) for l in range(4)]
        Sst = [[A.f32(128) for hd in range(4)] for l in range(2)]
        base_top = A.top
        hn = A.bf(8 * T).rearrange("p (c t) -> p c t", c=8)
        mixw = A.f32(4 * T)
        mix = mixw.bitcast(BF16).rearrange("p (c t) -> p c t", c=8)
        wp = [A.bf(8 * 512).rearrange("p (c n) -> p c n", c=8) for i in range(2)]
        ytile = A.f32(8 * 512).rearrange("p (c t) -> p c t", c=8)
        sqt = A.bf(8 * 512).rearrange("p (c t) -> p c t", c=8)
        rstd = A.f32(512)
        tmpf = A.f32(512)
        mix_top = A.top
        print("arena persistent", base_top, "common", mix_top)

        ctop = A.top
        cst = A.f32(NCONST)
        o = 0
        ident_t = cst[:, o:o + 128]; o += 128
        ones_f = cst[:, o:o + 128]; o += 128
        U_f = cst[:, o:o + 128]; o += 128
        hmask_f = cst[:, o:o + 128]; o += 128
        sbmask_f = [cst[:, o + r * 512:o + (r + 1) * 512] for r in range(4)]; o += 2048
        resetm_t = cst[:, o:o + T]; o += T
        dma('sp', cst, consts, [], ['cst0'])
        dma('sp', prm, params, [], ['prm'])
        cp('dve', ident, ident_t, ['cst0'], ['cst'])
        cp('dve', resetm, resetm_t, ['cst0'], ['cst'])
        cp('dve', identb, ident_t, ['cst0'], ['identb'])
        cp('dve', ones_ff, ones_f, ['cst0'], ['cst'])
        cp('dve', U_ff, U_f, ['cst0'], ['cst'])
        cp('dve', onesb, ones_f, ['cst0'], ['onesb'])
        cp('dve', Ub, U_f, ['cst0'], ['Ub'])
        cp('dve', hmaskb, hmask_f, ['cst0'], ['hmaskb'])
        for r in range(4):
            cp('dve', sbmaskb[r], sbmask_f[r], ['cst0'], ['sbmaskb'])
        S.barrier()
        A.top = ctop
        tt('dve', lb1, lbp[:, 4:8], lbp[:, 0:4], ALU.subtract, ['prm'], ['lb1'])
        act(lb1, lb1, AF.Sigmoid, ['lb1'], ['lb1'])
        ts('dve', oml1, lb1, -1.0, 1.0, ALU.mult, ALU.add, ['lb1'], ['oml1'])
        for l in range(2):
            for hd in range(4):
                S.add('pool', lambda e, l=l, hd=hd: e.memset(Sst[l][hd], 0.0), writes=['S%d%d' % (l, hd)])

        wq = {'n': 0, 's': 0}

        WSTAGE = False
        if WSTAGE:
            wstage = [A.f32(8 * 512).rearrange("p (c n) -> p c n", c=8) for i in range(2)]

        def wload(dst, src_ap, key, pat="(c p) n -> p c n"):
            if not WSTAGE:
                S.add('pool', lambda e: e.dma_start(out=dst, in_=src_ap.rearrange(pat, p=128)),
                      reads=[], writes=[key], dma=True)
            else:
                i = wq['s'] % 2
                wq['s'] += 1
                kc, n = dst.shape[1], dst.shape[2]
                stg = wstage[i][:, 0:kc, 0:n] if kc <= 8 else None
                assert stg is not None
                S.add('sp', lambda e: e.dma_start(out=stg, in_=src_ap.rearrange(pat, p=128)),
                      reads=[], writes=['wstage%d' % i], dma=True)
                S.add('pool', lambda e: e.tensor_copy(out=dst, in_=stg), reads=['wstage%d' % i], writes=[key])

        def load_w(src_ap, kc, ncols):
            i = wq['n'] % 2
            wq['n'] += 1
            dst = wp[i][:, 0:kc, 0:ncols]
            key = 'wp%d' % i
            wload(dst, src_ap, key)
            return dst, key

        def sumsq_rstd(src_tile, nch, r, inv_n, bank=2, ncol=512):
            act(sqt[:, 0:nch, 0:ncol], src_tile, AF.Square, r, ['sqt'])
            for c in range(nch):
                mm(PS[bank][:, 0:ncol], onesb, sqt[:, c, 0:ncol], c == 0, c == nch - 1, ['sqt', 'onesb'], ['ps%d' % bank])
            act(tmpf[:, 0:ncol], PS[bank][:, 0:ncol], AF.Sqrt, ['ps%d' % bank], ['tmpf'], bias=EPS, scale=inv_n)
            S.add('dve', lambda e: e.reciprocal(out=rstd[:, 0:ncol], in_=tmpf[:, 0:ncol]), reads=['tmpf'], writes=['rstd'])

        def norm_to_bf(dst, l, i, tiles, gvec=None):
            for t2, (t0, n) in enumerate(tiles):
                sl = slice(t0, t0 + n)
                sumsq_rstd(h[:, :, sl], 8, ['h%d' % t2], 1.0 / D, ncol=n)
                for c in range(8):
                    g = gvec[:, c:c + 1] if gvec is not None else gn[:, (l * 4 + i) * 8 + c:(l * 4 + i) * 8 + c + 1]
                    stt(dst[:, c, sl], h[:, c, sl], g, rstd[:, 0:n], ALU.mult, ALU.mult, ['h%d' % t2, 'rstd', 'prm'], ['hn%d' % t2])

        def postnorm_add(l, i, t2, t0, n):
            sl = slice(t0, t0 + n)
            sumsq_rstd(ytile[:, :, 0:n], 8, ['ytile'], 1.0 / D, ncol=n)
            for c in range(8):
                g = gn[:, (l * 4 + i) * 8 + c:(l * 4 + i) * 8 + c + 1]
                stt(ytile[:, c, 0:n], ytile[:, c, 0:n], g, rstd[:, 0:n], ALU.mult, ALU.mult, ['ytile', 'rstd', 'prm'], ['ytile'])
                tt('pool', h[:, c, sl], h[:, c, sl], ytile[:, c, 0:n], ALU.add, ['ytile', 'h%d' % t2], ['h%d' % t2])

        def mem_setup():
            top = A.top
            mraw = A.f32(2 * 1024).rearrange("p (t d) -> p t d", t=2)
            memT = A.bf(8 * 256).rearrange("p (c m) -> p c m", c=8)
            mko = A.f32(512)
            dma('sp', mraw, memp.rearrange("(t p) d -> p t d", p=128), [], ['mraw'])
            for mt in range(2):
                for c in range(8):
                    tr(PS[3][:, 0:128], mraw[:, mt, c * 128:(c + 1) * 128], ident, ['mraw', 'cst'], ['ps3'])
                    cp('dve', memT[:, c, mt * 128:(mt + 1) * 128], PS[3][:, 0:128], ['ps3'], ['memT'])
            for l in range(4):
                for kv in range(2):
                    wpc, wk = load_w(w_mem[l][:, kv * 512:(kv + 1) * 512], 8, 512)
                    outd = mk_p if kv == 0 else mv_p
                    for mt in range(2):
                        for c in range(8):
                            mm(PS[0][:, :], memT[:, c, mt * 128:(mt + 1) * 128], wpc[:, c, :], c == 0, c == 7, ['memT', wk], ['ps0'])
                        cp('act', mko, PS[0][:, :], ['ps0'], ['mko'])
                        if kv == 1:
                            cp('dve', MV[l][:, mt, :], PS[0][:, :], ['ps0'], ['MV%d' % l])
                        dma('sp', outd[l][mt * 128:(mt + 1) * 128, :], mko, ['mko'], [])
                    if kv == 0:
                        for hd in range(4):
                            for c in range(8):
                                mm(PS[1][:, 0:256], wpc[:, c, hd * 128:(hd + 1) * 128], memT[:, c, :], c == 0, c == 7, ['memT', wk], ['ps1'])
                            cp('dve', MKT[l][:, hd, :], PS[1][:, 0:256], ['ps1'], ['MKT%d' % l])
            S.barrier()
            A.top = top

        def mem_attn(l, hd, QM, TT, etile):
            for t2 in range(TT):
                sl = slice(t2 * 512, (t2 + 1) * 512)
                for mt in range(2):
                    mm(PS[4 + mt][:, :], MKT[l][:, hd, mt * 128:(mt + 1) * 128], QM[:, sl], True, True, ['QM', 'MKT%d' % l], ['ps%d' % (4 + mt)])
                    act(etile[mt], PS[4 + mt][:, :], AF.Exp, ['ps%d' % (4 + mt)], ['et%d' % mt], scale=SCALE)
                for mt in range(2):
                    mm(PS[6][:, :], MV[l][:, mt, hd * 128:(hd + 1) * 128], etile[mt], mt == 0, mt == 1, ['et%d' % mt, 'MV%d' % l], ['ps6'])
                for mt in range(2):
                    mm(PS[7][:, :], onesb, etile[mt], mt == 0, mt == 1, ['et%d' % mt, 'onesb'], ['ps7'])
                S.add('dve', lambda e: e.reciprocal(out=tmpf, in_=PS[7][:, :]), reads=['ps7'], writes=['tmpf'])
                tt('dve', mix[:, 4 + hd, sl], PS[6][:, :], tmpf, ALU.mult, ['ps6', 'tmpf'], ['mix'])

        def out_proj(l, tiles):
            top = A.top
            wo = A.bf(8 * 1024).rearrange("p (c n) -> p c n", c=8)
            for half in range(2):
                S.add('pool', lambda e, half=half: e.dma_start(out=wo[:, :, half * 512:(half + 1) * 512],
                                                               in_=w_o[l][:, half * 512:(half + 1) * 512].rearrange("(c p) n -> p c n", p=128)),
                      reads=[], writes=['wo'], dma=True)
            for t2, (t0, n) in enumerate(tiles):
                sl = slice(t0, t0 + n)
                for nn in range(8):
                    b = nn % 2
                    for c in range(8):
                        mm(PS[b][:, 0:n], wo[:, c, nn * 128:(nn + 1) * 128], mix[:, c, sl], c == 0, c == 7, ['wo', 'mix'], ['ps%d' % b])
                    cp('act', ytile[:, nn, 0:n], PS[b][:, 0:n], ['ps%d' % b], ['ytile'])
                postnorm_add(l, 1, t2, t0, n)
            S.barrier()
            A.top = top

        def ffn(l, tiles):
            top = A.top
            Tt = tiles[-1][0] + tiles[-1][1]
            actb = A.bf(22 * Tt).rearrange("p (j t) -> p j t", j=22)
            wd = [A.bf(22 * 128).rearrange("p (j n) -> p j n", j=22) for i in range(2)]
            sg = A.f32(512)
            norm_to_bf(hn, l, 2, tiles)
            for jp in range(11):
                i = wq['n'] % 2
                wq['n'] += 1
                key = 'wp%d' % i
                for gu in range(2):
                    S.add('pool', lambda e, i=i, gu=gu, jp=jp: e.dma_start(
                        out=wp[i][:, :, gu * 256:(gu + 1) * 256],
                        in_=w_gu[l][:, gu * DFF + jp * 256:gu * DFF + (jp + 1) * 256].rearrange("(c p) n -> p c n", p=128)),
                        reads=[], writes=[key], dma=True)
                for jj in range(2):
                    j = jp * 2 + jj
                    for t2, (t0, n) in enumerate(tiles):
                        sl = slice(t0, t0 + n)
                        for c in range(8):
                            mm(PS[0][:, 0:n], wp[i][:, c, jj * 128:(jj + 1) * 128], hn[:, c, sl], c == 0, c == 7, [key, 'hn%d' % t2], ['ps0'])
                        for c in range(8):
                            mm(PS[1][:, 0:n], wp[i][:, c, 256 + jj * 128:256 + (jj + 1) * 128], hn[:, c, sl], c == 0, c == 7, [key, 'hn%d' % t2], ['ps1'])
                        act(sg[:, 0:n], PS[0][:, 0:n], AF.Silu, ['ps0'], ['sg'])
                        tt('dve', actb[:, j, sl], PS[1][:, 0:n], sg[:, 0:n], ALU.mult, ['ps1', 'sg'], ['actb'])
            for nn in range(8):
                i = nn % 2
                key = 'wd%d' % i
                S.add('pool', lambda e, i=i, nn=nn: e.dma_start(
                    out=wd[i], in_=w_down[l][:, nn * 128:(nn + 1) * 128].rearrange("(j p) n -> p j n", p=128)),
                    reads=[], writes=[key], dma=True)
                for t2, (t0, n) in enumerate(tiles):
                    sl = slice(t0, t0 + n)
                    b = 2 + (nn * len(tiles) + t2) % 2
                    for j in range(22):
                        mm(PS[b][:, 0:n], wd[i][:, j, :], actb[:, j, sl], j == 0, j == 21, [key, 'actb'], ['ps%d' % b])
                    cp('act', y2[t2][:, nn, 0:n], PS[b][:, 0:n], ['ps%d' % b], ['y2_%d' % t2])
            for t2, (t0, n) in enumerate(tiles):
                sl = slice(t0, t0 + n)
                sumsq_rstd(y2[t2][:, :, 0:n], 8, ['y2_%d' % t2], 1.0 / D, bank=4, ncol=n)
                for c in range(8):
                    g = gn[:, (l * 4 + 3) * 8 + c:(l * 4 + 3) * 8 + c + 1]
                    stt(y2[t2][:, c, 0:n], y2[t2][:, c, 0:n], g, rstd[:, 0:n], ALU.mult, ALU.mult, ['y2_%d' % t2, 'rstd', 'prm'], ['y2_%d' % t2])
                    tt('pool', h[:, c, sl], h[:, c, sl], y2[t2][:, c, 0:n], ALU.add, ['y2_%d' % t2, 'h%d' % t2], ['h%d' % t2])
            S.barrier()
            A.top = top

        y2 = [ytile, mixw.rearrange("p (c t) -> p c t", c=8)]
        common_top = A.top

        def a_mixer(l, g, TT):
            top = A.top
            Tg = TT * 512
            NT = Tg // 128
            NCK = Tg // 64
            Vt = A.bf(NT * 512).rearrange("p (t c) -> p t c", t=NT)
            QS = A.f32(Tg); LF = A.f32(Tg); KK = A.f32(Tg); TM = A.f32(Tg)
            QT = A.bf(Tg); KT = A.bf(Tg); KH = A.bf(Tg); G = A.bf(Tg); QM = A.bf(Tg)
            O = A.f32(Tg)
            SB16 = A.bf((NCK + 1) * 128).rearrange("p (k v) -> p k v", k=NCK + 1)
            KHT = A.bf(128); AM = A.bf(128)
            dec = A.f32(NCK)
            et = [A.bf(512) for i in range(2)]
            norm_to_bf(hn, l, 0, PT)
            wpc, wk = load_w(w_in_a[l][:, 1024:1536], 8, 512)
            for t in range(NT):
                b = t % 2
                for c in range(8):
                    mm(PS[b][:, :], hn[:, c, t * 128:(t + 1) * 128], wpc[:, c, :], c == 0, c == 7, ['hn%d' % (t // 4), wk], ['ps%d' % b])
                cp('act', Vt[:, t, :], PS[b][:, :], ['ps%d' % b], ['Vt'])
            for hd in range(4):
                i = wq['n'] % 2
                wq['n'] += 1
                key = 'wp%d' % i
                for pi, off in enumerate([0, 512, 1536, 2048]):
                    S.add('pool', lambda e, i=i, pi=pi, off=off, hd=hd: e.dma_start(
                        out=wp[i][:, :, pi * 128:(pi + 1) * 128],
                        in_=w_in_a[l][:, off + hd * 128:off + (hd + 1) * 128].rearrange("(c p) n -> p c n", p=128)),
                        reads=[], writes=[key], dma=True)
                lbc = lb1[:, hd:hd + 1]
                omc = oml1[:, hd:hd + 1]
                for t2 in range(TT):
                    sl = slice(t2 * 512, (t2 + 1) * 512)
                    for pi in range(4):
                        b = pi % 2
                        for c in range(8):
                            mm(PS[b][:, :], wp[i][:, c, pi * 128:(pi + 1) * 128], hn[:, c, sl], c == 0, c == 7, [key, 'hn%d' % t2], ['ps%d' % b])
                        if pi == 0:
                            act(QS[:, sl], PS[b][:, :], AF.Silu, ['ps%d' % b], ['QS'])
                        elif pi == 1:
                            act(TM[:, sl], PS[b][:, :], AF.Sigmoid, ['ps%d' % b], ['TM'])
                        elif pi == 2:
                            act(G[:, sl], PS[b][:, :], AF.Silu, ['ps%d' % b], ['G'])
                        else:
                            cp('act', QM[:, sl], PS[b][:, :], ['ps%d' % b], ['QM'])
                if l == 1:
                    ts('dve', TM, TM, omc, lbc, ALU.mult, ALU.add, ['TM', 'lb1', 'oml1'], ['TM'])
                act(LF, TM, AF.Ln, ['TM'], ['LF'])
                ts('dve', KK, TM, -1.0, 1.0, ALU.mult, ALU.add, ['TM'], ['KK'])
                S.add('dve', lambda e: e.tensor_tensor_scan(out=TM, data0=resetm[:, 0:Tg], data1=LF, initial=0.0, op0=ALU.mult, op1=ALU.add),
                      reads=['LF', 'cst'], writes=['TM'])
                act(LF, TM, AF.Exp, ['TM'], ['LF'])
                tt('dve', QT, QS, LF, ALU.mult, ['QS', 'LF'], ['QT'])
                act(LF, TM, AF.Exp, ['TM', 'QT'], ['LF'], scale=-1.0)
                tt('dve', KT, KK, LF, ALU.mult, ['KK', 'LF'], ['KT'])
                b3 = TM.rearrange("p (k s) -> p k s", s=64)
                act(dec, b3[:, :, 63], AF.Exp, ['TM'], ['dec'])
                tt('dve', LF.rearrange("p (k s) -> p k s", s=64), b3[:, :, 63:64].to_broadcast([128, NCK, 64]), b3, ALU.subtract, ['TM', 'KT'], ['LF'])
                act(LF, LF, AF.Exp, ['LF'], ['LF'])
                tt('dve', KH, KK, LF, ALU.mult, ['KK', 'LF'], ['KH'])
                skey = 'S%d%d' % (l, hd)
                Sm = Sst[l][hd]
                cp('act', SB16[:, 0, :], Sm, [skey], ['SB16'])
                for t in range(NT):
                    tr(PS[3][:, 0:64].bitcast(BF16), KH[:, t * 128:(t + 1) * 128], identb, ['KH', 'identb'], ['ps3'])
                    cp('dve', KHT, PS[3][:, 0:64].bitcast(BF16), ['ps3'], ['KHT'])
                    for cc in range(2):
                        ck = t * 2 + cc
                        mm(PS[2][:, 0:128], KHT[cc * 64:(cc + 1) * 64, :], Vt[cc * 64:(cc + 1) * 64, t, hd * 128:(hd + 1) * 128], True, True, ['KHT', 'Vt'], ['ps2'])
                        stt(Sm, Sm, dec[:, ck:ck + 1], PS[2][:, 0:128], ALU.mult, ALU.add, [skey, 'dec', 'ps2'], [skey])
                        cp('act', SB16[:, ck + 1, :], Sm, [skey], ['SB16'])
                if g == NG - 1:
                    dma('sp', stp[l][hd], Sm, [skey], [])
                for t in range(NT):
                    tsl = slice(t * 128, (t + 1) * 128)
                    mm(PS[4][:, 0:128], KT[:, tsl], QT[:, tsl], True, True, ['KT', 'QT'], ['ps4'])
                    tt('dve', AM, PS[4][:, 0:128], hmaskb, ALU.mult, ['ps4', 'hmaskb'], ['AM'])
                    ob = 5 + (t // 4) % 2
                    oc = (t % 4) * 128
                    mm(PS[ob][:, oc:oc + 128], Vt[:, t, hd * 128:(hd + 1) * 128], AM, True, False, ['Vt', 'AM'], ['ps%d' % ob])
                    for cc in range(2):
                        ck = t * 2 + cc
                        mm(PS[ob][:, oc + cc * 64:oc + (cc + 1) * 64], SB16[:, ck, :], QT[:, t * 128 + cc * 64:t * 128 + (cc + 1) * 64], False, cc == 1,
                           ['SB16', 'QT'], ['ps%d' % ob])
                    if t % 4 == 3:
                        t2 = t // 4
                        cp('act', O[:, t2 * 512:(t2 + 1) * 512], PS[ob][:, :], ['ps%d' % ob], ['O'])
                for t2 in range(TT):
                    sl = slice(t2 * 512, (t2 + 1) * 512)
                    act(sqt[:, 0, :], O[:, sl], AF.Square, ['O'], ['sqt'])
                    mm(PS[7][:, :], onesb, sqt[:, 0, :], True, True, ['sqt', 'onesb'], ['ps7'])
                    act(tmpf, PS[7][:, :], AF.Sqrt, ['ps7'], ['tmpf'], bias=EPS, scale=1.0 / 128)
                    S.add('dve', lambda e: e.reciprocal(out=rstd, in_=tmpf), reads=['tmpf'], writes=['rstd'])
                    stt(O[:, sl], O[:, sl], hgn[:, l:l + 1], rstd, ALU.mult, ALU.mult, ['O', 'rstd', 'prm'], ['O'])
                    tt('dve', mix[:, hd, sl], O[:, sl], G[:, sl], ALU.mult, ['O', 'G'], ['mix'])
                mem_attn(l, hd, QM, TT, et)
            S.barrier()
            A.top = top

        def kv_proj(g, TT):
            top = A.top
            Tg = TT * 512
            NT = Tg // 128
            ko = [A.f32(512) for i in range(2)]
            vb = [A.bf(512) for i in range(2)]
            ktb = A.bf(Tg)
            norm_to_bf(hn, 0, 0, PT, gvec=gkv)
            for kv in range(2):
                wpc, wk = load_w(w_kv[:, kv * 512:(kv + 1) * 512], 8, 512)
                outd = sbk_p if kv == 0 else sbv_p
                for t in range(NT):
                    b = t % 2
                    r0 = g * T + t * 128
                    for c in range(8):
                        mm(PS[b][:, :], hn[:, c, t * 128:(t + 1) * 128], wpc[:, c, :], c == 0, c == 7, ['hn%d' % (t // 4), wk], ['ps%d' % b])
                    cp('act', ko[b], PS[b][:, :], ['ps%d' % b], ['ko%d' % b])
                    dma('sp', outd[r0:r0 + 128, :], ko[b], ['ko%d' % b], [])
                    if kv == 1:
                        cp('dve', vb[b], PS[b][:, :], ['ps%d' % b], ['vb%d' % b])
                        dma('sp', vc[r0:r0 + 128, :], vb[b], ['vb%d' % b], ['vc'])
                if kv == 0:
                    for hd in range(4):
                        for t2 in range(TT):
                            sl = slice(t2 * 512, (t2 + 1) * 512)
                            b = 2 + t2 % 2
                            for c in range(8):
                                mm(PS[b][:, :], wpc[:, c, hd * 128:(hd + 1) * 128], hn[:, c, sl], c == 0, c == 7, [wk, 'hn%d' % t2], ['ps%d' % b])
                            cp('dve', ktb[:, sl], PS[b][:, :], ['ps%d' % b], ['ktb'])
                        dma('sp', ktc[:, hd, g * T:g * T + Tg], ktb, ['ktb'], ['ktc'])
            S.barrier()
            A.top = top

        def b_mixer(l, g, TT):
            top = A.top
            Tg = TT * 512
            lb_ = l - 2
            QB = A.bf(4 * Tg).rearrange("p (h t) -> p h t", h=4)
            QM = A.bf(Tg)
            et = [A.bf(512) for i in range(2)]
            KBLK = 2048
            KB = [A.bf(KBLK) for i in range(2)]
            VB = [A.bf(16 * 128).rearrange("p (t v) -> p t v", t=16) for i in range(2)]
            ef = A.f32(512); spb = A.bf(512); t1 = A.f32(512); t2b = A.f32(512); t3 = A.f32(512); wb = A.bf(512)
            carry = A.f32(512)
            norm_to_bf(hn, l, 0, PT)
            for half in range(2):
                wpc, wk = load_w(w_in_b[lb_][:, half * 512:(half + 1) * 512], 8, 512)
                for hd in range(4):
                    for t2 in range(TT):
                        sl = slice(t2 * 512, (t2 + 1) * 512)
                        b = (hd * TT + t2) % 2
                        for c in range(8):
                            mm(PS[b][:, :], wpc[:, c, hd * 128:(hd + 1) * 128], hn[:, c, sl], c == 0, c == 7, [wk, 'hn%d' % t2], ['ps%d' % b])
                        if half == 0:
                            cp('act', QB[:, hd, sl], PS[b][:, :], ['ps%d' % b], ['QB'])
                        else:
                            cp('act', QM[:, sl], PS[b][:, :], ['ps%d' % b], ['QM'])
                    if half == 1:
                        mem_attn(l, hd, QM, TT, et)
            n = 0
            for t2 in range(TT):
                sl = slice(t2 * 512, (t2 + 1) * 512)
                P0 = g * T + t2 * 512
                nkeys = P0 + 512
                nkt = nkeys // 128
                for hd in range(4):
                    bias = sbb[:, lb_ * 4 + hd:lb_ * 4 + hd + 1]
                    S.add('pool', lambda e: e.memset(carry, 0.0), writes=['carry'])
                    nblk = (nkeys + KBLK - 1) // KBLK
                    for kb in range(nblk - 1, -1, -1):
                        k0 = kb * KBLK
                        k1 = min(nkeys, k0 + KBLK)
                        i = n % 2
                        n += 1
                        dma('sp', KB[i][:, 0:k1 - k0], ktc[:, hd, k0:k1], ['ktc'], ['KB%d' % i])
                        dma('sp', VB[i][:, 0:(k1 - k0) // 128, :], vc[k0:k1, hd * 128:(hd + 1) * 128].rearrange("(t p) v -> p t v", p=128), ['vc'], ['VB%d' % i])
                        for kt in range(k1 // 128 - 1, k0 // 128 - 1, -1):
                            kl = kt - k0 // 128
                            rdiag = kt - (nkt - 4)
                            zb = 4 + (kt % 2)
                            mm(PS[zb][:, :], KB[i][:, kl * 128:(kl + 1) * 128], QB[:, hd, sl], True, True, ['KB%d' % i, 'QB'], ['ps%d' % zb])
                            act(ef, PS[zb][:, :], AF.Exp, ['ps%d' % zb], ['ef'], bias=bias, scale=SCALE)
                            act(spb, ef, AF.Ln, ['ef'], ['spb'], bias=1.0)
                            if rdiag >= 0:
                                tt('pool', spb, spb, sbmaskb[rdiag], ALU.mult, ['spb', 'sbmaskb'], ['spb'])
                            mm(PS[6][:, :], Ub, spb, True, True, ['Ub', 'spb'], ['ps6'])
                            mm(PS[7][:, :], onesb, spb, True, True, ['onesb', 'spb'], ['ps7'])
                            stt(t1, PS[zb][:, :], SCALE, spb, ALU.mult, ALU.subtract, ['ps%d' % zb, 'spb'], ['t1'])
                            tt('dve', t2b, PS[6][:, :], carry, ALU.add, ['ps6', 'carry'], ['t2b'])
                            tt('pool', t3, t1, t2b, ALU.subtract, ['t1', 't2b'], ['t3'])
                            act(wb, t3, AF.Exp, ['t3'], ['wb'], bias=bias)
                            if rdiag >= 0:
                                tt('pool', wb, wb, sbmaskb[rdiag], ALU.mult, ['wb', 'sbmaskb'], ['wb'])
                            tt('dve', carry, PS[7][:, :], carry, ALU.add, ['ps7', 'carry'], ['carry'])
                            mm(PS[3][:, :], VB[i][:, kl, :], wb, kt == nkt - 1, kt == 0, ['VB%d' % i, 'wb'], ['ps3'])
                    cp('act', mix[:, hd, sl], PS[3][:, :], ['ps3'], ['mix'])
            S.barrier()
            A.top = top


        ST = [(0, NS)]
        NC2 = 4 * NS
        AX = mybir.AxisListType.X

        def flat2(ap3):
            return ap3.rearrange("p c t -> p (c t)")

        def mem_attn_s(l, QM2):
            top = A.top
            Kc = [A.f32(1024).rearrange("p (t c) -> p t c", t=2) for i in range(2)]
            Vc = [A.f32(1024).rearrange("p (t c) -> p t c", t=2) for i in range(2)]
            KcT = A.f32(1024).rearrange("p (h m) -> p h m", h=4)
            E2 = A.f32(16).rearrange("p (c t) -> p c t", t=2)
            om32 = A.f32(32)
            dn = A.f32(4); rd = A.f32(4)
            for s_ in range(NS):
                i = s_ % 2
                dma('sp', Kc[i], cmk[l][s_].rearrange("(t p) c -> p t c", p=128), [], ['Kc%d' % i])
                dma('sp', Vc[i], cmv[l][s_].rearrange("(t p) c -> p t c", p=128), [], ['Vc%d' % i])
                for hd in range(4):
                    for mt in range(2):
                        q = hd * 2 + mt
                        bk = 4 + q // 4
                        tr(PS[bk][:, (q % 4) * 128:(q % 4 + 1) * 128], Kc[i][:, mt, hd * 128:(hd + 1) * 128], ident, ['Kc%d' % i, 'cst'], ['ps%d' % bk])
                cp('act', KcT[:, 0:2, :], PS[4][:, :].rearrange("p (h m) -> p h m", h=2), ['ps4'], ['KcT'])
                cp('dve', KcT[:, 2:4, :], PS[5][:, :].rearrange("p (h m) -> p h m", h=2), ['ps5'], ['KcT'])
                for hd in range(4):
                    for mt in range(2):
                        q = hd * 2 + mt
                        mm(PS[6][:, q * 2:q * 2 + 2], KcT[:, hd, mt * 128:(mt + 1) * 128], QM2[:, hd * NS + s_, :], True, True, ['KcT', 'QM2'], ['ps6'])
                act(flat2(E2), PS[6][:, 0:16], AF.Exp, ['ps6'], ['E2'], scale=SCALE)
                for hd in range(4):
                    for mt in range(2):
                        mm(PS[7][:, hd * 2:hd * 2 + 2], Vc[i][:, mt, hd * 128:(hd + 1) * 128], E2[:, hd * 2 + mt, :], mt == 0, mt == 1, ['Vc%d' % i, 'E2'], ['ps7'])
                mm(PS[7][:, 16:32], ones_ff, flat2(E2), True, True, ['E2', 'cst'], ['ps7'])
                cp('act', om32, PS[7][:, 0:32], ['ps7'], ['om32'])
                cs4 = om32[:, 16:32].rearrange("p (h m t) -> p h m t", h=4, m=2)
                tt('dve', dn, cs4[:, :, 0, 0], cs4[:, :, 1, 0], ALU.add, ['om32'], ['dn'])
                S.add('dve', lambda e: e.reciprocal(out=rd, in_=dn), reads=['dn'], writes=['rd'])
                tt('dve', mix[:, 4:8, s_], om32[:, 0:8].rearrange("p (h t) -> p h t", t=2)[:, :, 0], rd, ALU.mult, ['om32', 'rd'], ['mix'])
            S.barrier()
            A.top = top

        def a_mixer_s(l):
            top = A.top
            Q2 = A.f32(NC2 * 2).rearrange("p (c t) -> p c t", t=2)
            QM2 = A.f32(NC2 * 2).rearrange("p (c t) -> p c t", t=2)
            Ff = A.f32(NC2); Kf = A.f32(NC2); Gf = A.f32(NC2); Vf = A.f32(NC2); Of = A.f32(NC2)
            Ktok = A.f32(512); Vtok = A.f32(512)
            Vexp = A.f32(NS * 128)
            Vexp3 = Vexp.rearrange("p (s v) -> p s v", s=NS)
            S0 = A.f32(NS * 128).rearrange("p (s v) -> p s v", s=NS)
            sq = A.bf(NC2)
            S.add('pool', lambda e: e.memset(flat2(Q2), 0.0), writes=['Q2'])
            S.add('pool', lambda e: e.memset(flat2(QM2), 0.0), writes=['QM2'])
            norm_to_bf(hn, l, 0, ST)
            for pi in range(5):
                wpc, wk = load_w(w_in_a[l][:, pi * 512:(pi + 1) * 512], 8, 512)
                for hd in range(4):
                    b = hd % 2
                    cs = slice(hd * NS, (hd + 1) * NS)
                    for c in range(8):
                        mm(PS[b][:, 0:NS], wpc[:, c, hd * 128:(hd + 1) * 128], hn[:, c, 0:NS], c == 0, c == 7, [wk, 'hn0'], ['ps%d' % b])
                    if pi == 0:
                        act(Q2[:, cs, 0], PS[b][:, 0:NS], AF.Silu, ['ps%d' % b], ['Q2'])
                    elif pi == 1:
                        act(Ff[:, cs], PS[b][:, 0:NS], AF.Sigmoid, ['ps%d' % b], ['Ff'])
                    elif pi == 2:
                        cp('act', Vf[:, cs], PS[b][:, 0:NS], ['ps%d' % b], ['Vf'])
                    elif pi == 3:
                        act(Gf[:, cs], PS[b][:, 0:NS], AF.Silu, ['ps%d' % b], ['Gf'])
                    else:
                        cp('act', QM2[:, cs, 0], PS[b][:, 0:NS], ['ps%d' % b], ['QM2'])
            if l == 1:
                for hd in range(4):
                    cs = slice(hd * NS, (hd + 1) * NS)
                    ts('dve', Ff[:, cs], Ff[:, cs], oml1[:, hd:hd + 1], lb1[:, hd:hd + 1], ALU.mult, ALU.add, ['Ff', 'lb1', 'oml1'], ['Ff'])
            ts('dve', Kf, Ff, -1.0, 1.0, ALU.mult, ALU.add, ['Ff'], ['Kf'])
            for hd in range(4):
                cs = slice(hd * NS, (hd + 1) * NS)
                tr(PS[3][0:NS, 0:128], Kf[:, cs], ident, ['Kf', 'cst'], ['ps3'])
                cp('dve', Ktok[0:NS, hd * 128:(hd + 1) * 128], PS[3][0:NS, 0:128], ['ps3'], ['Ktok'])
                tr(PS[3][0:NS, 128:256], Vf[:, cs], ident, ['Vf', 'cst'], ['ps3'])
                cp('dve', Vtok[0:NS, hd * 128:(hd + 1) * 128], PS[3][0:NS, 128:256], ['ps3'], ['Vtok'])
            for hd in range(4):
                c0 = hd * NS
                dma('sp', S0, st0[l][:, hd].rearrange("s k v -> k s v"), [], ['S0'])
                tt('dve', Vexp3[0:NS], Vtok[0:NS, hd * 128:(hd + 1) * 128].unsqueeze(1).to_broadcast([NS, NS, 128]),
                   ident[0:NS, 0:NS].unsqueeze(2).to_broadcast([NS, NS, 128]), ALU.mult, ['Vtok', 'cst'], ['Vexp'])
                for q4 in range(NS // 4):
                    b = 4 + q4 % 2
                    mm(PS[b][:, :], Ktok[0:NS, hd * 128:(hd + 1) * 128], Vexp[0:NS, q4 * 512:(q4 + 1) * 512], True, True, ['Ktok', 'Vexp'], ['ps%d' % b])
                    for s4 in range(4):
                        s_ = q4 * 4 + s4
                        stt(S0[:, s_, :], S0[:, s_, :], Ff[:, c0 + s_:c0 + s_ + 1], PS[b][:, s4 * 128:(s4 + 1) * 128], ALU.mult, ALU.add,
                            ['S0', 'Ff', 'ps%d' % b], ['S0'])
                dma('sp', sts[l][:, hd].rearrange("s k v -> k s v"), S0, ['S0'], [])
                for s_ in range(NS):
                    col = c0 + s_
                    mm(PS[6][:, 2 * col:2 * col + 2], S0[:, s_, :], Q2[:, col, :], True, True, ['S0', 'Q2'], ['ps6'])
            cp('act', Of, PS[6][:, 0:2 * NC2].rearrange("p (c t) -> p c t", t=2)[:, :, 0], ['ps6'], ['Of'])
            act(sq, Of, AF.Square, ['Of'], ['sq'])
            mm(PS[7][:, 0:NC2], onesb, sq, True, True, ['sq', 'onesb'], ['ps7'])
            act(tmpf[:, 0:NC2], PS[7][:, 0:NC2], AF.Sqrt, ['ps7'], ['tmpf'], bias=EPS, scale=1.0 / 128)
            S.add('dve', lambda e: e.reciprocal(out=rstd[:, 0:NC2], in_=tmpf[:, 0:NC2]), reads=['tmpf'], writes=['rstd'])
            stt(Of, Of, hgn[:, l:l + 1], rstd[:, 0:NC2], ALU.mult, ALU.mult, ['Of', 'rstd', 'prm'], ['Of'])
            tt('dve', mix[:, 0:4, 0:NS], Of.rearrange("p (h s) -> p h s", h=4), Gf.rearrange("p (h s) -> p h s", h=4), ALU.mult, ['Of', 'Gf'], ['mix'])
            mem_attn_s(l, QM2)
            S.barrier()
            A.top = top

        def b_mixer_s(l, idxt):
            top = A.top
            lb_ = l - 2
            Q2 = A.f32(NC2 * 2).rearrange("p (c t) -> p c t", t=2)
            QM2 = A.f32(NC2 * 2).rearrange("p (c t) -> p c t", t=2)
            KP = [A.f32(512) for i in range(2)]
            KT = [A.f32(512) for i in range(2)]
            VP = [A.f32(512) for i in range(2)]
            ZB = A.f32(64); EE = A.f32(64); SP = A.f32(64); CR = A.f32(64)
            TLCS = A.f32(128)
            W2 = A.f32(128).rearrange("p (c t) -> p c t", t=2)
            OSs = A.f32(128)
            os4 = A.f32(4)
            r3 = lambda ap: ap.rearrange("p (j h) -> p j h", h=4)
            S.add('pool', lambda e: e.memset(flat2(Q2), 0.0), writes=['Q2'])
            S.add('pool', lambda e: e.memset(flat2(QM2), 0.0), writes=['QM2'])
            S.add('pool', lambda e: e.memset(flat2(W2), 0.0), writes=['W2'])
            S.add('pool', lambda e: e.memset(CR, 0.0), writes=['CR'])
            norm_to_bf(hn, l, 0, ST)
            for pi in range(2):
                wpc, wk = load_w(w_in_b[lb_][:, pi * 512:(pi + 1) * 512], 8, 512)
                for hd in range(4):
                    b = hd % 2
                    cs = slice(hd * NS, (hd + 1) * NS)
                    for c in range(8):
                        mm(PS[b][:, 0:NS], wpc[:, c, hd * 128:(hd + 1) * 128], hn[:, c, 0:NS], c == 0, c == 7, [wk, 'hn0'], ['ps%d' % b])
                    cp('act', (Q2 if pi == 0 else QM2)[:, cs, 0], PS[b][:, 0:NS], ['ps%d' % b], ['Q2' if pi == 0 else 'QM2'])
            n = 0
            for s_ in range(NS):
                for j in range(16):
                    i = n % 2
                    n += 1
                    col = s_ * 16 + j
                    S.add('pool', lambda e, i=i, col=col: e.indirect_dma_start(
                        out=KP[i], out_offset=None, in_=pool_k, in_offset=bass.IndirectOffsetOnAxis(ap=idxt[:, col:col + 1], axis=0)),
                        reads=['idxt'], writes=['KP%d' % i], dma=True)
                    for hd in range(4):
                        tr(PS[4][:, hd * 128:(hd + 1) * 128], KP[i][:, hd * 128:(hd + 1) * 128], ident, ['KP%d' % i, 'cst'], ['ps4'])
                    cp('act' if j % 2 else 'dve', KT[i], PS[4][:, :], ['ps4'], ['KT%d' % i])
                    for hd in range(4):
                        q = j * 4 + hd
                        mm(PS[6][:, 2 * q:2 * q + 2], KT[i][:, hd * 128:(hd + 1) * 128], Q2[:, hd * NS + s_, :], True, True, ['KT%d' % i, 'Q2'], ['ps6'])
                Zv = PS[6][:, 0:128].rearrange("p (j h t) -> p j h t", j=16, h=4)[:, :, :, 0]
                stt(r3(ZB), Zv, SCALE, sbb[:, lb_ * 4:lb_ * 4 + 4].unsqueeze(1).to_broadcast([128, 16, 4]), ALU.mult, ALU.add, ['ps6', 'prm'], ['ZB'])
                act(EE, ZB, AF.Exp, ['ZB'], ['EE'])
                act(SP, EE, AF.Ln, ['EE'], ['SP'], bias=1.0)
                mm(PS[7][:, 0:64], U_ff, SP, True, True, ['SP', 'cst'], ['ps7'])
                mm(PS[7][:, 64:128], ones_ff, SP, True, True, ['SP', 'cst'], ['ps7'])
                cp('act', TLCS, PS[7][:, 0:128], ['ps7'], ['TLCS'])
                CS3 = r3(TLCS[:, 64:128]); CR3 = r3(CR)
                for j in range(14, -1, -1):
                    tt('dve', CR3[:, j, :], CR3[:, j + 1, :], CS3[:, j + 1, :], ALU.add, ['CR', 'TLCS'], ['CR'])
                tt('dve', ZB, ZB, SP, ALU.subtract, ['ZB', 'SP'], ['ZB'])
                tt('dve', ZB, ZB, TLCS[:, 0:64], ALU.subtract, ['ZB', 'TLCS'], ['ZB'])
                tt('dve', ZB, ZB, CR, ALU.subtract, ['ZB', 'CR'], ['ZB'])
                act(W2[:, :, 0], ZB, AF.Exp, ['ZB'], ['W2'])
                for j in range(16):
                    i = n % 2
                    n += 1
                    col = s_ * 16 + j
                    S.add('pool', lambda e, i=i, col=col: e.indirect_dma_start(
                        out=VP[i], out_offset=None, in_=pool_v, in_offset=bass.IndirectOffsetOnAxis(ap=idxt[:, col:col + 1], axis=0)),
                        reads=['idxt'], writes=['VP%d' % i], dma=True)
                    for hd in range(4):
                        q = j * 4 + hd
                        mm(PS[5][:, 2 * q:2 * q + 2], VP[i][:, hd * 128:(hd + 1) * 128], W2[:, q, :], True, True, ['VP%d' % i, 'W2'], ['ps5'])
                cp('act', OSs, PS[5][:, 0:128], ['ps5'], ['OSs'])
                S.add('dve', lambda e: e.tensor_reduce(out=os4, in_=OSs.rearrange("p (j h t) -> p h j t", j=16, h=4)[:, :, :, 0], axis=AX, op=ALU.add),
                      reads=['OSs'], writes=['os4'])
                cp('dve', mix[:, 0:4, s_], os4, ['os4'], ['mix'])
            mem_attn_s(l, QM2)
            S.barrier()
            A.top = top

        def kv_proj_s():
            top = A.top
            ko = A.f32(512)
            norm_to_bf(hn, 0, 0, ST, gvec=gkv)
            for kv in range(2):
                wpc, wk = load_w(w_kv[:, kv * 512:(kv + 1) * 512], 8, 512)
                for c in range(8):
                    mm(PS[0][0:NS, :], hn[:, c, 0:NS], wpc[:, c, :], c == 0, c == 7, ['hn0', wk], ['ps0'])
                cp('act', ko[0:NS, :], PS[0][0:NS, :], ['ps0'], ['ko'])
                dma('sp', sbk_s if kv == 0 else sbv_s, ko[0:NS, :], ['ko'], [])
            S.barrier()
            A.top = top

        def sample_group():
            top = A.top
            idxt = A.i32(NS * 16)
            idf = A.f32(NS * 16)
            xraw = A.f32(1024)
            dma('sp', idxt, ptab.partition_broadcast(128), [], ['idxt'])
            cp('dve', idf, idxt, ['idxt'], ['idf'])
            ts('dve', idf, idf, 128.0, iota_c, ALU.mult, ALU.add, ['idf', 'prm'], ['idf'])
            cp('dve', idxt, idf, ['idf'], ['idxt'])
            dma('sp', xraw[0:NS, :], xs, [], ['xraw'])
            for c in range(8):
                b = c % 2
                tr(PS[b][:, 0:NS], xraw[0:NS, c * 128:(c + 1) * 128], ident[0:NS, 0:NS], ['xraw', 'cst'], ['ps%d' % b])
                cp('dve' if c % 2 else 'act', h[:, c, 0:NS], PS[b][:, 0:NS], ['ps%d' % b], ['h0'])
            S.barrier()
            for l in range(4):
                if l < 2:
                    a_mixer_s(l)
                else:
                    b_mixer_s(l, idxt)
                out_proj(l, ST)
                ffn(l, ST)
                if l == 1:
                    kv_proj_s()
            for c in range(8):
                b = c % 2
                tr(PS[b][0:NS, 0:128], h[:, c, 0:NS], ident, ['h0', 'cst'], ['ps%d' % b])
                cp('dve' if c % 2 else 'act', xraw[0:NS, c * 128:(c + 1) * 128], PS[b][0:NS, 0:128], ['ps%d' % b], ['xraw'])
            dma('sp', ys, xraw[0:NS, :], ['xraw'], [])
            S.barrier()
            A.top = top

        if STOP != -2:
            mem_setup()
        if SAMPLE:
            sample_group()
        TT = T // 512
        PT = [(i * 512, 512) for i in range(TT)]
        for g in range(NG if STOP != -1 else 0):
            top = A.top
            xraw = [A.f32(1024) for i in range(2)]
            for t in range(T // 128):
                i = t % 2
                r0 = g * T + t * 128
                dma('sp', xraw[i], xp[r0:r0 + 128, :], [], ['xraw%d' % i])
                for c in range(8):
                    b = c % 2
                    tr(PS[b][:, 0:128], xraw[i][:, c * 128:(c + 1) * 128], ident, ['xraw%d' % i, 'cst'], ['ps%d' % b])
                    cp('dve' if c % 2 else 'act', h[:, c, t * 128:(t + 1) * 128], PS[b][:, 0:128], ['ps%d' % b], ['h%d' % (t // 4)])
            S.barrier()
            A.top = top
            for l in range(4):
                if STOP <= 1 + 3 * l:
                    break
                if l < 2:
                    a_mixer(l, g, TT)
                else:
                    b_mixer(l, g, TT)
                if STOP <= 2 + 3 * l:
                    break
                out_proj(l, PT)
                if STOP <= 3 + 3 * l:
                    break
                ffn(l, PT)
                if l == 1:
                    kv_proj(g, TT)
            top = A.top
            yraw = [A.f32(1024) for i in range(2)]
            for t in range(T // 128):
                i = t % 2
                r0 = g * T + t * 128
                for c in range(8):
                    b = c % 2
                    tr(PS[b][:, 0:128], h[:, c, t * 128:(t + 1) * 128], ident, ['h%d' % (t // 4), 'cst'], ['ps%d' % b])
                    cp('dve' if c % 2 else 'act', yraw[i][:, c * 128:(c + 1) * 128], PS[b][:, 0:128], ['ps%d' % b], ['yraw%d' % i])
                dma('sp', yp[r0:r0 + 128, :], yraw[i], ['yraw%d' % i], [])
            S.barrier()
            A.top = top
        S.emit()
    return nc


_NC = None


def make_params(g_norm, hg_lb, hg_norm, sb_bias, g_kv):
    params = np.zeros((128, 160), np.float32)
    params[:, 0:128] = g_norm.reshape(4, 4, 8, 128).transpose(3, 0, 1, 2).reshape(128, 128)
    params[:, 128:136] = hg_lb.reshape(2, 4, 128).transpose(2, 0, 1).reshape(128, 8)
    params[:, 136:138] = hg_norm.T
    params[:, 138:146] = np.broadcast_to(sb_bias.reshape(1, 8), (128, 8))
    params[:, 146:154] = g_kv.reshape(8, 128).T
    params[:, 154] = np.arange(128)
    return params


def kernel(x_prompt, x_sample, mem_prompt, cache_sb_k, cache_sb_v, cache_mem_k, cache_mem_v, state_hgrn,
           page_table, g_norm, w_in_a, hg_lb, hg_norm, w_in_b, sb_bias, g_kv, w_kv, w_mem_kv, w_o, w_gu, w_down):
    global _NC
    f = lambda a: np.ascontiguousarray(np.asarray(a, dtype=np.float32))
    if _NC is None:
        _NC = build()
    nc = _NC
    consts = host_consts()
    g_norm = f(g_norm); hg_lb = f(hg_lb); hg_norm = f(hg_norm); sb_bias = f(sb_bias); g_kv = f(g_kv)
    params = make_params(g_norm, hg_lb, hg_norm, sb_bias, g_kv)
    x_prompt = f(x_prompt); mem_prompt = f(mem_prompt); x_sample = f(x_sample)
    npool = 2560
    if cache_sb_k is None:
        pk = np.zeros((npool * 128, 512), np.float32); pv = pk
        cache_mem_k = np.zeros((4, 128, 256, 4, 128), np.float32); cache_mem_v = cache_mem_k
        state_hgrn = np.zeros((2, 128, 4, 128, 128), np.float32)
        page_table = np.zeros((128, 16), np.int32)
    else:
        pk = f(cache_sb_k).reshape(npool * 128, 512); pv = f(cache_sb_v).reshape(npool * 128, 512)
    cache_mem_k = f(cache_mem_k); cache_mem_v = f(cache_mem_v); state_hgrn = f(state_hgrn)
    page_table = np.ascontiguousarray(np.asarray(page_table, dtype=np.int32))
    shared = {"consts": consts, "params": params, "w_in_a": f(w_in_a), "w_in_b": f(w_in_b), "w_kv": f(w_kv),
              "w_mem": f(w_mem_kv), "w_o": f(w_o), "w_gu": f(w_gu), "w_down": f(w_down), "pool_k": pk, "pool_v": pv}
    in_maps = []
    for c in range(NCORES):
        b = c % 2
        sl = slice(c * NS, (c + 1) * NS)
        m = dict(shared)
        m["xp"] = x_prompt[b]
        m["xs"] = np.ascontiguousarray(x_sample[sl, 0, :])
        m["memp"] = mem_prompt[b]
        m["cmk"] = np.ascontiguousarray(cache_mem_k[:, sl]).reshape(4, NS, 256, 512)
        m["cmv"] = np.ascontiguousarray(cache_mem_v[:, sl]).reshape(4, NS, 256, 512)
        m["st0"] = np.ascontiguousarray(state_hgrn[:, sl])
        m["ptab"] = np.ascontiguousarray(page_table[sl]).reshape(1, NS * 16)
        in_maps.append(m)
    res = run_bass_kernel_spmd(nc, in_maps, core_ids=list(range(NCORES)))
    r = list(res.results)
    while len(r) < 8:
        r.append(r[0])
    y_prompt = np.stack([r[0]["yp"], r[1]["yp"]]).astype(np.float32)
    st_p = np.stack([r[0]["stp"], r[1]["stp"]], axis=1).astype(np.float32)
    sbk = np.stack([r[0]["sbk_p"], r[1]["sbk_p"]]).reshape(2, SEQ, 4, 128).astype(np.float32)
    sbv = np.stack([r[0]["sbv_p"], r[1]["sbv_p"]]).reshape(2, SEQ, 4, 128).astype(np.float32)
    mk = np.stack([r[0]["mk_p"], r[1]["mk_p"]], axis=1).reshape(4, 2, 256, 4, 128).astype(np.float32)
    mv = np.stack([r[0]["mv_p"], r[1]["mv_p"]], axis=1).reshape(4, 2, 256, 4, 128).astype(np.float32)
    y_sample = np.concatenate([r[c]["ys"] for c in range(8)], axis=0).reshape(128, 1, D).astype(np.float32)
    st_s = np.concatenate([r[c]["sts"] for c in range(8)], axis=1).astype(np.float32)
    sbk_s = np.concatenate([r[c]["sbk_s"] for c in range(8)], axis=0).reshape(128, 1, 4, 128).astype(np.float32)
    sbv_s = np.concatenate([r[c]["sbv_s"] for c in range(8)], axis=0).reshape(128, 1, 4, 128).astype(np.float32)
    return (y_prompt, y_sample, st_p, st_s, sbk, sbv, sbk_s, sbv_s, mk, mv)
```

```python
from contextlib import ExitStack
import os
import numpy as np
import concourse.bass as bass
import concourse.mybir as mybir
from concourse.bass_utils import run_bass_kernel_spmd

F32 = mybir.dt.float32
BF16 = mybir.dt.bfloat16
I32 = mybir.dt.int32
AF = mybir.ActivationFunctionType
ALU = mybir.AluOpType

D = 1024
SEQ = 8192
T = 1024
NG = SEQ // T
DFF = 2816
NS = 16
EPS = 1e-6
SCALE = 128 ** -0.5
STOP = 100
KBLK_G = 2048
SAMPLE = True
NCORES = 8

CE = ['pe', 'act', 'dve', 'pool']
QE = ['sp', 'act', 'pool']
NSLOT = 8


class Sched:
    def __init__(self, nc, stack):
        self.nc = nc
        self.ops = {e: [] for e in ['pe', 'act', 'dve', 'pool', 'sp']}
        self.count = {e: 0 for e in CE}
        self.known = {e: {} for e in self.ops}
        self.last_w = {}
        self.readers = {}
        self.sem = {}
        for e in CE:
            self.sem[e] = stack.enter_context(nc.semaphore("c_" + e))
        self.dma_n = {q: 0 for q in QE}
        for q in QE:
            for s in range(NSLOT):
                self.sem[(q, s)] = stack.enter_context(nc.semaphore("d_%s%d" % (q, s)))

    def _need(self, eng, toks):
        best = {}
        for (k, v) in toks:
            if v <= 0:
                continue
            if best.get(k, 0) < v:
                best[k] = v
        out = []
        kn = self.known[eng]
        for k, v in best.items():
            if kn.get(k, 0) >= v:
                continue
            kn[k] = v
            out.append((k, v))
        return out

    def add(self, eng, fn, reads=(), writes=(), dma=False):
        toks = []
        for k in reads:
            t = self.last_w.get(k)
            if t is not None:
                toks.append(t)
            if k.startswith('ps'):
                toks.extend(t2 for t2 in self.readers.get(k, ()) if t2[0] != eng)
        for k in writes:
            t = self.last_w.get(k)
            if t is not None:
                toks.append(t)
            toks.extend(self.readers.get(k, ()))
        if eng == 'pe' and not dma:
            toks = [t for t in toks if t[0] != 'pe']
        if dma:
            n = self.dma_n[eng]
            self.dma_n[eng] = n + 1
            slot = (eng, n % NSLOT)
            val = 16 * (n // NSLOT + 1)
            toks.append((slot, val - 16))
            tok = (slot, val)
            inc = 16
        else:
            self.count[eng] += 1
            tok = (eng, self.count[eng])
            inc = 1
        waits = self._need(eng, toks)
        for k in writes:
            self.last_w[k] = tok
            self.readers[k] = []
        for k in reads:
            if k in writes:
                continue
            lst = self.readers.setdefault(k, [])
            lst[:] = [t for t in lst if t[0] != tok[0]] + [tok]
        self.ops[eng].append((waits, fn, tok, inc))
        return tok

    def barrier(self):
        toks = [(e, self.count[e]) for e in CE]
        for q in QE:
            n = self.dma_n[q]
            for s in range(NSLOT):
                cnt = (n - s + NSLOT - 1) // NSLOT if n > s else 0
                toks.append(((q, s), 16 * cnt))
        for e in self.ops:
            waits = self._need(e, list(toks))
            if waits:
                self.ops[e].append((waits, None, None, 0))

    def emit(self):
        nc = self.nc
        self.barrier()
        names = {'pe': 'tensor', 'act': 'scalar', 'dve': 'vector', 'pool': 'gpsimd', 'sp': 'sync'}
        with nc.Block() as block:
            for e, bn in names.items():
                ops = self.ops[e]

                def body(eng, ops=ops):
                    for (waits, fn, tok, inc) in ops:
                        for (k, v) in waits:
                            eng.wait_ge(self.sem[k], v)
                        if fn is not None:
                            ins = fn(eng)
                            ins.then_inc(self.sem[tok[0]], inc)
                getattr(block, bn)(body)


class Arena:
    def __init__(self, ap, nwords):
        self.ap = ap
        self.n = nwords
        self.top = 0

    def f32(self, words):
        a = self.top
        self.top += words
        assert self.top <= self.n, ("arena overflow", self.top, self.n)
        return self.ap[:, a:a + words]

    def bf(self, elems):
        w = (elems + 1) // 2
        return self.f32(w).bitcast(BF16)

    def i32(self, words):
        return self.f32(words).bitcast(I32)


NCONST = 128 + 128 + 128 + 128 + 4 * 512 + T + 256


def host_consts():
    c = np.zeros((128, NCONST), np.float32)
    o = 0
    c[:, o:o + 128] = np.eye(128); o += 128
    c[:, o:o + 128] = 1.0; o += 128
    j = np.arange(128)[:, None]; k = np.arange(128)[None, :]
    c[:, o:o + 128] = (j > k); o += 128
    c[:, o:o + 128] = ((j // 64 == k // 64) & (j <= k)); o += 128
    q = np.arange(512)[None, :]
    for r in range(4):
        c[:, o:o + 512] = ((r * 128 + j) < q); o += 512
    rm = np.ones(T, np.float32); rm[0::64] = 0.0
    c[:, o:o + T] = rm[None, :]; o += T
    c[:, o:o + 128] = -1.0 * (j >= k); o += 128
    c[:, o:o + 128] = -1.0; o += 128
    assert o == NCONST
    return c


def build():
    nc = bass.Bass("TRN2", target_bir_lowering=False)

    def din(name, shape, dt=F32):
        return nc.dram_tensor(name, list(shape), dt, kind="ExternalInput").ap()

    def dout(name, shape):
        return nc.dram_tensor(name, list(shape), F32, kind="ExternalOutput").ap()

    xp = din("xp", [SEQ, D]); xs = din("xs", [NS, D]); memp = din("memp", [256, D])
    consts = din("consts", [128, NCONST]); params = din("params", [128, 160])
    w_in_a = din("w_in_a", [2, D, 2560]); w_in_b = din("w_in_b", [2, D, D]); w_kv = din("w_kv", [D, D])
    w_mem = din("w_mem", [4, D, D]); w_o = din("w_o", [4, D, D]); w_gu = din("w_gu", [4, D, 2 * DFF])
    w_down = din("w_down", [4, DFF, D])
    yp = dout("yp", [SEQ, D]); stp = dout("stp", [2, 4, 128, 128])
    sbk_p = dout("sbk_p", [SEQ, 512]); sbv_p = dout("sbv_p", [SEQ, 512])
    mk_p = dout("mk_p", [4, 256, 512]); mv_p = dout("mv_p", [4, 256, 512])
    NPOOL = 2560
    pool_k = din("pool_k", [NPOOL * 128, 512]); pool_v = din("pool_v", [NPOOL * 128, 512])
    cmk = din("cmk", [4, NS, 256, 512]); cmv = din("cmv", [4, NS, 256, 512])
    st0 = din("st0", [2, NS, 4, 128, 128]); ptab = din("ptab", [1, NS * 16], I32)
    ys = dout("ys", [NS, D]); sts = dout("sts", [2, NS, 4, 128, 128])
    sbk_s = dout("sbk_s", [NS, 512]); sbv_s = dout("sbv_s", [NS, 512])
    ktc = nc.dram_tensor("ktc", [128, 4, SEQ], BF16).ap()
    vc = nc.dram_tensor("vc", [SEQ, 512], BF16).ap()

    st = ExitStack()
    with st:
        S = Sched(nc, st)
        NW = 52000
        arena_t = st.enter_context(nc.sbuf_tensor("arena", [128, NW], F32))
        A = Arena(arena_t, NW)
        PS = [st.enter_context(nc.psum_tensor("ps%d" % i, [128, 512], F32)) for i in range(8)]

        def PSB(i, n=512, dt=None):
            ap = PS[i][:, 0:n]
            return ap

        def mm(out, lhsT, rhs, start, stop, r, w):
            S.add('pe', lambda e: e.matmul(out, lhsT=lhsT, rhs=rhs, start=start, stop=stop), reads=r, writes=w)

        def mm3(out, lhsT, rhs, start, stop, r, w):
            S.add('pe', lambda e: e.matmul(out, lhsT=lhsT, rhs=rhs, start=start, stop=stop, skip_group_check=True), reads=r, writes=w)

        def tr(out, in_, ident, r, w):
            S.add('pe', lambda e: e.transpose(out, in_, ident), reads=r, writes=w)

        def act(out, in_, func, r, w, bias=None, scale=None):
            kw = {}
            if bias is not None:
                kw['bias'] = bias
            if scale is not None:
                kw['scale'] = scale
            S.add('act', lambda e: e.activation(out=out, in_=in_, func=func, **kw), reads=r, writes=w)

        def tt(eng, out, in0, in1, op, r, w):
            S.add(eng, lambda e: e.tensor_tensor(out=out, in0=in0, in1=in1, op=op), reads=r, writes=w)

        def ts(eng, out, in0, s1, s2, op0, op1, r, w):
            if s2 is None:
                S.add(eng, lambda e: e.tensor_scalar(out=out, in0=in0, scalar1=s1, scalar2=None, op0=op0), reads=r, writes=w)
            else:
                S.add(eng, lambda e: e.tensor_scalar(out=out, in0=in0, scalar1=s1, scalar2=s2, op0=op0, op1=op1), reads=r, writes=w)

        def stt(out, in0, scalar, in1, op0, op1, r, w):
            S.add('dve', lambda e: e.scalar_tensor_tensor(out=out, in0=in0, scalar=scalar, in1=in1, op0=op0, op1=op1), reads=r, writes=w)

        def cp(eng, out, in_, r, w):
            if eng == 'act':
                S.add('act', lambda e: e.copy(out=out, in_=in_), reads=r, writes=w)
            else:
                S.add(eng, lambda e: e.tensor_copy(out=out, in_=in_), reads=r, writes=w)

        def dma(q, out, in_, r, w):
            S.add(q, lambda e: e.dma_start(out=out, in_=in_), reads=r, writes=w, dma=True)

        ident = A.f32(128)
        resetm = A.f32(T)
        prm = A.f32(160)
        gn = prm[:, 0:128]
        lbp = prm[:, 128:136]
        hgn = prm[:, 136:138]
        sbb = prm[:, 138:146]
        gkv = prm[:, 146:154]
        lb1 = A.f32(4); oml1 = A.f32(4)
        ones_ff = A.f32(128); U_ff = A.f32(128)
        iota_c = prm[:, 154:155]
        identb = A.bf(128); onesb = A.bf(128); Ub = A.bf(128); hmaskb = A.bf(128); NUb = A.bf(128); NOb = A.bf(128)
        sbmaskb = [A.bf(512) for r in range(4)]
        h = A.f32(8 * T).rearrange("p (c t) -> p c t", c=8)
        MKT = [A.bf(4 * 256).rearrange("p (h m) -> p h m", h=4) for l in range(4)]
        MV = [A.bf(2 * 512).rearrange("p (t c) -> p t c", t=2) for l in range(4)]
        Sst = [[A.f32(128) for hd in range(4)] for l in range(2)]
        base_top = A.top
        hn = A.bf(8 * T).rearrange("p (c t) -> p c t", c=8)
        mixw = A.f32(4 * T)
        mix = mixw.bitcast(BF16).rearrange("p (c t) -> p c t", c=8)
        wp = [A.bf(8 * 512).rearrange("p (c n) -> p c n", c=8) for i in range(2)]
        ytile = A.f32(8 * 512).rearrange("p (c t) -> p c t", c=8)
        sqt = A.bf(8 * 512).rearrange("p (c t) -> p c t", c=8)
        rstd = A.f32(512)
        tmpf = A.f32(512)
        mix_top = A.top
        print("arena persistent", base_top, "common", mix_top)

        ctop = A.top
        cst = A.f32(NCONST)
        o = 0
        ident_t = cst[:, o:o + 128]; o += 128
        ones_f = cst[:, o:o + 128]; o += 128
        U_f = cst[:, o:o + 128]; o += 128
        hmask_f = cst[:, o:o + 128]; o += 128
        sbmask_f = [cst[:, o + r * 512:o + (r + 1) * 512] for r in range(4)]; o += 2048
        resetm_t = cst[:, o:o + T]; o += T
        nup_f = cst[:, o:o + 128]; o += 128
        nones_f = cst[:, o:o + 128]; o += 128
        dma('sp', cst, consts, [], ['cst0'])
        dma('sp', prm, params, [], ['prm'])
        cp('dve', ident, ident_t, ['cst0'], ['cst'])
        cp('dve', resetm, resetm_t, ['cst0'], ['cst'])
        cp('dve', identb, ident_t, ['cst0'], ['identb'])
        cp('dve', ones_ff, ones_f, ['cst0'], ['cst'])
        cp('dve', U_ff, U_f, ['cst0'], ['cst'])
        cp('dve', onesb, ones_f, ['cst0'], ['onesb'])
        cp('dve', Ub, U_f, ['cst0'], ['Ub'])
        cp('dve', NUb, nup_f, ['cst0'], ['NUb'])
        cp('dve', NOb, nones_f, ['cst0'], ['NOb'])
        cp('dve', hmaskb, hmask_f, ['cst0'], ['hmaskb'])
        for r in range(4):
            cp('dve', sbmaskb[r], sbmask_f[r], ['cst0'], ['sbmaskb'])
        S.barrier()
        A.top = ctop
        tt('dve', lb1, lbp[:, 4:8], lbp[:, 0:4], ALU.subtract, ['prm'], ['lb1'])
        act(lb1, lb1, AF.Sigmoid, ['lb1'], ['lb1'])
        ts('dve', oml1, lb1, -1.0, 1.0, ALU.mult, ALU.add, ['lb1'], ['oml1'])
        for l in range(2):
            for hd in range(4):
                S.add('pool', lambda e, l=l, hd=hd: e.memset(Sst[l][hd], 0.0), writes=['S%d%d' % (l, hd)])

        wq = {'n': 0, 's': 0}

        WSTAGE = False
        if WSTAGE:
            wstage = [A.f32(8 * 512).rearrange("p (c n) -> p c n", c=8) for i in range(2)]

        def wload(dst, src_ap, key, pat="(c p) n -> p c n"):
            if not WSTAGE:
                S.add('pool', lambda e: e.dma_start(out=dst, in_=src_ap.rearrange(pat, p=128)),
                      reads=[], writes=[key], dma=True)
            else:
                i = wq['s'] % 2
                wq['s'] += 1
                kc, n = dst.shape[1], dst.shape[2]
                stg = wstage[i][:, 0:kc, 0:n] if kc <= 8 else None
                assert stg is not None
                S.add('sp', lambda e: e.dma_start(out=stg, in_=src_ap.rearrange(pat, p=128)),
                      reads=[], writes=['wstage%d' % i], dma=True)
                S.add('pool', lambda e: e.tensor_copy(out=dst, in_=stg), reads=['wstage%d' % i], writes=[key])

        def load_w(src_ap, kc, ncols):
            i = wq['n'] % 2
            wq['n'] += 1
            dst = wp[i][:, 0:kc, 0:ncols]
            key = 'wp%d' % i
            wload(dst, src_ap, key)
            return dst, key

        def sumsq_rstd(src_tile, nch, r, inv_n, bank=2, ncol=512):
            act(sqt[:, 0:nch, 0:ncol], src_tile, AF.Square, r, ['sqt'])
            for c in range(nch):
                mm(PS[bank][:, 0:ncol], onesb, sqt[:, c, 0:ncol], c == 0, c == nch - 1, ['sqt', 'onesb'], ['ps%d' % bank])
            act(tmpf[:, 0:ncol], PS[bank][:, 0:ncol], AF.Sqrt, ['ps%d' % bank], ['tmpf'], bias=EPS, scale=inv_n)
            S.add('dve', lambda e: e.reciprocal(out=rstd[:, 0:ncol], in_=tmpf[:, 0:ncol]), reads=['tmpf'], writes=['rstd'])

        def norm_to_bf(dst, l, i, tiles, gvec=None):
            for t2, (t0, n) in enumerate(tiles):
                sl = slice(t0, t0 + n)
                sumsq_rstd(h[:, :, sl], 8, ['h%d' % t2], 1.0 / D, ncol=n)
                for c in range(8):
                    g = gvec[:, c:c + 1] if gvec is not None else gn[:, (l * 4 + i) * 8 + c:(l * 4 + i) * 8 + c + 1]
                    stt(dst[:, c, sl], h[:, c, sl], g, rstd[:, 0:n], ALU.mult, ALU.mult, ['h%d' % t2, 'rstd', 'prm'], ['hn%d' % t2])

        def postnorm_add(l, i, t2, t0, n):
            sl = slice(t0, t0 + n)
            sumsq_rstd(ytile[:, :, 0:n], 8, ['ytile'], 1.0 / D, ncol=n)
            for c in range(8):
                g = gn[:, (l * 4 + i) * 8 + c:(l * 4 + i) * 8 + c + 1]
                stt(ytile[:, c, 0:n], ytile[:, c, 0:n], g, rstd[:, 0:n], ALU.mult, ALU.mult, ['ytile', 'rstd', 'prm'], ['ytile'])
                tt('pool', h[:, c, sl], h[:, c, sl], ytile[:, c, 0:n], ALU.add, ['ytile', 'h%d' % t2], ['h%d' % t2])

        def mem_setup():
            top = A.top
            mraw = A.f32(2 * 1024).rearrange("p (t d) -> p t d", t=2)
            memT = A.bf(8 * 256).rearrange("p (c m) -> p c m", c=8)
            mko = A.f32(512)
            dma('sp', mraw, memp.rearrange("(t p) d -> p t d", p=128), [], ['mraw'])
            for mt in range(2):
                for c in range(8):
                    tr(PS[3][:, 0:128], mraw[:, mt, c * 128:(c + 1) * 128], ident, ['mraw', 'cst'], ['ps3'])
                    cp('dve', memT[:, c, mt * 128:(mt + 1) * 128], PS[3][:, 0:128], ['ps3'], ['memT'])
            for l in range(4):
                for kv in range(2):
                    wpc, wk = load_w(w_mem[l][:, kv * 512:(kv + 1) * 512], 8, 512)
                    outd = mk_p if kv == 0 else mv_p
                    for mt in range(2):
                        for c in range(8):
                            mm(PS[0][:, :], memT[:, c, mt * 128:(mt + 1) * 128], wpc[:, c, :], c == 0, c == 7, ['memT', wk], ['ps0'])
                        cp('act', mko, PS[0][:, :], ['ps0'], ['mko'])
                        if kv == 1:
                            cp('dve', MV[l][:, mt, :], PS[0][:, :], ['ps0'], ['MV%d' % l])
                        dma('sp', outd[l][mt * 128:(mt + 1) * 128, :], mko, ['mko'], [])
                    if kv == 0:
                        for hd in range(4):
                            for c in range(8):
                                mm(PS[1][:, 0:256], wpc[:, c, hd * 128:(hd + 1) * 128], memT[:, c, :], c == 0, c == 7, ['memT', wk], ['ps1'])
                            cp('dve', MKT[l][:, hd, :], PS[1][:, 0:256], ['ps1'], ['MKT%d' % l])
            S.barrier()
            A.top = top

        def mem_attn(l, hd, QM, TT, etile):
            for t2 in range(TT):
                sl = slice(t2 * 512, (t2 + 1) * 512)
                for mt in range(2):
                    mm(PS[4 + mt][:, :], MKT[l][:, hd, mt * 128:(mt + 1) * 128], QM[:, sl], True, True, ['QM', 'MKT%d' % l], ['ps%d' % (4 + mt)])
                    act(etile[mt], PS[4 + mt][:, :], AF.Exp, ['ps%d' % (4 + mt)], ['et%d' % mt], scale=SCALE)
                for mt in range(2):
                    mm(PS[6][:, :], MV[l][:, mt, hd * 128:(hd + 1) * 128], etile[mt], mt == 0, mt == 1, ['et%d' % mt, 'MV%d' % l], ['ps6'])
                for mt in range(2):
                    mm(PS[7][:, :], onesb, etile[mt], mt == 0, mt == 1, ['et%d' % mt, 'onesb'], ['ps7'])
                S.add('dve', lambda e: e.reciprocal(out=tmpf, in_=PS[7][:, :]), reads=['ps7'], writes=['tmpf'])
                tt('dve', mix[:, 4 + hd, sl], PS[6][:, :], tmpf, ALU.mult, ['ps6', 'tmpf'], ['mix'])

        def out_proj(l, tiles):
            top = A.top
            wo = A.bf(8 * 1024).rearrange("p (c n) -> p c n", c=8)
            for half in range(2):
                S.add('pool', lambda e, half=half: e.dma_start(out=wo[:, :, half * 512:(half + 1) * 512],
                                                               in_=w_o[l][:, half * 512:(half + 1) * 512].rearrange("(c p) n -> p c n", p=128)),
                      reads=[], writes=['wo'], dma=True)
            for t2, (t0, n) in enumerate(tiles):
                sl = slice(t0, t0 + n)
                for nn in range(8):
                    b = nn % 2
                    for c in range(8):
                        mm(PS[b][:, 0:n], wo[:, c, nn * 128:(nn + 1) * 128], mix[:, c, sl], c == 0, c == 7, ['wo', 'mix'], ['ps%d' % b])
                    cp('act', ytile[:, nn, 0:n], PS[b][:, 0:n], ['ps%d' % b], ['ytile'])
                postnorm_add(l, 1, t2, t0, n)
            S.barrier()
            A.top = top

        def ffn(l, tiles):
            top = A.top
            Tt = tiles[-1][0] + tiles[-1][1]
            actb = A.bf(22 * Tt).rearrange("p (j t) -> p j t", j=22)
            wd = [A.bf(22 * 128).rearrange("p (j n) -> p j n", j=22) for i in range(2)]
            sg = A.f32(512)
            norm_to_bf(hn, l, 2, tiles)
            for jp in range(11):
                i = wq['n'] % 2
                wq['n'] += 1
                key = 'wp%d' % i
                for gu in range(2):
                    S.add('pool', lambda e, i=i, gu=gu, jp=jp: e.dma_start(
                        out=wp[i][:, :, gu * 256:(gu + 1) * 256],
                        in_=w_gu[l][:, gu * DFF + jp * 256:gu * DFF + (jp + 1) * 256].rearrange("(c p) n -> p c n", p=128)),
                        reads=[], writes=[key], dma=True)
                for jj in range(2):
                    j = jp * 2 + jj
                    for t2, (t0, n) in enumerate(tiles):
                        sl = slice(t0, t0 + n)
                        for c in range(8):
                            mm(PS[0][:, 0:n], wp[i][:, c, jj * 128:(jj + 1) * 128], hn[:, c, sl], c == 0, c == 7, [key, 'hn%d' % t2], ['ps0'])
                        for c in range(8):
                            mm(PS[1][:, 0:n], wp[i][:, c, 256 + jj * 128:256 + (jj + 1) * 128], hn[:, c, sl], c == 0, c == 7, [key, 'hn%d' % t2], ['ps1'])
                        act(sg[:, 0:n], PS[0][:, 0:n], AF.Silu, ['ps0'], ['sg'])
                        tt('dve', actb[:, j, sl], PS[1][:, 0:n], sg[:, 0:n], ALU.mult, ['ps1', 'sg'], ['actb'])
            for nn in range(8):
                i = nn % 2
                key = 'wd%d' % i
                S.add('pool', lambda e, i=i, nn=nn: e.dma_start(
                    out=wd[i], in_=w_down[l][:, nn * 128:(nn + 1) * 128].rearrange("(j p) n -> p j n", p=128)),
                    reads=[], writes=[key], dma=True)
                for t2, (t0, n) in enumerate(tiles):
                    sl = slice(t0, t0 + n)
                    b = 2 + (nn * len(tiles) + t2) % 2
                    for j in range(22):
                        mm(PS[b][:, 0:n], wd[i][:, j, :], actb[:, j, sl], j == 0, j == 21, [key, 'actb'], ['ps%d' % b])
                    cp('act', y2[t2][:, nn, 0:n], PS[b][:, 0:n], ['ps%d' % b], ['y2_%d' % t2])
            for t2, (t0, n) in enumerate(tiles):
                sl = slice(t0, t0 + n)
                sumsq_rstd(y2[t2][:, :, 0:n], 8, ['y2_%d' % t2], 1.0 / D, bank=4, ncol=n)
                for c in range(8):
                    g = gn[:, (l * 4 + 3) * 8 + c:(l * 4 + 3) * 8 + c + 1]
                    stt(y2[t2][:, c, 0:n], y2[t2][:, c, 0:n], g, rstd[:, 0:n], ALU.mult, ALU.mult, ['y2_%d' % t2, 'rstd', 'prm'], ['y2_%d' % t2])
                    tt('pool', h[:, c, sl], h[:, c, sl], y2[t2][:, c, 0:n], ALU.add, ['y2_%d' % t2, 'h%d' % t2], ['h%d' % t2])
            S.barrier()
            A.top = top

        y2 = [ytile, mixw.rearrange("p (c t) -> p c t", c=8)]
        common_top = A.top

        def a_mixer(l, g, TT):
            top = A.top
            Tg = TT * 512
            NT = Tg // 128
            NCK = Tg // 64
            Vt = A.bf(NT * 512).rearrange("p (t c) -> p t c", t=NT)
            QS = A.f32(Tg); LF = A.f32(Tg); KK = A.f32(Tg); TM = A.f32(Tg)
            QT = A.bf(Tg); KT = A.bf(Tg); KH = A.bf(Tg); G = A.bf(Tg); QM = A.bf(Tg)
            O = A.f32(Tg)
            SB16 = A.bf((NCK + 1) * 128).rearrange("p (k v) -> p k v", k=NCK + 1)
            KHT = A.bf(128); AM = A.bf(128)
            dec = A.f32(NCK)
            et = [A.bf(512) for i in range(2)]
            norm_to_bf(hn, l, 0, PT)
            wpc, wk = load_w(w_in_a[l][:, 1024:1536], 8, 512)
            for t in range(NT):
                b = t % 2
                for c in range(8):
                    mm(PS[b][:, :], hn[:, c, t * 128:(t + 1) * 128], wpc[:, c, :], c == 0, c == 7, ['hn%d' % (t // 4), wk], ['ps%d' % b])
                cp('act', Vt[:, t, :], PS[b][:, :], ['ps%d' % b], ['Vt'])
            for hd in range(4):
                i = wq['n'] % 2
                wq['n'] += 1
                key = 'wp%d' % i
                for pi, off in enumerate([0, 512, 1536, 2048]):
                    S.add('pool', lambda e, i=i, pi=pi, off=off, hd=hd: e.dma_start(
                        out=wp[i][:, :, pi * 128:(pi + 1) * 128],
                        in_=w_in_a[l][:, off + hd * 128:off + (hd + 1) * 128].rearrange("(c p) n -> p c n", p=128)),
                        reads=[], writes=[key], dma=True)
                lbc = lb1[:, hd:hd + 1]
                omc = oml1[:, hd:hd + 1]
                for t2 in range(TT):
                    sl = slice(t2 * 512, (t2 + 1) * 512)
                    for pi in range(4):
                        b = pi % 2
                        for c in range(8):
                            mm(PS[b][:, :], wp[i][:, c, pi * 128:(pi + 1) * 128], hn[:, c, sl], c == 0, c == 7, [key, 'hn%d' % t2], ['ps%d' % b])
                        if pi == 0:
                            act(QS[:, sl], PS[b][:, :], AF.Silu, ['ps%d' % b], ['QS'])
                        elif pi == 1:
                            act(TM[:, sl], PS[b][:, :], AF.Sigmoid, ['ps%d' % b], ['TM'])
                        elif pi == 2:
                            act(G[:, sl], PS[b][:, :], AF.Silu, ['ps%d' % b], ['G'])
                        else:
                            cp('act', QM[:, sl], PS[b][:, :], ['ps%d' % b], ['QM'])
                if l == 1:
                    ts('dve', TM, TM, omc, lbc, ALU.mult, ALU.add, ['TM', 'lb1', 'oml1'], ['TM'])
                act(LF, TM, AF.Ln, ['TM'], ['LF'])
                ts('dve', KK, TM, -1.0, 1.0, ALU.mult, ALU.add, ['TM'], ['KK'])
                S.add('dve', lambda e: e.tensor_tensor_scan(out=TM, data0=resetm[:, 0:Tg], data1=LF, initial=0.0, op0=ALU.mult, op1=ALU.add),
                      reads=['LF', 'cst'], writes=['TM'])
                act(LF, TM, AF.Exp, ['TM'], ['LF'])
                tt('dve', QT, QS, LF, ALU.mult, ['QS', 'LF'], ['QT'])
                act(LF, TM, AF.Exp, ['TM', 'QT'], ['LF'], scale=-1.0)
                tt('dve', KT, KK, LF, ALU.mult, ['KK', 'LF'], ['KT'])
                b3 = TM.rearrange("p (k s) -> p k s", s=64)
                act(dec, b3[:, :, 63], AF.Exp, ['TM'], ['dec'])
                tt('dve', LF.rearrange("p (k s) -> p k s", s=64), b3[:, :, 63:64].to_broadcast([128, NCK, 64]), b3, ALU.subtract, ['TM', 'KT'], ['LF'])
                act(LF, LF, AF.Exp, ['LF'], ['LF'])
                tt('dve', KH, KK, LF, ALU.mult, ['KK', 'LF'], ['KH'])
                skey = 'S%d%d' % (l, hd)
                Sm = Sst[l][hd]
                cp('act', SB16[:, 0, :], Sm, [skey], ['SB16'])
                for t in range(NT):
                    tr(PS[3][:, 0:64].bitcast(BF16), KH[:, t * 128:(t + 1) * 128], identb, ['KH', 'identb'], ['ps3'])
                    cp('dve', KHT, PS[3][:, 0:64].bitcast(BF16), ['ps3'], ['KHT'])
                    for cc in range(2):
                        ck = t * 2 + cc
                        mm(PS[2][:, 0:128], KHT[cc * 64:(cc + 1) * 64, :], Vt[cc * 64:(cc + 1) * 64, t, hd * 128:(hd + 1) * 128], True, True, ['KHT', 'Vt'], ['ps2'])
                        stt(Sm, Sm, dec[:, ck:ck + 1], PS[2][:, 0:128], ALU.mult, ALU.add, [skey, 'dec', 'ps2'], [skey])
                        cp('act', SB16[:, ck + 1, :], Sm, [skey], ['SB16'])
                if g == NG - 1:
                    dma('sp', stp[l][hd], Sm, [skey], [])
                for t in range(NT):
                    tsl = slice(t * 128, (t + 1) * 128)
                    mm(PS[4][:, 0:128], KT[:, tsl], QT[:, tsl], True, True, ['KT', 'QT'], ['ps4'])
                    tt('dve', AM, PS[4][:, 0:128], hmaskb, ALU.mult, ['ps4', 'hmaskb'], ['AM'])
                    ob = 5 + (t // 4) % 2
                    oc = (t % 4) * 128
                    mm(PS[ob][:, oc:oc + 128], Vt[:, t, hd * 128:(hd + 1) * 128], AM, True, False, ['Vt', 'AM'], ['ps%d' % ob])
                    for cc in range(2):
                        ck = t * 2 + cc
                        mm(PS[ob][:, oc + cc * 64:oc + (cc + 1) * 64], SB16[:, ck, :], QT[:, t * 128 + cc * 64:t * 128 + (cc + 1) * 64], False, cc == 1,
                           ['SB16', 'QT'], ['ps%d' % ob])
                    if t % 4 == 3:
                        t2 = t // 4
                        cp('act', O[:, t2 * 512:(t2 + 1) * 512], PS[ob][:, :], ['ps%d' % ob], ['O'])
                for t2 in range(TT):
                    sl = slice(t2 * 512, (t2 + 1) * 512)
                    act(sqt[:, 0, :], O[:, sl], AF.Square, ['O'], ['sqt'])
                    mm(PS[7][:, :], onesb, sqt[:, 0, :], True, True, ['sqt', 'onesb'], ['ps7'])
                    act(tmpf, PS[7][:, :], AF.Sqrt, ['ps7'], ['tmpf'], bias=EPS, scale=1.0 / 128)
                    S.add('dve', lambda e: e.reciprocal(out=rstd, in_=tmpf), reads=['tmpf'], writes=['rstd'])
                    stt(O[:, sl], O[:, sl], hgn[:, l:l + 1], rstd, ALU.mult, ALU.mult, ['O', 'rstd', 'prm'], ['O'])
                    tt('dve', mix[:, hd, sl], O[:, sl], G[:, sl], ALU.mult, ['O', 'G'], ['mix'])
                mem_attn(l, hd, QM, TT, et)
            S.barrier()
            A.top = top

        def kv_proj(g, TT):
            top = A.top
            Tg = TT * 512
            NT = Tg // 128
            ko = [A.f32(512) for i in range(2)]
            vb = [A.bf(512) for i in range(2)]
            ktb = A.bf(Tg)
            norm_to_bf(hn, 0, 0, PT, gvec=gkv)
            for kv in range(2):
                wpc, wk = load_w(w_kv[:, kv * 512:(kv + 1) * 512], 8, 512)
                outd = sbk_p if kv == 0 else sbv_p
                for t in range(NT):
                    b = t % 2
                    r0 = g * T + t * 128
                    for c in range(8):
                        mm(PS[b][:, :], hn[:, c, t * 128:(t + 1) * 128], wpc[:, c, :], c == 0, c == 7, ['hn%d' % (t // 4), wk], ['ps%d' % b])
                    cp('act', ko[b], PS[b][:, :], ['ps%d' % b], ['ko%d' % b])
                    dma('sp', outd[r0:r0 + 128, :], ko[b], ['ko%d' % b], [])
                    if kv == 1:
                        cp('dve', vb[b], PS[b][:, :], ['ps%d' % b], ['vb%d' % b])
                        dma('sp', vc[r0:r0 + 128, :], vb[b], ['vb%d' % b], ['vc'])
                if kv == 0:
                    for hd in range(4):
                        for t2 in range(TT):
                            sl = slice(t2 * 512, (t2 + 1) * 512)
                            b = 2 + t2 % 2
                            for c in range(8):
                                mm(PS[b][:, :], wpc[:, c, hd * 128:(hd + 1) * 128], hn[:, c, sl], c == 0, c == 7, [wk, 'hn%d' % t2], ['ps%d' % b])
                            ts('dve', ktb[:, sl], PS[b][:, :], SCALE, None, ALU.mult, None, ['ps%d' % b], ['ktb'])
                        dma('sp', ktc[:, hd, g * T:g * T + Tg], ktb, ['ktb'], ['ktc'])
            S.barrier()
            A.top = top

        def b_mixer(l, g, TT):
            top = A.top
            Tg = TT * 512
            lb_ = l - 2
            QB = A.bf(4 * Tg).rearrange("p (h t) -> p h t", h=4)
            QM = A.bf(Tg)
            et = [A.bf(512) for i in range(2)]
            KBLK = KBLK_G
            KB = [A.bf(KBLK) for i in range(2)]
            VB = [A.bf(16 * 128).rearrange("p (t v) -> p t v", t=16) for i in range(2)]
            NB = 3
            ef = [A.f32(512) for i in range(NB)]; spb = [A.bf(512) for i in range(NB)]; wbb = [A.bf(512) for i in range(NB)]
            Sacc = A.f32(512); Sb = [A.bf(512) for i in range(2)]
            norm_to_bf(hn, l, 0, PT)
            for half in range(2):
                wpc, wk = load_w(w_in_b[lb_][:, half * 512:(half + 1) * 512], 8, 512)
                for hd in range(4):
                    for t2 in range(TT):
                        sl = slice(t2 * 512, (t2 + 1) * 512)
                        b = (hd * TT + t2) % 2
                        for c in range(8):
                            mm(PS[b][:, :], wpc[:, c, hd * 128:(hd + 1) * 128], hn[:, c, sl], c == 0, c == 7, [wk, 'hn%d' % t2], ['ps%d' % b])
                        if half == 0:
                            cp('act', QB[:, hd, sl], PS[b][:, :], ['ps%d' % b], ['QB'])
                        else:
                            cp('act', QM[:, sl], PS[b][:, :], ['ps%d' % b], ['QM'])
                    if half == 1:
                        mem_attn(l, hd, QM, TT, et)
            jobs = []
            blocks = []
            for t2 in range(TT):
                P0 = g * T + t2 * 512
                nkeys = P0 + 512
                nkt = nkeys // 128
                for hd in range(4):
                    tl = []
                    nblk = (nkeys + KBLK - 1) // KBLK
                    for kb in range(nblk - 1, -1, -1):
                        k0 = kb * KBLK
                        k1 = min(nkeys, k0 + KBLK)
                        bi_ = len(blocks)
                        blocks.append((hd, k0, k1))
                        for kt in range(k1 // 128 - 1, k0 // 128 - 1, -1):
                            tl.append((kt, bi_, kt - k0 // 128, kt - (nkt - 4)))
                    jobs.append((t2, hd, tl))

            def load_block(m):
                hd_, k0, k1 = blocks[m]
                i = m % 2
                dma('sp', KB[i][:, 0:k1 - k0], ktc[:, hd_, k0:k1], ['ktc'], ['KB%d' % i])
                dma('sp', VB[i][:, 0:(k1 - k0) // 128, :], vc[k0:k1, hd_ * 128:(hd_ + 1) * 128].rearrange("(t p) v -> p t v", p=128), ['vc'], ['VB%d' % i])

            load_block(0)
            loaded = 1
            rr = 0
            for ji, (t2, hd, tl) in enumerate(jobs):
                sl = slice(t2 * 512, (t2 + 1) * 512)
                bias = sbb[:, lb_ * 4 + hd:lb_ * 4 + hd + 1]
                ob = 3 if ji % 2 == 0 else 7
                nt = len(tl)

                def stageA(idx, rr0=rr):
                    kt, m, kl, rdiag = tl[idx]
                    i = m % 2
                    r = rr0 + idx
                    zb = 4 + r % 3
                    bi = r % NB
                    zk = 'ps%d' % zb
                    mm3(PS[zb][:, :], KB[i][:, kl * 128:(kl + 1) * 128], QB[:, hd, sl], True, False, ['KB%d' % i, 'QB'], [zk])
                    act(ef[bi], PS[zb][:, :], AF.Exp, [zk], ['ef%d' % bi], bias=bias)
                    act(spb[bi], ef[bi], AF.Ln, ['ef%d' % bi], ['spb%d' % bi], bias=1.0)
                    if rdiag >= 0:
                        tt('pool', spb[bi], spb[bi], sbmaskb[rdiag], ALU.mult, ['spb%d' % bi, 'sbmaskb'], ['spb%d' % bi])
                    mm3(PS[zb][:, :], NUb, spb[bi], False, idx == 0, ['NUb', 'spb%d' % bi], [zk])
                    if idx > 0:
                        mm3(PS[zb][:, :], NOb, Sb[(idx - 1) % 2], False, True, ['NOb', 'Sb%d' % ((idx - 1) % 2)], [zk])
                    if idx < nt - 1:
                        if idx == 0:
                            cp('dve', Sacc, spb[bi], ['spb%d' % bi], ['Sacc'])
                        else:
                            tt('dve', Sacc, Sacc, spb[bi], ALU.add, ['Sacc', 'spb%d' % bi], ['Sacc'])
                        cp('dve', Sb[idx % 2], Sacc, ['Sacc'], ['Sb%d' % (idx % 2)])

                def stageB(idx, rr0=rr):
                    kt, m, kl, rdiag = tl[idx]
                    i = m % 2
                    r = rr0 + idx
                    zb = 4 + r % 3
                    bi = r % NB
                    zk = 'ps%d' % zb
                    act(wbb[bi], PS[zb][:, :], AF.Exp, [zk], ['wb%d' % bi], bias=bias)
                    if rdiag >= 0:
                        tt('pool', wbb[bi], wbb[bi], sbmaskb[rdiag], ALU.mult, ['wb%d' % bi, 'sbmaskb'], ['wb%d' % bi])
                    mm(PS[ob][:, :], VB[i][:, kl, :], wbb[bi], idx == 0, idx == nt - 1, ['VB%d' % i, 'wb%d' % bi], ['ps%d' % ob])

                for idx in range(nt):
                    m = tl[idx][1]
                    stageA(idx)
                    if idx > 0:
                        stageB(idx - 1)
                    if m + 1 >= loaded and m + 1 < len(blocks):
                        load_block(m + 1)
                        loaded = m + 2
                stageB(nt - 1)
                rr += nt
                cp('act', mix[:, hd, sl], PS[ob][:, :], ['ps%d' % ob], ['mix'])
            S.barrier()
            A.top = top


        ST = [(0, NS)]
        NC2 = 4 * NS
        AX = mybir.AxisListType.X

        def flat2(ap3):
            return ap3.rearrange("p c t -> p (c t)")

        def mem_attn_s(l, QM2):
            top = A.top
            Kc = [A.f32(1024).rearrange("p (t c) -> p t c", t=2) for i in range(2)]
            Vc = [A.f32(1024).rearrange("p (t c) -> p t c", t=2) for i in range(2)]
            KcT = A.f32(1024).rearrange("p (h m) -> p h m", h=4)
            E2 = A.f32(16).rearrange("p (c t) -> p c t", t=2)
            om32 = A.f32(32)
            dn = A.f32(4); rd = A.f32(4)
            for s_ in range(NS):
                i = s_ % 2
                dma('sp', Kc[i], cmk[l][s_].rearrange("(t p) c -> p t c", p=128), [], ['Kc%d' % i])
                dma('sp', Vc[i], cmv[l][s_].rearrange("(t p) c -> p t c", p=128), [], ['Vc%d' % i])
                for hd in range(4):
                    for mt in range(2):
                        q = hd * 2 + mt
                        bk = 4 + q // 4
                        tr(PS[bk][:, (q % 4) * 128:(q % 4 + 1) * 128], Kc[i][:, mt, hd * 128:(hd + 1) * 128], ident, ['Kc%d' % i, 'cst'], ['ps%d' % bk])
                cp('act', KcT[:, 0:2, :], PS[4][:, :].rearrange("p (h m) -> p h m", h=2), ['ps4'], ['KcT'])
                cp('dve', KcT[:, 2:4, :], PS[5][:, :].rearrange("p (h m) -> p h m", h=2), ['ps5'], ['KcT'])
                for hd in range(4):
                    for mt in range(2):
                        q = hd * 2 + mt
                        mm(PS[6][:, q * 2:q * 2 + 2], KcT[:, hd, mt * 128:(mt + 1) * 128], QM2[:, hd * NS + s_, :], True, True, ['KcT', 'QM2'], ['ps6'])
                act(flat2(E2), PS[6][:, 0:16], AF.Exp, ['ps6'], ['E2'], scale=SCALE)
                for hd in range(4):
                    for mt in range(2):
                        mm(PS[7][:, hd * 2:hd * 2 + 2], Vc[i][:, mt, hd * 128:(hd + 1) * 128], E2[:, hd * 2 + mt, :], mt == 0, mt == 1, ['Vc%d' % i, 'E2'], ['ps7'])
                mm(PS[7][:, 16:32], ones_ff, flat2(E2), True, True, ['E2', 'cst'], ['ps7'])
                cp('act', om32, PS[7][:, 0:32], ['ps7'], ['om32'])
                cs4 = om32[:, 16:32].rearrange("p (h m t) -> p h m t", h=4, m=2)
                tt('dve', dn, cs4[:, :, 0, 0], cs4[:, :, 1, 0], ALU.add, ['om32'], ['dn'])
                S.add('dve', lambda e: e.reciprocal(out=rd, in_=dn), reads=['dn'], writes=['rd'])
                tt('dve', mix[:, 4:8, s_], om32[:, 0:8].rearrange("p (h t) -> p h t", t=2)[:, :, 0], rd, ALU.mult, ['om32', 'rd'], ['mix'])
            S.barrier()
            A.top = top

        def a_mixer_s(l):
            top = A.top
            Q2 = A.f32(NC2 * 2).rearrange("p (c t) -> p c t", t=2)
            QM2 = A.f32(NC2 * 2).rearrange("p (c t) -> p c t", t=2)
            Ff = A.f32(NC2); Kf = A.f32(NC2); Gf = A.f32(NC2); Vf = A.f32(NC2); Of = A.f32(NC2)
            Ktok = A.f32(512); Vtok = A.f32(512)
            Vexp = A.f32(NS * 128)
            Vexp3 = Vexp.rearrange("p (s v) -> p s v", s=NS)
            S0 = A.f32(NS * 128).rearrange("p (s v) -> p s v", s=NS)
            sq = A.bf(NC2)
            S.add('pool', lambda e: e.memset(flat2(Q2), 0.0), writes=['Q2'])
            S.add('pool', lambda e: e.memset(flat2(QM2), 0.0), writes=['QM2'])
            norm_to_bf(hn, l, 0, ST)
            for pi in range(5):
                wpc, wk = load_w(w_in_a[l][:, pi * 512:(pi + 1) * 512], 8, 512)
                for hd in range(4):
                    b = hd % 2
                    cs = slice(hd * NS, (hd + 1) * NS)
                    for c in range(8):
                        mm(PS[b][:, 0:NS], wpc[:, c, hd * 128:(hd + 1) * 128], hn[:, c, 0:NS], c == 0, c == 7, [wk, 'hn0'], ['ps%d' % b])
                    if pi == 0:
                        act(Q2[:, cs, 0], PS[b][:, 0:NS], AF.Silu, ['ps%d' % b], ['Q2'])
                    elif pi == 1:
                        act(Ff[:, cs], PS[b][:, 0:NS], AF.Sigmoid, ['ps%d' % b], ['Ff'])
                    elif pi == 2:
                        cp('act', Vf[:, cs], PS[b][:, 0:NS], ['ps%d' % b], ['Vf'])
                    elif pi == 3:
                        act(Gf[:, cs], PS[b][:, 0:NS], AF.Silu, ['ps%d' % b], ['Gf'])
                    else:
                        cp('act', QM2[:, cs, 0], PS[b][:, 0:NS], ['ps%d' % b], ['QM2'])
            if l == 1:
                for hd in range(4):
                    cs = slice(hd * NS, (hd + 1) * NS)
                    ts('dve', Ff[:, cs], Ff[:, cs], oml1[:, hd:hd + 1], lb1[:, hd:hd + 1], ALU.mult, ALU.add, ['Ff', 'lb1', 'oml1'], ['Ff'])
            ts('dve', Kf, Ff, -1.0, 1.0, ALU.mult, ALU.add, ['Ff'], ['Kf'])
            for hd in range(4):
                cs = slice(hd * NS, (hd + 1) * NS)
                tr(PS[3][0:NS, 0:128], Kf[:, cs], ident, ['Kf', 'cst'], ['ps3'])
                cp('dve', Ktok[0:NS, hd * 128:(hd + 1) * 128], PS[3][0:NS, 0:128], ['ps3'], ['Ktok'])
                tr(PS[3][0:NS, 128:256], Vf[:, cs], ident, ['Vf', 'cst'], ['ps3'])
                cp('dve', Vtok[0:NS, hd * 128:(hd + 1) * 128], PS[3][0:NS, 128:256], ['ps3'], ['Vtok'])
            for hd in range(4):
                c0 = hd * NS
                dma('sp', S0, st0[l][:, hd].rearrange("s k v -> k s v"), [], ['S0'])
                tt('dve', Vexp3[0:NS], Vtok[0:NS, hd * 128:(hd + 1) * 128].unsqueeze(1).to_broadcast([NS, NS, 128]),
                   ident[0:NS, 0:NS].unsqueeze(2).to_broadcast([NS, NS, 128]), ALU.mult, ['Vtok', 'cst'], ['Vexp'])
                for q4 in range(NS // 4):
                    b = 4 + q4 % 2
                    mm(PS[b][:, :], Ktok[0:NS, hd * 128:(hd + 1) * 128], Vexp[0:NS, q4 * 512:(q4 + 1) * 512], True, True, ['Ktok', 'Vexp'], ['ps%d' % b])
                    for s4 in range(4):
                        s_ = q4 * 4 + s4
                        stt(S0[:, s_, :], S0[:, s_, :], Ff[:, c0 + s_:c0 + s_ + 1], PS[b][:, s4 * 128:(s4 + 1) * 128], ALU.mult, ALU.add,
                            ['S0', 'Ff', 'ps%d' % b], ['S0'])
                dma('sp', sts[l][:, hd].rearrange("s k v -> k s v"), S0, ['S0'], [])
                for s_ in range(NS):
                    col = c0 + s_
                    mm(PS[6][:, 2 * col:2 * col + 2], S0[:, s_, :], Q2[:, col, :], True, True, ['S0', 'Q2'], ['ps6'])
            cp('act', Of, PS[6][:, 0:2 * NC2].rearrange("p (c t) -> p c t", t=2)[:, :, 0], ['ps6'], ['Of'])
            act(sq, Of, AF.Square, ['Of'], ['sq'])
            mm(PS[7][:, 0:NC2], onesb, sq, True, True, ['sq', 'onesb'], ['ps7'])
            act(tmpf[:, 0:NC2], PS[7][:, 0:NC2], AF.Sqrt, ['ps7'], ['tmpf'], bias=EPS, scale=1.0 / 128)
            S.add('dve', lambda e: e.reciprocal(out=rstd[:, 0:NC2], in_=tmpf[:, 0:NC2]), reads=['tmpf'], writes=['rstd'])
            stt(Of, Of, hgn[:, l:l + 1], rstd[:, 0:NC2], ALU.mult, ALU.mult, ['Of', 'rstd', 'prm'], ['Of'])
            tt('dve', mix[:, 0:4, 0:NS], Of.rearrange("p (h s) -> p h s", h=4), Gf.rearrange("p (h s) -> p h s", h=4), ALU.mult, ['Of', 'Gf'], ['mix'])
            mem_attn_s(l, QM2)
            S.barrier()
            A.top = top

        def b_mixer_s(l, idxt):
            top = A.top
            lb_ = l - 2
            Q2 = A.f32(NC2 * 2).rearrange("p (c t) -> p c t", t=2)
            QM2 = A.f32(NC2 * 2).rearrange("p (c t) -> p c t", t=2)
            KP = [A.f32(512) for i in range(4)]
            KT = [A.f32(512) for i in range(4)]
            VP = [A.f32(512) for i in range(4)]
            ZB = A.f32(64); EE = A.f32(64); SP = A.f32(64); CR = A.f32(64)
            TLCS = A.f32(128)
            W2 = A.f32(128).rearrange("p (c t) -> p c t", t=2)
            OSs = A.f32(128)
            os4 = A.f32(4)
            r3 = lambda ap: ap.rearrange("p (j h) -> p j h", h=4)
            S.add('pool', lambda e: e.memset(flat2(Q2), 0.0), writes=['Q2'])
            S.add('pool', lambda e: e.memset(flat2(QM2), 0.0), writes=['QM2'])
            S.add('pool', lambda e: e.memset(flat2(W2), 0.0), writes=['W2'])
            S.add('pool', lambda e: e.memset(CR, 0.0), writes=['CR'])
            norm_to_bf(hn, l, 0, ST)
            for pi in range(2):
                wpc, wk = load_w(w_in_b[lb_][:, pi * 512:(pi + 1) * 512], 8, 512)
                for hd in range(4):
                    b = hd % 2
                    cs = slice(hd * NS, (hd + 1) * NS)
                    for c in range(8):
                        mm(PS[b][:, 0:NS], wpc[:, c, hd * 128:(hd + 1) * 128], hn[:, c, 0:NS], c == 0, c == 7, [wk, 'hn0'], ['ps%d' % b])
                    cp('act', (Q2 if pi == 0 else QM2)[:, cs, 0], PS[b][:, 0:NS], ['ps%d' % b], ['Q2' if pi == 0 else 'QM2'])
            n = 0
            for s_ in range(NS):
                for j in range(16):
                    i = n % 4
                    n += 1
                    col = s_ * 16 + j
                    S.add('pool', lambda e, i=i, col=col: e.indirect_dma_start(
                        out=KP[i], out_offset=None, in_=pool_k, in_offset=bass.IndirectOffsetOnAxis(ap=idxt[:, col:col + 1], axis=0)),
                        reads=['idxt'], writes=['KP%d' % i], dma=True)
                    for hd in range(4):
                        tr(PS[4][:, hd * 128:(hd + 1) * 128], KP[i][:, hd * 128:(hd + 1) * 128], ident, ['KP%d' % i, 'cst'], ['ps4'])
                    cp('act' if j % 2 else 'dve', KT[i], PS[4][:, :], ['ps4'], ['KT%d' % i])
                    for hd in range(4):
                        q = j * 4 + hd
                        mm(PS[6][:, 2 * q:2 * q + 2], KT[i][:, hd * 128:(hd + 1) * 128], Q2[:, hd * NS + s_, :], True, True, ['KT%d' % i, 'Q2'], ['ps6'])
                Zv = PS[6][:, 0:128].rearrange("p (j h t) -> p j h t", j=16, h=4)[:, :, :, 0]
                stt(r3(ZB), Zv, SCALE, sbb[:, lb_ * 4:lb_ * 4 + 4].unsqueeze(1).to_broadcast([128, 16, 4]), ALU.mult, ALU.add, ['ps6', 'prm'], ['ZB'])
                act(EE, ZB, AF.Exp, ['ZB'], ['EE'])
                act(SP, EE, AF.Ln, ['EE'], ['SP'], bias=1.0)
                mm(PS[7][:, 0:64], U_ff, SP, True, True, ['SP', 'cst'], ['ps7'])
                mm(PS[7][:, 64:128], ones_ff, SP, True, True, ['SP', 'cst'], ['ps7'])
                cp('act', TLCS, PS[7][:, 0:128], ['ps7'], ['TLCS'])
                CS3 = r3(TLCS[:, 64:128]); CR3 = r3(CR)
                for j in range(14, -1, -1):
                    tt('dve', CR3[:, j, :], CR3[:, j + 1, :], CS3[:, j + 1, :], ALU.add, ['CR', 'TLCS'], ['CR'])
                tt('dve', ZB, ZB, SP, ALU.subtract, ['ZB', 'SP'], ['ZB'])
                tt('dve', ZB, ZB, TLCS[:, 0:64], ALU.subtract, ['ZB', 'TLCS'], ['ZB'])
                tt('dve', ZB, ZB, CR, ALU.subtract, ['ZB', 'CR'], ['ZB'])
                act(W2[:, :, 0], ZB, AF.Exp, ['ZB'], ['W2'])
                for j in range(16):
                    i = n % 4
                    n += 1
                    col = s_ * 16 + j
                    S.add('pool', lambda e, i=i, col=col: e.indirect_dma_start(
                        out=VP[i], out_offset=None, in_=pool_v, in_offset=bass.IndirectOffsetOnAxis(ap=idxt[:, col:col + 1], axis=0)),
                        reads=['idxt'], writes=['VP%d' % i], dma=True)
                    for hd in range(4):
                        q = j * 4 + hd
                        mm(PS[5][:, 2 * q:2 * q + 2], VP[i][:, hd * 128:(hd + 1) * 128], W2[:, q, :], True, True, ['VP%d' % i, 'W2'], ['ps5'])
                cp('act', OSs, PS[5][:, 0:128], ['ps5'], ['OSs'])
                S.add('dve', lambda e: e.tensor_reduce(out=os4, in_=OSs.rearrange("p (j h t) -> p h j t", j=16, h=4)[:, :, :, 0], axis=AX, op=ALU.add),
                      reads=['OSs'], writes=['os4'])
                cp('dve', mix[:, 0:4, s_], os4, ['os4'], ['mix'])
            mem_attn_s(l, QM2)
            S.barrier()
            A.top = top

        def kv_proj_s():
            top = A.top
            ko = A.f32(512)
            norm_to_bf(hn, 0, 0, ST, gvec=gkv)
            for kv in range(2):
                wpc, wk = load_w(w_kv[:, kv * 512:(kv + 1) * 512], 8, 512)
                for c in range(8):
                    mm(PS[0][0:NS, :], hn[:, c, 0:NS], wpc[:, c, :], c == 0, c == 7, ['hn0', wk], ['ps0'])
                cp('act', ko[0:NS, :], PS[0][0:NS, :], ['ps0'], ['ko'])
                dma('sp', sbk_s if kv == 0 else sbv_s, ko[0:NS, :], ['ko'], [])
            S.barrier()
            A.top = top

        def sample_group():
            top = A.top
            idxt = A.i32(NS * 16)
            idf = A.f32(NS * 16)
            xraw = A.f32(1024)
            dma('sp', idxt, ptab.partition_broadcast(128), [], ['idxt'])
            cp('dve', idf, idxt, ['idxt'], ['idf'])
            ts('dve', idf, idf, 128.0, iota_c, ALU.mult, ALU.add, ['idf', 'prm'], ['idf'])
            cp('dve', idxt, idf, ['idf'], ['idxt'])
            dma('sp', xraw[0:NS, :], xs, [], ['xraw'])
            for c in range(8):
                b = c % 2
                tr(PS[b][:, 0:NS], xraw[0:NS, c * 128:(c + 1) * 128], ident[0:NS, 0:NS], ['xraw', 'cst'], ['ps%d' % b])
                cp('dve' if c % 2 else 'act', h[:, c, 0:NS], PS[b][:, 0:NS], ['ps%d' % b], ['h0'])
            S.barrier()
            for l in range(4):
                if l < 2:
                    a_mixer_s(l)
                else:
                    b_mixer_s(l, idxt)
                out_proj(l, ST)
                ffn(l, ST)
                if l == 1:
                    kv_proj_s()
            for c in range(8):
                b = c % 2
                tr(PS[b][0:NS, 0:128], h[:, c, 0:NS], ident, ['h0', 'cst'], ['ps%d' % b])
                cp('dve' if c % 2 else 'act', xraw[0:NS, c * 128:(c + 1) * 128], PS[b][0:NS, 0:128], ['ps%d' % b], ['xraw'])
            dma('sp', ys, xraw[0:NS, :], ['xraw'], [])
            S.barrier()
            A.top = top

        if STOP != -2:
            mem_setup()
        if SAMPLE:
            sample_group()
        TT = T // 512
        PT = [(i * 512, 512) for i in range(TT)]
        for g in range(NG if STOP != -1 else 0):
            top = A.top
            xraw = [A.f32(1024) for i in range(2)]
            for t in range(T // 128):
                i = t % 2
                r0 = g * T + t * 128
                dma('sp', xraw[i], xp[r0:r0 + 128, :], [], ['xraw%d' % i])
                for c in range(8):
                    b = c % 2
                    tr(PS[b][:, 0:128], xraw[i][:, c * 128:(c + 1) * 128], ident, ['xraw%d' % i, 'cst'], ['ps%d' % b])
                    cp('dve' if c % 2 else 'act', h[:, c, t * 128:(t + 1) * 128], PS[b][:, 0:128], ['ps%d' % b], ['h%d' % (t // 4)])
            S.barrier()
            A.top = top
            for l in range(4):
                if STOP <= 1 + 3 * l:
                    break
                if l < 2:
                    a_mixer(l, g, TT)
                else:
                    b_mixer(l, g, TT)
                if STOP <= 2 + 3 * l:
                    break
                out_proj(l, PT)
                if STOP <= 3 + 3 * l:
                    break
                ffn(l, PT)
                if l == 1:
                    kv_proj(g, TT)
            top = A.top
            yraw = [A.f32(1024) for i in range(2)]
            for t in range(T // 128):
                i = t % 2
                r0 = g * T + t * 128
                for c in range(8):
                    b = c % 2
                    tr(PS[b][:, 0:128], h[:, c, t * 128:(t + 1) * 128], ident, ['h%d' % (t // 4), 'cst'], ['ps%d' % b])
                    cp('dve' if c % 2 else 'act', yraw[i][:, c * 128:(c + 1) * 128], PS[b][:, 0:128], ['ps%d' % b], ['yraw%d' % i])
                dma('sp', yp[r0:r0 + 128, :], yraw[i], ['yraw%d' % i], [])
            S.barrier()
            A.top = top
        S.emit()
    return nc


_NC = None


def make_params(g_norm, hg_lb, hg_norm, sb_bias, g_kv):
    params = np.zeros((128, 160), np.float32)
    params[:, 0:128] = g_norm.reshape(4, 4, 8, 128).transpose(3, 0, 1, 2).reshape(128, 128)
    params[:, 128:136] = hg_lb.reshape(2, 4, 128).transpose(2, 0, 1).reshape(128, 8)
    params[:, 136:138] = hg_norm.T
    params[:, 138:146] = np.broadcast_to(sb_bias.reshape(1, 8), (128, 8))
    params[:, 146:154] = g_kv.reshape(8, 128).T
    params[:, 154] = np.arange(128)
    return params


def kernel(x_prompt, x_sample, mem_prompt, cache_sb_k, cache_sb_v, cache_mem_k, cache_mem_v, state_hgrn,
           page_table, g_norm, w_in_a, hg_lb, hg_norm, w_in_b, sb_bias, g_kv, w_kv, w_mem_kv, w_o, w_gu, w_down):
    global _NC
    f = lambda a: np.ascontiguousarray(np.asarray(a, dtype=np.float32))
    if _NC is None:
        _NC = build()
    nc = _NC
    consts = host_consts()
    g_norm = f(g_norm); hg_lb = f(hg_lb); hg_norm = f(hg_norm); sb_bias = f(sb_bias); g_kv = f(g_kv)
    params = make_params(g_norm, hg_lb, hg_norm, sb_bias, g_kv)
    x_prompt = f(x_prompt); mem_prompt = f(mem_prompt); x_sample = f(x_sample)
    npool = 2560
    if cache_sb_k is None:
        pk = np.zeros((npool * 128, 512), np.float32); pv = pk
        cache_mem_k = np.zeros((4, 128, 256, 4, 128), np.float32); cache_mem_v = cache_mem_k
        state_hgrn = np.zeros((2, 128, 4, 128, 128), np.float32)
        page_table = np.zeros((128, 16), np.int32)
    else:
        pk = f(cache_sb_k).reshape(npool * 128, 512); pv = f(cache_sb_v).reshape(npool * 128, 512)
    cache_mem_k = f(cache_mem_k); cache_mem_v = f(cache_mem_v); state_hgrn = f(state_hgrn)
    page_table = np.ascontiguousarray(np.asarray(page_table, dtype=np.int32))
    shared = {"consts": consts, "params": params, "w_in_a": f(w_in_a), "w_in_b": f(w_in_b), "w_kv": f(w_kv),
              "w_mem": f(w_mem_kv), "w_o": f(w_o), "w_gu": f(w_gu), "w_down": f(w_down), "pool_k": pk, "pool_v": pv}
    in_maps = []
    for c in range(NCORES):
        b = c % 2
        sl = slice(c * NS, (c + 1) * NS)
        m = dict(shared)
        m["xp"] = x_prompt[b]
        m["xs"] = np.ascontiguousarray(x_sample[sl, 0, :])
        m["memp"] = mem_prompt[b]
        m["cmk"] = np.ascontiguousarray(cache_mem_k[:, sl]).reshape(4, NS, 256, 512)
        m["cmv"] = np.ascontiguousarray(cache_mem_v[:, sl]).reshape(4, NS, 256, 512)
        m["st0"] = np.ascontiguousarray(state_hgrn[:, sl])
        m["ptab"] = np.ascontiguousarray(page_table[sl]).reshape(1, NS * 16)
        in_maps.append(m)
    res = run_bass_kernel_spmd(nc, in_maps, core_ids=list(range(NCORES)))
    r = list(res.results)
    while len(r) < 8:
        r.append(r[0])
    y_prompt = np.stack([r[0]["yp"], r[1]["yp"]]).astype(np.float32)
    st_p = np.stack([r[0]["stp"], r[1]["stp"]], axis=1).astype(np.float32)
    sbk = np.stack([r[0]["sbk_p"], r[1]["sbk_p"]]).reshape(2, SEQ, 4, 128).astype(np.float32)
    sbv = np.stack([r[0]["sbv_p"], r[1]["sbv_p"]]).reshape(2, SEQ, 4, 128).astype(np.float32)
    mk = np.stack([r[0]["mk_p"], r[1]["mk_p"]], axis=1).reshape(4, 2, 256, 4, 128).astype(np.float32)
    mv = np.stack([r[0]["mv_p"], r[1]["mv_p"]], axis=1).reshape(4, 2, 256, 4, 128).astype(np.float32)
    y_sample = np.concatenate([r[c]["ys"] for c in range(8)], axis=0).reshape(128, 1, D).astype(np.float32)
    st_s = np.concatenate([r[c]["sts"] for c in range(8)], axis=1).astype(np.float32)
    sbk_s = np.concatenate([r[c]["sbk_s"] for c in range(8)], axis=0).reshape(128, 1, 4, 128).astype(np.float32)
    sbv_s = np.concatenate([r[c]["sbv_s"] for c in range(8)], axis=0).reshape(128, 1, 4, 128).astype(np.float32)
    return (y_prompt, y_sample, st_p, st_s, sbk, sbv, sbk_s, sbv_s, mk, mv)
```

```python
from contextlib import ExitStack
import os
import numpy as np
import concourse.bass as bass
import concourse.mybir as mybir
from concourse.bass_utils import run_bass_kernel_spmd

F32 = mybir.dt.float32
BF16 = mybir.dt.bfloat16
I32 = mybir.dt.int32
AF = mybir.ActivationFunctionType
ALU = mybir.AluOpType

D = 1024
SEQ = 8192
T = 1024
NG = SEQ // T
DFF = 2816
NS = 16
EPS = 1e-6
SCALE = 128 ** -0.5
STOP = 100
KBLK_G = 2048
SAMPLE = True
NCORES = 8

CE = ['pe', 'act', 'dve', 'pool']
QE = ['sp', 'act', 'pool']
NSLOT = 8


class Sched:
    def __init__(self, nc, stack):
        self.nc = nc
        self.ops = {e: [] for e in ['pe', 'act', 'dve', 'pool', 'sp']}
        self.count = {e: 0 for e in CE}
        self.known = {e: {} for e in self.ops}
        self.last_w = {}
        self.readers = {}
        self.sem = {}
        for e in CE:
            self.sem[e] = stack.enter_context(nc.semaphore("c_" + e))
        self.dma_n = {q: 0 for q in QE}
        for q in QE:
            for s in range(NSLOT):
                self.sem[(q, s)] = stack.enter_context(nc.semaphore("d_%s%d" % (q, s)))

    def _need(self, eng, toks):
        best = {}
        for (k, v) in toks:
            if v <= 0:
                continue
            if best.get(k, 0) < v:
                best[k] = v
        out = []
        kn = self.known[eng]
        for k, v in best.items():
            if kn.get(k, 0) >= v:
                continue
            kn[k] = v
            out.append((k, v))
        return out

    def add(self, eng, fn, reads=(), writes=(), dma=False):
        toks = []
        for k in reads:
            t = self.last_w.get(k)
            if t is not None:
                toks.append(t)
            if k.startswith('ps'):
                toks.extend(t2 for t2 in self.readers.get(k, ()) if t2[0] != eng)
        for k in writes:
            t = self.last_w.get(k)
            if t is not None:
                toks.append(t)
            toks.extend(self.readers.get(k, ()))
        if eng == 'pe' and not dma:
            toks = [t for t in toks if t[0] != 'pe']
        if dma:
            n = self.dma_n[eng]
            self.dma_n[eng] = n + 1
            slot = (eng, n % NSLOT)
            val = 16 * (n // NSLOT + 1)
            toks.append((slot, val - 16))
            tok = (slot, val)
            inc = 16
        else:
            self.count[eng] += 1
            tok = (eng, self.count[eng])
            inc = 1
        waits = self._need(eng, toks)
        for k in writes:
            self.last_w[k] = tok
            self.readers[k] = []
        for k in reads:
            if k in writes:
                continue
            lst = self.readers.setdefault(k, [])
            lst[:] = [t for t in lst if t[0] != tok[0]] + [tok]
        self.ops[eng].append((waits, fn, tok, inc))
        return tok

    def barrier(self):
        toks = [(e, self.count[e]) for e in CE]
        for q in QE:
            n = self.dma_n[q]
            for s in range(NSLOT):
                cnt = (n - s + NSLOT - 1) // NSLOT if n > s else 0
                toks.append(((q, s), 16 * cnt))
        for e in self.ops:
            waits = self._need(e, list(toks))
            if waits:
                self.ops[e].append((waits, None, None, 0))

    def emit(self):
        nc = self.nc
        self.barrier()
        names = {'pe': 'tensor', 'act': 'scalar', 'dve': 'vector', 'pool': 'gpsimd', 'sp': 'sync'}
        with nc.Block() as block:
            for e, bn in names.items():
                ops = self.ops[e]

                def body(eng, ops=ops):
                    for (waits, fn, tok, inc) in ops:
                        for (k, v) in waits:
                            eng.wait_ge(self.sem[k], v)
                        if fn is not None:
                            ins = fn(eng)
                            ins.then_inc(self.sem[tok[0]], inc)
                getattr(block, bn)(body)


class Arena:
    def __init__(self, ap, nwords):
        self.ap = ap
        self.n = nwords
        self.top = 0

    def f32(self, words):
        a = self.top
        self.top += words
        assert self.top <= self.n, ("arena overflow", self.top, self.n)
        return self.ap[:, a:a + words]

    def bf(self, elems):
        w = (elems + 1) // 2
        return self.f32(w).bitcast(BF16)

    def i32(self, words):
        return self.f32(words).bitcast(I32)


NCONST = 128 + 128 + 128 + 128 + 4 * 512 + T + 256


def host_consts():
    c = np.zeros((128, NCONST), np.float32)
    o = 0
    c[:, o:o + 128] = np.eye(128); o += 128
    c[:, o:o + 128] = 1.0; o += 128
    j = np.arange(128)[:, None]; k = np.arange(128)[None, :]
    c[:, o:o + 128] = (j > k); o += 128
    c[:, o:o + 128] = ((j // 64 == k // 64) & (j <= k)); o += 128
    q = np.arange(512)[None, :]
    for r in range(4):
        c[:, o:o + 512] = ((r * 128 + j) < q); o += 512
    rm = np.ones(T, np.float32); rm[0::64] = 0.0
    c[:, o:o + T] = rm[None, :]; o += T
    c[:, o:o + 128] = -1.0 * (j >= k); o += 128
    c[:, o:o + 128] = -1.0; o += 128
    assert o == NCONST
    return c


def build():
    nc = bass.Bass("TRN2", target_bir_lowering=False)

    def din(name, shape, dt=F32):
        return nc.dram_tensor(name, list(shape), dt, kind="ExternalInput").ap()

    def dout(name, shape):
        return nc.dram_tensor(name, list(shape), F32, kind="ExternalOutput").ap()

    xp = din("xp", [SEQ, D]); xs = din("xs", [NS, D]); memp = din("memp", [256, D])
    consts = din("consts", [128, NCONST]); params = din("params", [128, 160])
    w_in_a = din("w_in_a", [2, D, 2560]); w_in_b = din("w_in_b", [2, D, D]); w_kv = din("w_kv", [D, D])
    w_mem = din("w_mem", [4, D, D]); w_o = din("w_o", [4, D, D]); w_gu = din("w_gu", [4, D, 2 * DFF])
    w_down = din("w_down", [4, DFF, D])
    yp = dout("yp", [SEQ, D]); stp = dout("stp", [2, 4, 128, 128])
    sbk_p = dout("sbk_p", [SEQ, 512]); sbv_p = dout("sbv_p", [SEQ, 512])
    mk_p = dout("mk_p", [4, 256, 512]); mv_p = dout("mv_p", [4, 256, 512])
    NPOOL = 2560
    pool_k = din("pool_k", [NPOOL * 128, 512]); pool_v = din("pool_v", [NPOOL * 128, 512])
    cmk = din("cmk", [4, NS, 256, 512]); cmv = din("cmv", [4, NS, 256, 512])
    st0 = din("st0", [2, NS, 4, 128, 128]); ptab = din("ptab", [1, NS * 16], I32)
    ys = dout("ys", [NS, D]); sts = dout("sts", [2, NS, 4, 128, 128])
    sbk_s = dout("sbk_s", [NS, 512]); sbv_s = dout("sbv_s", [NS, 512])
    ktc = nc.dram_tensor("ktc", [128, 4, SEQ], BF16).ap()
    vc = nc.dram_tensor("vc", [SEQ, 512], BF16).ap()

    st = ExitStack()
    with st:
        S = Sched(nc, st)
        NW = 52000
        arena_t = st.enter_context(nc.sbuf_tensor("arena", [128, NW], F32))
        A = Arena(arena_t, NW)
        PS = [st.enter_context(nc.psum_tensor("ps%d" % i, [128, 512], F32)) for i in range(8)]

        def PSB(i, n=512, dt=None):
            ap = PS[i][:, 0:n]
            return ap

        def mm(out, lhsT, rhs, start, stop, r, w):
            S.add('pe', lambda e: e.matmul(out, lhsT=lhsT, rhs=rhs, start=start, stop=stop), reads=r, writes=w)

        def mm3(out, lhsT, rhs, start, stop, r, w):
            S.add('pe', lambda e: e.matmul(out, lhsT=lhsT, rhs=rhs, start=start, stop=stop, skip_group_check=True), reads=r, writes=w)

        def tr(out, in_, ident, r, w):
            S.add('pe', lambda e: e.transpose(out, in_, ident), reads=r, writes=w)

        def act(out, in_, func, r, w, bias=None, scale=None):
            kw = {}
            if bias is not None:
                kw['bias'] = bias
            if scale is not None:
                kw['scale'] = scale
            S.add('act', lambda e: e.activation(out=out, in_=in_, func=func, **kw), reads=r, writes=w)

        def tt(eng, out, in0, in1, op, r, w):
            S.add(eng, lambda e: e.tensor_tensor(out=out, in0=in0, in1=in1, op=op), reads=r, writes=w)

        def ts(eng, out, in0, s1, s2, op0, op1, r, w):
            if s2 is None:
                S.add(eng, lambda e: e.tensor_scalar(out=out, in0=in0, scalar1=s1, scalar2=None, op0=op0), reads=r, writes=w)
            else:
                S.add(eng, lambda e: e.tensor_scalar(out=out, in0=in0, scalar1=s1, scalar2=s2, op0=op0, op1=op1), reads=r, writes=w)

        def stt(out, in0, scalar, in1, op0, op1, r, w):
            S.add('dve', lambda e: e.scalar_tensor_tensor(out=out, in0=in0, scalar=scalar, in1=in1, op0=op0, op1=op1), reads=r, writes=w)

        def cp(eng, out, in_, r, w):
            if eng == 'act':
                S.add('act', lambda e: e.copy(out=out, in_=in_), reads=r, writes=w)
            else:
                S.add(eng, lambda e: e.tensor_copy(out=out, in_=in_), reads=r, writes=w)

        def dma(q, out, in_, r, w):
            S.add(q, lambda e: e.dma_start(out=out, in_=in_), reads=r, writes=w, dma=True)

        ident = A.f32(128)
        resetm = A.f32(T)
        prm = A.f32(160)
        gn = prm[:, 0:128]
        lbp = prm[:, 128:136]
        hgn = prm[:, 136:138]
        sbb = prm[:, 138:146]
        gkv = prm[:, 146:154]
        lb1 = A.f32(4); oml1 = A.f32(4)
        ones_ff = A.f32(128); U_ff = A.f32(128)
        iota_c = prm[:, 154:155]
        identb = A.bf(128); onesb = A.bf(128); Ub = A.bf(128); hmaskb = A.bf(128); NUb = A.bf(128); NOb = A.bf(128)
        sbmaskb = [A.bf(512) for r in range(4)]
        h = A.f32(8 * T).rearrange("p (c t) -> p c t", c=8)
        MKT = [A.bf(4 * 256).rearrange("p (h m) -> p h m", h=4) for l in range(4)]
        MV = [A.bf(2 * 512).rearrange("p (t c) -> p t c", t=2) for l in range(4)]
        Sst = [[A.f32(128) for hd in range(4)] for l in range(2)]
        base_top = A.top
        hn = A.bf(8 * T).rearrange("p (c t) -> p c t", c=8)
        mixw = A.f32(4 * T)
        mix = mixw.bitcast(BF16).rearrange("p (c t) -> p c t", c=8)
        wp = [A.bf(8 * 512).rearrange("p (c n) -> p c n", c=8) for i in range(2)]
        ytile = A.f32(8 * 512).rearrange("p (c t) -> p c t", c=8)
        sqt = A.bf(8 * 512).rearrange("p (c t) -> p c t", c=8)
        rstd = A.f32(512)
        tmpf = A.f32(512)
        mix_top = A.top
        print("arena persistent", base_top, "common", mix_top)

        ctop = A.top
        cst = A.f32(NCONST)
        o = 0
        ident_t = cst[:, o:o + 128]; o += 128
        ones_f = cst[:, o:o + 128]; o += 128
        U_f = cst[:, o:o + 128]; o += 128
        hmask_f = cst[:, o:o + 128]; o += 128
        sbmask_f = [cst[:, o + r * 512:o + (r + 1) * 512] for r in range(4)]; o += 2048
        resetm_t = cst[:, o:o + T]; o += T
        nup_f = cst[:, o:o + 128]; o += 128
        nones_f = cst[:, o:o + 128]; o += 128
        dma('sp', cst, consts, [], ['cst0'])
        dma('sp', prm, params, [], ['prm'])
        cp('dve', ident, ident_t, ['cst0'], ['cst'])
        cp('dve', resetm, resetm_t, ['cst0'], ['cst'])
        cp('dve', identb, ident_t, ['cst0'], ['identb'])
        cp('dve', ones_ff, ones_f, ['cst0'], ['cst'])
        cp('dve', U_ff, U_f, ['cst0'], ['cst'])
        cp('dve', onesb, ones_f, ['cst0'], ['onesb'])
        cp('dve', Ub, U_f, ['cst0'], ['Ub'])
        cp('dve', NUb, nup_f, ['cst0'], ['NUb'])
        cp('dve', NOb, nones_f, ['cst0'], ['NOb'])
        cp('dve', hmaskb, hmask_f, ['cst0'], ['hmaskb'])
        for r in range(4):
            cp('dve', sbmaskb[r], sbmask_f[r], ['cst0'], ['sbmaskb'])
        S.barrier()
        A.top = ctop
        tt('dve', lb1, lbp[:, 4:8], lbp[:, 0:4], ALU.subtract, ['prm'], ['lb1'])
        act(lb1, lb1, AF.Sigmoid, ['lb1'], ['lb1'])
        ts('dve', oml1, lb1, -1.0, 1.0, ALU.mult, ALU.add, ['lb1'], ['oml1'])
        for l in range(2):
            for hd in range(4):
                S.add('pool', lambda e, l=l, hd=hd: e.memset(Sst[l][hd], 0.0), writes=['S%d%d' % (l, hd)])

        wq = {'n': 0, 's': 0}

        WSTAGE = False
        if WSTAGE:
            wstage = [A.f32(8 * 512).rearrange("p (c n) -> p c n", c=8) for i in range(2)]

        def wload(dst, src_ap, key, pat="(c p) n -> p c n"):
            if not WSTAGE:
                S.add('pool', lambda e: e.dma_start(out=dst, in_=src_ap.rearrange(pat, p=128)),
                      reads=[], writes=[key], dma=True)
            else:
                i = wq['s'] % 2
                wq['s'] += 1
                kc, n = dst.shape[1], dst.shape[2]
                stg = wstage[i][:, 0:kc, 0:n] if kc <= 8 else None
                assert stg is not None
                S.add('sp', lambda e: e.dma_start(out=stg, in_=src_ap.rearrange(pat, p=128)),
                      reads=[], writes=['wstage%d' % i], dma=True)
                S.add('pool', lambda e: e.tensor_copy(out=dst, in_=stg), reads=['wstage%d' % i], writes=[key])

        def load_w(src_ap, kc, ncols):
            i = wq['n'] % 2
            wq['n'] += 1
            dst = wp[i][:, 0:kc, 0:ncols]
            key = 'wp%d' % i
            wload(dst, src_ap, key)
            return dst, key

        def sumsq_rstd(src_tile, nch, r, inv_n, bank=2, ncol=512):
            act(sqt[:, 0:nch, 0:ncol], src_tile, AF.Square, r, ['sqt'])
            for c in range(nch):
                mm(PS[bank][:, 0:ncol], onesb, sqt[:, c, 0:ncol], c == 0, c == nch - 1, ['sqt', 'onesb'], ['ps%d' % bank])
            act(tmpf[:, 0:ncol], PS[bank][:, 0:ncol], AF.Sqrt, ['ps%d' % bank], ['tmpf'], bias=EPS, scale=inv_n)
            S.add('dve', lambda e: e.reciprocal(out=rstd[:, 0:ncol], in_=tmpf[:, 0:ncol]), reads=['tmpf'], writes=['rstd'])

        def norm_to_bf(dst, l, i, tiles, gvec=None):
            for t2, (t0, n) in enumerate(tiles):
                sl = slice(t0, t0 + n)
                sumsq_rstd(h[:, :, sl], 8, ['h%d' % t2], 1.0 / D, ncol=n)
                for c in range(8):
                    g = gvec[:, c:c + 1] if gvec is not None else gn[:, (l * 4 + i) * 8 + c:(l * 4 + i) * 8 + c + 1]
                    stt(dst[:, c, sl], h[:, c, sl], g, rstd[:, 0:n], ALU.mult, ALU.mult, ['h%d' % t2, 'rstd', 'prm'], ['hn%d' % t2])

        def postnorm_add(l, i, t2, t0, n):
            sl = slice(t0, t0 + n)
            sumsq_rstd(ytile[:, :, 0:n], 8, ['ytile'], 1.0 / D, ncol=n)
            for c in range(8):
                g = gn[:, (l * 4 + i) * 8 + c:(l * 4 + i) * 8 + c + 1]
                stt(ytile[:, c, 0:n], ytile[:, c, 0:n], g, rstd[:, 0:n], ALU.mult, ALU.mult, ['ytile', 'rstd', 'prm'], ['ytile'])
                tt('pool', h[:, c, sl], h[:, c, sl], ytile[:, c, 0:n], ALU.add, ['ytile', 'h%d' % t2], ['h%d' % t2])

        def mem_setup():
            top = A.top
            mraw = A.f32(2 * 1024).rearrange("p (t d) -> p t d", t=2)
            memT = A.bf(8 * 256).rearrange("p (c m) -> p c m", c=8)
            mko = A.f32(512)
            dma('sp', mraw, memp.rearrange("(t p) d -> p t d", p=128), [], ['mraw'])
            for mt in range(2):
                for c in range(8):
                    tr(PS[3][:, 0:128], mraw[:, mt, c * 128:(c + 1) * 128], ident, ['mraw', 'cst'], ['ps3'])
                    cp('dve', memT[:, c, mt * 128:(mt + 1) * 128], PS[3][:, 0:128], ['ps3'], ['memT'])
            for l in range(4):
                for kv in range(2):
                    wpc, wk = load_w(w_mem[l][:, kv * 512:(kv + 1) * 512], 8, 512)
                    outd = mk_p if kv == 0 else mv_p
                    for mt in range(2):
                        for c in range(8):
                            mm(PS[0][:, :], memT[:, c, mt * 128:(mt + 1) * 128], wpc[:, c, :], c == 0, c == 7, ['memT', wk], ['ps0'])
                        cp('act', mko, PS[0][:, :], ['ps0'], ['mko'])
                        if kv == 1:
                            cp('dve', MV[l][:, mt, :], PS[0][:, :], ['ps0'], ['MV%d' % l])
                        dma('sp', outd[l][mt * 128:(mt + 1) * 128, :], mko, ['mko'], [])
                    if kv == 0:
                        for hd in range(4):
                            for c in range(8):
                                mm(PS[1][:, 0:256], wpc[:, c, hd * 128:(hd + 1) * 128], memT[:, c, :], c == 0, c == 7, ['memT', wk], ['ps1'])
                            cp('dve', MKT[l][:, hd, :], PS[1][:, 0:256], ['ps1'], ['MKT%d' % l])
            S.barrier()
            A.top = top

        def mem_attn(l, hd, QM, TT, etile):
            for t2 in range(TT):
                sl = slice(t2 * 512, (t2 + 1) * 512)
                for mt in range(2):
                    mm(PS[4 + mt][:, :], MKT[l][:, hd, mt * 128:(mt + 1) * 128], QM[:, sl], True, True, ['QM', 'MKT%d' % l], ['ps%d' % (4 + mt)])
                    act(etile[mt], PS[4 + mt][:, :], AF.Exp, ['ps%d' % (4 + mt)], ['et%d' % mt], scale=SCALE)
                for mt in range(2):
                    mm(PS[6][:, :], MV[l][:, mt, hd * 128:(hd + 1) * 128], etile[mt], mt == 0, mt == 1, ['et%d' % mt, 'MV%d' % l], ['ps6'])
                for mt in range(2):
                    mm(PS[7][:, :], onesb, etile[mt], mt == 0, mt == 1, ['et%d' % mt, 'onesb'], ['ps7'])
                S.add('dve', lambda e: e.reciprocal(out=tmpf, in_=PS[7][:, :]), reads=['ps7'], writes=['tmpf'])
                tt('dve', mix[:, 4 + hd, sl], PS[6][:, :], tmpf, ALU.mult, ['ps6', 'tmpf'], ['mix'])

        def out_proj(l, tiles):
            top = A.top
            wo = A.bf(8 * 1024).rearrange("p (c n) -> p c n", c=8)
            for half in range(2):
                S.add('pool', lambda e, half=half: e.dma_start(out=wo[:, :, half * 512:(half + 1) * 512],
                                                               in_=w_o[l][:, half * 512:(half + 1) * 512].rearrange("(c p) n -> p c n", p=128)),
                      reads=[], writes=['wo'], dma=True)
            for t2, (t0, n) in enumerate(tiles):
                sl = slice(t0, t0 + n)
                for nn in range(8):
                    b = nn % 2
                    for c in range(8):
                        mm(PS[b][:, 0:n], wo[:, c, nn * 128:(nn + 1) * 128], mix[:, c, sl], c == 0, c == 7, ['wo', 'mix'], ['ps%d' % b])
                    cp('act', ytile[:, nn, 0:n], PS[b][:, 0:n], ['ps%d' % b], ['ytile'])
                postnorm_add(l, 1, t2, t0, n)
            S.barrier()
            A.top = top

        def ffn(l, tiles):
            top = A.top
            Tt = tiles[-1][0] + tiles[-1][1]
            actb = A.bf(22 * Tt).rearrange("p (j t) -> p j t", j=22)
            wd = [A.bf(22 * 128).rearrange("p (j n) -> p j n", j=22) for i in range(3)]
            sgs = [sqt.bitcast(F32).rearrange("p c t -> p (c t)")[:, 0:512], sqt.bitcast(F32).rearrange("p c t -> p (c t)")[:, 512:1024]]
            norm_to_bf(hn, l, 2, tiles)
            for jp in range(11):
                i = wq['n'] % 2
                wq['n'] += 1
                key = 'wp%d' % i
                for gu in range(2):
                    S.add('pool', lambda e, i=i, gu=gu, jp=jp: e.dma_start(
                        out=wp[i][:, :, gu * 256:(gu + 1) * 256],
                        in_=w_gu[l][:, gu * DFF + jp * 256:gu * DFF + (jp + 1) * 256].rearrange("(c p) n -> p c n", p=128)),
                        reads=[], writes=[key], dma=True)
                for jj in range(2):
                    j = jp * 2 + jj
                    for t2, (t0, n) in enumerate(tiles):
                        sl = slice(t0, t0 + n)
                        it = (j * len(tiles) + t2) % 2
                        bg = 0 if it == 0 else 4
                        bu = bg + 1
                        for c in range(8):
                            mm(PS[bg][:, 0:n], wp[i][:, c, jj * 128:(jj + 1) * 128], hn[:, c, sl], c == 0, c == 7, [key, 'hn%d' % t2], ['ps%d' % bg])
                        for c in range(8):
                            mm(PS[bu][:, 0:n], wp[i][:, c, 256 + jj * 128:256 + (jj + 1) * 128], hn[:, c, sl], c == 0, c == 7, [key, 'hn%d' % t2], ['ps%d' % bu])
                        act(sgs[it][:, 0:n], PS[bg][:, 0:n], AF.Silu, ['ps%d' % bg], ['sg%d' % it])
                        tt('dve', actb[:, j, sl], PS[bu][:, 0:n], sgs[it][:, 0:n], ALU.mult, ['ps%d' % bu, 'sg%d' % it], ['actb'])
            for nn in range(8):
                i = nn % 3
                key = 'wd%d' % i
                S.add('pool', lambda e, i=i, nn=nn: e.dma_start(
                    out=wd[i], in_=w_down[l][:, nn * 128:(nn + 1) * 128].rearrange("(j p) n -> p j n", p=128)),
                    reads=[], writes=[key], dma=True)
                for t2, (t0, n) in enumerate(tiles):
                    sl = slice(t0, t0 + n)
                    b = [2, 3, 6, 7][(nn * len(tiles) + t2) % 4]
                    for j in range(22):
                        mm(PS[b][:, 0:n], wd[i][:, j, :], actb[:, j, sl], j == 0, j == 21, [key, 'actb'], ['ps%d' % b])
                    cp('act', y2[t2][:, nn, 0:n], PS[b][:, 0:n], ['ps%d' % b], ['y2_%d' % t2])
            for t2, (t0, n) in enumerate(tiles):
                sl = slice(t0, t0 + n)
                sumsq_rstd(y2[t2][:, :, 0:n], 8, ['y2_%d' % t2], 1.0 / D, bank=4, ncol=n)
                for c in range(8):
                    g = gn[:, (l * 4 + 3) * 8 + c:(l * 4 + 3) * 8 + c + 1]
                    stt(y2[t2][:, c, 0:n], y2[t2][:, c, 0:n], g, rstd[:, 0:n], ALU.mult, ALU.mult, ['y2_%d' % t2, 'rstd', 'prm'], ['y2_%d' % t2])
                    tt('pool', h[:, c, sl], h[:, c, sl], y2[t2][:, c, 0:n], ALU.add, ['y2_%d' % t2, 'h%d' % t2], ['h%d' % t2])
            S.barrier()
            A.top = top

        y2 = [ytile, mixw.rearrange("p (c t) -> p c t", c=8)]
        common_top = A.top

        def a_mixer(l, g, TT):
            top = A.top
            Tg = TT * 512
            NT = Tg // 128
            NCK = Tg // 64
            Vt = A.bf(NT * 512).rearrange("p (t c) -> p t c", t=NT)
            QS = A.f32(Tg); LF = A.f32(Tg); KK = A.f32(Tg); TM = A.f32(Tg)
            QT = A.bf(Tg); KT = A.bf(Tg); KH = A.bf(Tg); G = A.bf(Tg); QM = A.bf(Tg)
            O = A.f32(Tg)
            SB16 = A.bf((NCK + 1) * 128).rearrange("p (k v) -> p k v", k=NCK + 1)
            KHT = A.bf(128); AM = A.bf(128)
            dec = A.f32(NCK)
            et = [A.bf(512) for i in range(2)]
            norm_to_bf(hn, l, 0, PT)
            wpc, wk = load_w(w_in_a[l][:, 1024:1536], 8, 512)
            for t in range(NT):
                b = t % 2
                for c in range(8):
                    mm(PS[b][:, :], hn[:, c, t * 128:(t + 1) * 128], wpc[:, c, :], c == 0, c == 7, ['hn%d' % (t // 4), wk], ['ps%d' % b])
                cp('act', Vt[:, t, :], PS[b][:, :], ['ps%d' % b], ['Vt'])
            for hd in range(4):
                i = wq['n'] % 2
                wq['n'] += 1
                key = 'wp%d' % i
                for pi, off in enumerate([0, 512, 1536, 2048]):
                    S.add('pool', lambda e, i=i, pi=pi, off=off, hd=hd: e.dma_start(
                        out=wp[i][:, :, pi * 128:(pi + 1) * 128],
                        in_=w_in_a[l][:, off + hd * 128:off + (hd + 1) * 128].rearrange("(c p) n -> p c n", p=128)),
                        reads=[], writes=[key], dma=True)
                lbc = lb1[:, hd:hd + 1]
                omc = oml1[:, hd:hd + 1]
                for t2 in range(TT):
                    sl = slice(t2 * 512, (t2 + 1) * 512)
                    for pi in range(4):
                        b = pi % 2
                        for c in range(8):
                            mm(PS[b][:, :], wp[i][:, c, pi * 128:(pi + 1) * 128], hn[:, c, sl], c == 0, c == 7, [key, 'hn%d' % t2], ['ps%d' % b])
                        if pi == 0:
                            act(QS[:, sl], PS[b][:, :], AF.Silu, ['ps%d' % b], ['QS'])
                        elif pi == 1:
                            act(TM[:, sl], PS[b][:, :], AF.Sigmoid, ['ps%d' % b], ['TM'])
                        elif pi == 2:
                            act(G[:, sl], PS[b][:, :], AF.Silu, ['ps%d' % b], ['G'])
                        else:
                            cp('act', QM[:, sl], PS[b][:, :], ['ps%d' % b], ['QM'])
                if l == 1:
                    ts('dve', TM, TM, omc, lbc, ALU.mult, ALU.add, ['TM', 'lb1', 'oml1'], ['TM'])
                act(LF, TM, AF.Ln, ['TM'], ['LF'])
                ts('dve', KK, TM, -1.0, 1.0, ALU.mult, ALU.add, ['TM'], ['KK'])
                S.add('dve', lambda e: e.tensor_tensor_scan(out=TM, data0=resetm[:, 0:Tg], data1=LF, initial=0.0, op0=ALU.mult, op1=ALU.add),
                      reads=['LF', 'cst'], writes=['TM'])
                act(LF, TM, AF.Exp, ['TM'], ['LF'])
                tt('dve', QT, QS, LF, ALU.mult, ['QS', 'LF'], ['QT'])
                act(LF, TM, AF.Exp, ['TM', 'QT'], ['LF'], scale=-1.0)
                tt('dve', KT, KK, LF, ALU.mult, ['KK', 'LF'], ['KT'])
                b3 = TM.rearrange("p (k s) -> p k s", s=64)
                act(dec, b3[:, :, 63], AF.Exp, ['TM'], ['dec'])
                tt('dve', LF.rearrange("p (k s) -> p k s", s=64), b3[:, :, 63:64].to_broadcast([128, NCK, 64]), b3, ALU.subtract, ['TM', 'KT'], ['LF'])
                act(LF, LF, AF.Exp, ['LF'], ['LF'])
                tt('dve', KH, KK, LF, ALU.mult, ['KK', 'LF'], ['KH'])
                skey = 'S%d%d' % (l, hd)
                Sm = Sst[l][hd]
                cp('act', SB16[:, 0, :], Sm, [skey], ['SB16'])
                for t in range(NT):
                    tr(PS[3][:, 0:64].bitcast(BF16), KH[:, t * 128:(t + 1) * 128], identb, ['KH', 'identb'], ['ps3'])
                    cp('dve', KHT, PS[3][:, 0:64].bitcast(BF16), ['ps3'], ['KHT'])
                    for cc in range(2):
                        ck = t * 2 + cc
                        mm(PS[2][:, 0:128], KHT[cc * 64:(cc + 1) * 64, :], Vt[cc * 64:(cc + 1) * 64, t, hd * 128:(hd + 1) * 128], True, True, ['KHT', 'Vt'], ['ps2'])
                        stt(Sm, Sm, dec[:, ck:ck + 1], PS[2][:, 0:128], ALU.mult, ALU.add, [skey, 'dec', 'ps2'], [skey])
                        cp('act', SB16[:, ck + 1, :], Sm, [skey], ['SB16'])
                if g == NG - 1:
                    dma('sp', stp[l][hd], Sm, [skey], [])
                for t in range(NT):
                    tsl = slice(t * 128, (t + 1) * 128)
                    mm(PS[4][:, 0:128], KT[:, tsl], QT[:, tsl], True, True, ['KT', 'QT'], ['ps4'])
                    tt('dve', AM, PS[4][:, 0:128], hmaskb, ALU.mult, ['ps4', 'hmaskb'], ['AM'])
                    ob = 5 + (t // 4) % 2
                    oc = (t % 4) * 128
                    mm(PS[ob][:, oc:oc + 128], Vt[:, t, hd * 128:(hd + 1) * 128], AM, True, False, ['Vt', 'AM'], ['ps%d' % ob])
                    for cc in range(2):
                        ck = t * 2 + cc
                        mm(PS[ob][:, oc + cc * 64:oc + (cc + 1) * 64], SB16[:, ck, :], QT[:, t * 128 + cc * 64:t * 128 + (cc + 1) * 64], False, cc == 1,
                           ['SB16', 'QT'], ['ps%d' % ob])
                    if t % 4 == 3:
                        t2 = t // 4
                        cp('act', O[:, t2 * 512:(t2 + 1) * 512], PS[ob][:, :], ['ps%d' % ob], ['O'])
                for t2 in range(TT):
                    sl = slice(t2 * 512, (t2 + 1) * 512)
                    act(sqt[:, 0, :], O[:, sl], AF.Square, ['O'], ['sqt'])
                    mm(PS[7][:, :], onesb, sqt[:, 0, :], True, True, ['sqt', 'onesb'], ['ps7'])
                    act(tmpf, PS[7][:, :], AF.Sqrt, ['ps7'], ['tmpf'], bias=EPS, scale=1.0 / 128)
                    S.add('dve', lambda e: e.reciprocal(out=rstd, in_=tmpf), reads=['tmpf'], writes=['rstd'])
                    stt(O[:, sl], O[:, sl], hgn[:, l:l + 1], rstd, ALU.mult, ALU.mult, ['O', 'rstd', 'prm'], ['O'])
                    tt('dve', mix[:, hd, sl], O[:, sl], G[:, sl], ALU.mult, ['O', 'G'], ['mix'])
                mem_attn(l, hd, QM, TT, et)
            S.barrier()
            A.top = top

        def kv_proj(g, TT):
            top = A.top
            Tg = TT * 512
            NT = Tg // 128
            ko = [A.f32(512) for i in range(2)]
            vb = [A.bf(512) for i in range(2)]
            ktb = A.bf(Tg)
            norm_to_bf(hn, 0, 0, PT, gvec=gkv)
            for kv in range(2):
                wpc, wk = load_w(w_kv[:, kv * 512:(kv + 1) * 512], 8, 512)
                outd = sbk_p if kv == 0 else sbv_p
                for t in range(NT):
                    b = t % 2
                    r0 = g * T + t * 128
                    for c in range(8):
                        mm(PS[b][:, :], hn[:, c, t * 128:(t + 1) * 128], wpc[:, c, :], c == 0, c == 7, ['hn%d' % (t // 4), wk], ['ps%d' % b])
                    cp('act', ko[b], PS[b][:, :], ['ps%d' % b], ['ko%d' % b])
                    dma('sp', outd[r0:r0 + 128, :], ko[b], ['ko%d' % b], [])
                    if kv == 1:
                        cp('dve', vb[b], PS[b][:, :], ['ps%d' % b], ['vb%d' % b])
                        dma('sp', vc[r0:r0 + 128, :], vb[b], ['vb%d' % b], ['vc'])
                if kv == 0:
                    for hd in range(4):
                        for t2 in range(TT):
                            sl = slice(t2 * 512, (t2 + 1) * 512)
                            b = 2 + t2 % 2
                            for c in range(8):
                                mm(PS[b][:, :], wpc[:, c, hd * 128:(hd + 1) * 128], hn[:, c, sl], c == 0, c == 7, [wk, 'hn%d' % t2], ['ps%d' % b])
                            ts('dve', ktb[:, sl], PS[b][:, :], SCALE, None, ALU.mult, None, ['ps%d' % b], ['ktb'])
                        dma('sp', ktc[:, hd, g * T:g * T + Tg], ktb, ['ktb'], ['ktc'])
            S.barrier()
            A.top = top

        def b_mixer(l, g, TT):
            top = A.top
            Tg = TT * 512
            lb_ = l - 2
            QB = A.bf(4 * Tg).rearrange("p (h t) -> p h t", h=4)
            QM = A.bf(Tg)
            et = [A.bf(512) for i in range(2)]
            KBLK = KBLK_G
            KB = [A.bf(KBLK) for i in range(2)]
            VB = [A.bf(16 * 128).rearrange("p (t v) -> p t v", t=16) for i in range(2)]
            NB = 3
            ef = [A.f32(512) for i in range(NB)]; spb = [A.bf(512) for i in range(NB)]; wbb = [A.bf(512) for i in range(NB)]
            Sacc = A.f32(512); Sb = [A.bf(512) for i in range(2)]
            norm_to_bf(hn, l, 0, PT)
            for half in range(2):
                wpc, wk = load_w(w_in_b[lb_][:, half * 512:(half + 1) * 512], 8, 512)
                for hd in range(4):
                    for t2 in range(TT):
                        sl = slice(t2 * 512, (t2 + 1) * 512)
                        b = (hd * TT + t2) % 2
                        for c in range(8):
                            mm(PS[b][:, :], wpc[:, c, hd * 128:(hd + 1) * 128], hn[:, c, sl], c == 0, c == 7, [wk, 'hn%d' % t2], ['ps%d' % b])
                        if half == 0:
                            cp('act', QB[:, hd, sl], PS[b][:, :], ['ps%d' % b], ['QB'])
                        else:
                            cp('act', QM[:, sl], PS[b][:, :], ['ps%d' % b], ['QM'])
                    if half == 1:
                        mem_attn(l, hd, QM, TT, et)
            jobs = []
            blocks = []
            for t2 in range(TT):
                P0 = g * T + t2 * 512
                nkeys = P0 + 512
                nkt = nkeys // 128
                for hd in range(4):
                    tl = []
                    nblk = (nkeys + KBLK - 1) // KBLK
                    for kb in range(nblk - 1, -1, -1):
                        k0 = kb * KBLK
                        k1 = min(nkeys, k0 + KBLK)
                        bi_ = len(blocks)
                        blocks.append((hd, k0, k1))
                        for kt in range(k1 // 128 - 1, k0 // 128 - 1, -1):
                            tl.append((kt, bi_, kt - k0 // 128, kt - (nkt - 4)))
                    jobs.append((t2, hd, tl))

            def load_block(m):
                hd_, k0, k1 = blocks[m]
                i = m % 2
                dma('sp', KB[i][:, 0:k1 - k0], ktc[:, hd_, k0:k1], ['ktc'], ['KB%d' % i])
                dma('sp', VB[i][:, 0:(k1 - k0) // 128, :], vc[k0:k1, hd_ * 128:(hd_ + 1) * 128].rearrange("(t p) v -> p t v", p=128), ['vc'], ['VB%d' % i])

            load_block(0)
            loaded = 1
            rr = 0
            for ji, (t2, hd, tl) in enumerate(jobs):
                sl = slice(t2 * 512, (t2 + 1) * 512)
                bias = sbb[:, lb_ * 4 + hd:lb_ * 4 + hd + 1]
                ob = 3 if ji % 2 == 0 else 7
                nt = len(tl)

                def stageA(idx, rr0=rr):
                    kt, m, kl, rdiag = tl[idx]
                    i = m % 2
                    r = rr0 + idx
                    zb = 4 + r % 3
                    bi = r % NB
                    zk = 'ps%d' % zb
                    mm3(PS[zb][:, :], KB[i][:, kl * 128:(kl + 1) * 128], QB[:, hd, sl], True, False, ['KB%d' % i, 'QB'], [zk])
                    act(ef[bi], PS[zb][:, :], AF.Exp, [zk], ['ef%d' % bi], bias=bias)
                    act(spb[bi], ef[bi], AF.Ln, ['ef%d' % bi], ['spb%d' % bi], bias=1.0)
                    if rdiag >= 0:
                        tt('pool', spb[bi], spb[bi], sbmaskb[rdiag], ALU.mult, ['spb%d' % bi, 'sbmaskb'], ['spb%d' % bi])
                    mm3(PS[zb][:, :], NUb, spb[bi], False, idx == 0, ['NUb', 'spb%d' % bi], [zk])
                    if idx > 0:
                        mm3(PS[zb][:, :], NOb, Sb[(idx - 1) % 2], False, True, ['NOb', 'Sb%d' % ((idx - 1) % 2)], [zk])
                    if idx < nt - 1:
                        if idx == 0:
                            cp('dve', Sacc, spb[bi], ['spb%d' % bi], ['Sacc'])
                        else:
                            tt('dve', Sacc, Sacc, spb[bi], ALU.add, ['Sacc', 'spb%d' % bi], ['Sacc'])
                        cp('dve', Sb[idx % 2], Sacc, ['Sacc'], ['Sb%d' % (idx % 2)])

                def stageB(idx, rr0=rr):
                    kt, m, kl, rdiag = tl[idx]
                    i = m % 2
                    r = rr0 + idx
                    zb = 4 + r % 3
                    bi = r % NB
                    zk = 'ps%d' % zb
                    act(wbb[bi], PS[zb][:, :], AF.Exp, [zk], ['wb%d' % bi], bias=bias)
                    if rdiag >= 0:
                        tt('pool', wbb[bi], wbb[bi], sbmaskb[rdiag], ALU.mult, ['wb%d' % bi, 'sbmaskb'], ['wb%d' % bi])
                    mm(PS[ob][:, :], VB[i][:, kl, :], wbb[bi], idx == 0, idx == nt - 1, ['VB%d' % i, 'wb%d' % bi], ['ps%d' % ob])

                for idx in range(nt):
                    m = tl[idx][1]
                    stageA(idx)
                    if idx > 0:
                        stageB(idx - 1)
                    if m + 1 >= loaded and m + 1 < len(blocks):
                        load_block(m + 1)
                        loaded = m + 2
                stageB(nt - 1)
                rr += nt
                cp('act', mix[:, hd, sl], PS[ob][:, :], ['ps%d' % ob], ['mix'])
            S.barrier()
            A.top = top


        ST = [(0, NS)]
        NC2 = 4 * NS
        AX = mybir.AxisListType.X

        def flat2(ap3):
            return ap3.rearrange("p c t -> p (c t)")

        def mem_attn_s(l, QM2):
            top = A.top
            Kc = [A.f32(1024).rearrange("p (t c) -> p t c", t=2) for i in range(2)]
            Vc = [A.f32(1024).rearrange("p (t c) -> p t c", t=2) for i in range(2)]
            KcT = A.f32(1024).rearrange("p (h m) -> p h m", h=4)
            E2 = A.f32(16).rearrange("p (c t) -> p c t", t=2)
            om32 = A.f32(32)
            dn = A.f32(4); rd = A.f32(4)
            for s_ in range(NS):
                i = s_ % 2
                dma('sp', Kc[i], cmk[l][s_].rearrange("(t p) c -> p t c", p=128), [], ['Kc%d' % i])
                dma('sp', Vc[i], cmv[l][s_].rearrange("(t p) c -> p t c", p=128), [], ['Vc%d' % i])
                for hd in range(4):
                    for mt in range(2):
                        q = hd * 2 + mt
                        bk = 4 + q // 4
                        tr(PS[bk][:, (q % 4) * 128:(q % 4 + 1) * 128], Kc[i][:, mt, hd * 128:(hd + 1) * 128], ident, ['Kc%d' % i, 'cst'], ['ps%d' % bk])
                cp('act', KcT[:, 0:2, :], PS[4][:, :].rearrange("p (h m) -> p h m", h=2), ['ps4'], ['KcT'])
                cp('dve', KcT[:, 2:4, :], PS[5][:, :].rearrange("p (h m) -> p h m", h=2), ['ps5'], ['KcT'])
                for hd in range(4):
                    for mt in range(2):
                        q = hd * 2 + mt
                        mm(PS[6][:, q * 2:q * 2 + 2], KcT[:, hd, mt * 128:(mt + 1) * 128], QM2[:, hd * NS + s_, :], True, True, ['KcT', 'QM2'], ['ps6'])
                act(flat2(E2), PS[6][:, 0:16], AF.Exp, ['ps6'], ['E2'], scale=SCALE)
                for hd in range(4):
                    for mt in range(2):
                        mm(PS[7][:, hd * 2:hd * 2 + 2], Vc[i][:, mt, hd * 128:(hd + 1) * 128], E2[:, hd * 2 + mt, :], mt == 0, mt == 1, ['Vc%d' % i, 'E2'], ['ps7'])
                mm(PS[7][:, 16:32], ones_ff, flat2(E2), True, True, ['E2', 'cst'], ['ps7'])
                cp('act', om32, PS[7][:, 0:32], ['ps7'], ['om32'])
                cs4 = om32[:, 16:32].rearrange("p (h m t) -> p h m t", h=4, m=2)
                tt('dve', dn, cs4[:, :, 0, 0], cs4[:, :, 1, 0], ALU.add, ['om32'], ['dn'])
                S.add('dve', lambda e: e.reciprocal(out=rd, in_=dn), reads=['dn'], writes=['rd'])
                tt('dve', mix[:, 4:8, s_], om32[:, 0:8].rearrange("p (h t) -> p h t", t=2)[:, :, 0], rd, ALU.mult, ['om32', 'rd'], ['mix'])
            S.barrier()
            A.top = top

        def a_mixer_s(l):
            top = A.top
            Q2 = A.f32(NC2 * 2).rearrange("p (c t) -> p c t", t=2)
            QM2 = A.f32(NC2 * 2).rearrange("p (c t) -> p c t", t=2)
            Ff = A.f32(NC2); Kf = A.f32(NC2); Gf = A.f32(NC2); Vf = A.f32(NC2); Of = A.f32(NC2)
            Ktok = A.f32(512); Vtok = A.f32(512)
            Vexp = A.f32(NS * 128)
            Vexp3 = Vexp.rearrange("p (s v) -> p s v", s=NS)
            S0 = A.f32(NS * 128).rearrange("p (s v) -> p s v", s=NS)
            sq = A.bf(NC2)
            S.add('pool', lambda e: e.memset(flat2(Q2), 0.0), writes=['Q2'])
            S.add('pool', lambda e: e.memset(flat2(QM2), 0.0), writes=['QM2'])
            norm_to_bf(hn, l, 0, ST)
            for pi in range(5):
                wpc, wk = load_w(w_in_a[l][:, pi * 512:(pi + 1) * 512], 8, 512)
                for hd in range(4):
                    b = hd % 2
                    cs = slice(hd * NS, (hd + 1) * NS)
                    for c in range(8):
                        mm(PS[b][:, 0:NS], wpc[:, c, hd * 128:(hd + 1) * 128], hn[:, c, 0:NS], c == 0, c == 7, [wk, 'hn0'], ['ps%d' % b])
                    if pi == 0:
                        act(Q2[:, cs, 0], PS[b][:, 0:NS], AF.Silu, ['ps%d' % b], ['Q2'])
                    elif pi == 1:
                        act(Ff[:, cs], PS[b][:, 0:NS], AF.Sigmoid, ['ps%d' % b], ['Ff'])
                    elif pi == 2:
                        cp('act', Vf[:, cs], PS[b][:, 0:NS], ['ps%d' % b], ['Vf'])
                    elif pi == 3:
                        act(Gf[:, cs], PS[b][:, 0:NS], AF.Silu, ['ps%d' % b], ['Gf'])
                    else:
                        cp('act', QM2[:, cs, 0], PS[b][:, 0:NS], ['ps%d' % b], ['QM2'])
            if l == 1:
                for hd in range(4):
                    cs = slice(hd * NS, (hd + 1) * NS)
                    ts('dve', Ff[:, cs], Ff[:, cs], oml1[:, hd:hd + 1], lb1[:, hd:hd + 1], ALU.mult, ALU.add, ['Ff', 'lb1', 'oml1'], ['Ff'])
            ts('dve', Kf, Ff, -1.0, 1.0, ALU.mult, ALU.add, ['Ff'], ['Kf'])
            for hd in range(4):
                cs = slice(hd * NS, (hd + 1) * NS)
                tr(PS[3][0:NS, 0:128], Kf[:, cs], ident, ['Kf', 'cst'], ['ps3'])
                cp('dve', Ktok[0:NS, hd * 128:(hd + 1) * 128], PS[3][0:NS, 0:128], ['ps3'], ['Ktok'])
                tr(PS[3][0:NS, 128:256], Vf[:, cs], ident, ['Vf', 'cst'], ['ps3'])
                cp('dve', Vtok[0:NS, hd * 128:(hd + 1) * 128], PS[3][0:NS, 128:256], ['ps3'], ['Vtok'])
            for hd in range(4):
                c0 = hd * NS
                dma('sp', S0, st0[l][:, hd].rearrange("s k v -> k s v"), [], ['S0'])
                tt('dve', Vexp3[0:NS], Vtok[0:NS, hd * 128:(hd + 1) * 128].unsqueeze(1).to_broadcast([NS, NS, 128]),
                   ident[0:NS, 0:NS].unsqueeze(2).to_broadcast([NS, NS, 128]), ALU.mult, ['Vtok', 'cst'], ['Vexp'])
                for q4 in range(NS // 4):
                    b = 4 + q4 % 2
                    mm(PS[b][:, :], Ktok[0:NS, hd * 128:(hd + 1) * 128], Vexp[0:NS, q4 * 512:(q4 + 1) * 512], True, True, ['Ktok', 'Vexp'], ['ps%d' % b])
                    for s4 in range(4):
                        s_ = q4 * 4 + s4
                        stt(S0[:, s_, :], S0[:, s_, :], Ff[:, c0 + s_:c0 + s_ + 1], PS[b][:, s4 * 128:(s4 + 1) * 128], ALU.mult, ALU.add,
                            ['S0', 'Ff', 'ps%d' % b], ['S0'])
                dma('sp', sts[l][:, hd].rearrange("s k v -> k s v"), S0, ['S0'], [])
                for s_ in range(NS):
                    col = c0 + s_
                    mm(PS[6][:, 2 * col:2 * col + 2], S0[:, s_, :], Q2[:, col, :], True, True, ['S0', 'Q2'], ['ps6'])
            cp('act', Of, PS[6][:, 0:2 * NC2].rearrange("p (c t) -> p c t", t=2)[:, :, 0], ['ps6'], ['Of'])
            act(sq, Of, AF.Square, ['Of'], ['sq'])
            mm(PS[7][:, 0:NC2], onesb, sq, True, True, ['sq', 'onesb'], ['ps7'])
            act(tmpf[:, 0:NC2], PS[7][:, 0:NC2], AF.Sqrt, ['ps7'], ['tmpf'], bias=EPS, scale=1.0 / 128)
            S.add('dve', lambda e: e.reciprocal(out=rstd[:, 0:NC2], in_=tmpf[:, 0:NC2]), reads=['tmpf'], writes=['rstd'])
            stt(Of, Of, hgn[:, l:l + 1], rstd[:, 0:NC2], ALU.mult, ALU.mult, ['Of', 'rstd', 'prm'], ['Of'])
            tt('dve', mix[:, 0:4, 0:NS], Of.rearrange("p (h s) -> p h s", h=4), Gf.rearrange("p (h s) -> p h s", h=4), ALU.mult, ['Of', 'Gf'], ['mix'])
            mem_attn_s(l, QM2)
            S.barrier()
            A.top = top

        def b_mixer_s(l, idxt):
            top = A.top
            lb_ = l - 2
            Q2 = A.f32(NC2 * 2).rearrange("p (c t) -> p c t", t=2)
            QM2 = A.f32(NC2 * 2).rearrange("p (c t) -> p c t", t=2)
            KP = [A.f32(512) for i in range(4)]
            KT = [A.f32(512) for i in range(4)]
            VP = [A.f32(512) for i in range(4)]
            ZB = A.f32(64); EE = A.f32(64); SP = A.f32(64); CR = A.f32(64)
            TLCS = A.f32(128)
            W2 = A.f32(128).rearrange("p (c t) -> p c t", t=2)
            OSs = A.f32(128)
            os4 = A.f32(4)
            r3 = lambda ap: ap.rearrange("p (j h) -> p j h", h=4)
            S.add('pool', lambda e: e.memset(flat2(Q2), 0.0), writes=['Q2'])
            S.add('pool', lambda e: e.memset(flat2(QM2), 0.0), writes=['QM2'])
            S.add('pool', lambda e: e.memset(flat2(W2), 0.0), writes=['W2'])
            S.add('pool', lambda e: e.memset(CR, 0.0), writes=['CR'])
            norm_to_bf(hn, l, 0, ST)
            for pi in range(2):
                wpc, wk = load_w(w_in_b[lb_][:, pi * 512:(pi + 1) * 512], 8, 512)
                for hd in range(4):
                    b = hd % 2
                    cs = slice(hd * NS, (hd + 1) * NS)
                    for c in range(8):
                        mm(PS[b][:, 0:NS], wpc[:, c, hd * 128:(hd + 1) * 128], hn[:, c, 0:NS], c == 0, c == 7, [wk, 'hn0'], ['ps%d' % b])
                    cp('act', (Q2 if pi == 0 else QM2)[:, cs, 0], PS[b][:, 0:NS], ['ps%d' % b], ['Q2' if pi == 0 else 'QM2'])
            n = 0
            for s_ in range(NS):
                for j in range(16):
                    i = n % 4
                    n += 1
                    col = s_ * 16 + j
                    S.add('pool', lambda e, i=i, col=col: e.indirect_dma_start(
                        out=KP[i], out_offset=None, in_=pool_k, in_offset=bass.IndirectOffsetOnAxis(ap=idxt[:, col:col + 1], axis=0)),
                        reads=['idxt'], writes=['KP%d' % i], dma=True)
                    for hd in range(4):
                        tr(PS[4][:, hd * 128:(hd + 1) * 128], KP[i][:, hd * 128:(hd + 1) * 128], ident, ['KP%d' % i, 'cst'], ['ps4'])
                    cp('act' if j % 2 else 'dve', KT[i], PS[4][:, :], ['ps4'], ['KT%d' % i])
                    for hd in range(4):
                        q = j * 4 + hd
                        mm(PS[6][:, 2 * q:2 * q + 2], KT[i][:, hd * 128:(hd + 1) * 128], Q2[:, hd * NS + s_, :], True, True, ['KT%d' % i, 'Q2'], ['ps6'])
                Zv = PS[6][:, 0:128].rearrange("p (j h t) -> p j h t", j=16, h=4)[:, :, :, 0]
                stt(r3(ZB), Zv, SCALE, sbb[:, lb_ * 4:lb_ * 4 + 4].unsqueeze(1).to_broadcast([128, 16, 4]), ALU.mult, ALU.add, ['ps6', 'prm'], ['ZB'])
                act(EE, ZB, AF.Exp, ['ZB'], ['EE'])
                act(SP, EE, AF.Ln, ['EE'], ['SP'], bias=1.0)
                mm(PS[7][:, 0:64], U_ff, SP, True, True, ['SP', 'cst'], ['ps7'])
                mm(PS[7][:, 64:128], ones_ff, SP, True, True, ['SP', 'cst'], ['ps7'])
                cp('act', TLCS, PS[7][:, 0:128], ['ps7'], ['TLCS'])
                CS3 = r3(TLCS[:, 64:128]); CR3 = r3(CR)
                for j in range(14, -1, -1):
                    tt('dve', CR3[:, j, :], CR3[:, j + 1, :], CS3[:, j + 1, :], ALU.add, ['CR', 'TLCS'], ['CR'])
                tt('dve', ZB, ZB, SP, ALU.subtract, ['ZB', 'SP'], ['ZB'])
                tt('dve', ZB, ZB, TLCS[:, 0:64], ALU.subtract, ['ZB', 'TLCS'], ['ZB'])
                tt('dve', ZB, ZB, CR, ALU.subtract, ['ZB', 'CR'], ['ZB'])
                act(W2[:, :, 0], ZB, AF.Exp, ['ZB'], ['W2'])
                for j in range(16):
                    i = n % 4
                    n += 1
                    col = s_ * 16 + j
                    S.add('pool', lambda e, i=i, col=col: e.indirect_dma_start(
                        out=VP[i], out_offset=None, in_=pool_v, in_offset=bass.IndirectOffsetOnAxis(ap=idxt[:, col:col + 1], axis=0)),
                        reads=['idxt'], writes=['VP%d' % i], dma=True)
                    for hd in range(4):
                        q = j * 4 + hd
                        mm(PS[5][:, 2 * q:2 * q + 2], VP[i][:, hd * 128:(hd + 1) * 128], W2[:, q, :], True, True, ['VP%d' % i, 'W2'], ['ps5'])
                cp('act', OSs, PS[5][:, 0:128], ['ps5'], ['OSs'])
                S.add('dve', lambda e: e.tensor_reduce(out=os4, in_=OSs.rearrange("p (j h t) -> p h j t", j=16, h=4)[:, :, :, 0], axis=AX, op=ALU.add),
                      reads=['OSs'], writes=['os4'])
                cp('dve', mix[:, 0:4, s_], os4, ['os4'], ['mix'])
            mem_attn_s(l, QM2)
            S.barrier()
            A.top = top

        def kv_proj_s():
            top = A.top
            ko = A.f32(512)
            norm_to_bf(hn, 0, 0, ST, gvec=gkv)
            for kv in range(2):
                wpc, wk = load_w(w_kv[:, kv * 512:(kv + 1) * 512], 8, 512)
                for c in range(8):
                    mm(PS[0][0:NS, :], hn[:, c, 0:NS], wpc[:, c, :], c == 0, c == 7, ['hn0', wk], ['ps0'])
                cp('act', ko[0:NS, :], PS[0][0:NS, :], ['ps0'], ['ko'])
                dma('sp', sbk_s if kv == 0 else sbv_s, ko[0:NS, :], ['ko'], [])
            S.barrier()
            A.top = top

        def sample_group():
            top = A.top
            idxt = A.i32(NS * 16)
            idf = A.f32(NS * 16)
            xraw = A.f32(1024)
            dma('sp', idxt, ptab.partition_broadcast(128), [], ['idxt'])
            cp('dve', idf, idxt, ['idxt'], ['idf'])
            ts('dve', idf, idf, 128.0, iota_c, ALU.mult, ALU.add, ['idf', 'prm'], ['idf'])
            cp('dve', idxt, idf, ['idf'], ['idxt'])
            dma('sp', xraw[0:NS, :], xs, [], ['xraw'])
            for c in range(8):
                b = c % 2
                tr(PS[b][:, 0:NS], xraw[0:NS, c * 128:(c + 1) * 128], ident[0:NS, 0:NS], ['xraw', 'cst'], ['ps%d' % b])
                cp('dve' if c % 2 else 'act', h[:, c, 0:NS], PS[b][:, 0:NS], ['ps%d' % b], ['h0'])
            S.barrier()
            for l in range(4):
                if l < 2:
                    a_mixer_s(l)
                else:
                    b_mixer_s(l, idxt)
                out_proj(l, ST)
                ffn(l, ST)
                if l == 1:
                    kv_proj_s()
            for c in range(8):
                b = c % 2
                tr(PS[b][0:NS, 0:128], h[:, c, 0:NS], ident, ['h0', 'cst'], ['ps%d' % b])
                cp('dve' if c % 2 else 'act', xraw[0:NS, c * 128:(c + 1) * 128], PS[b][0:NS, 0:128], ['ps%d' % b], ['xraw'])
            dma('sp', ys, xraw[0:NS, :], ['xraw'], [])
            S.barrier()
            A.top = top

        if STOP != -2:
            mem_setup()
        if SAMPLE:
            sample_group()
        TT = T // 512
        PT = [(i * 512, 512) for i in range(TT)]
        for g in range(NG if STOP != -1 else 0):
            top = A.top
            xraw = [A.f32(1024) for i in range(2)]
            for t in range(T // 128):
                i = t % 2
                r0 = g * T + t * 128
                dma('sp', xraw[i], xp[r0:r0 + 128, :], [], ['xraw%d' % i])
                for c in range(8):
                    b = c % 2
                    tr(PS[b][:, 0:128], xraw[i][:, c * 128:(c + 1) * 128], ident, ['xraw%d' % i, 'cst'], ['ps%d' % b])
                    cp('dve' if c % 2 else 'act', h[:, c, t * 128:(t + 1) * 128], PS[b][:, 0:128], ['ps%d' % b], ['h%d' % (t // 4)])
            S.barrier()
            A.top = top
            for l in range(4):
                if STOP <= 1 + 3 * l:
                    break
                if l < 2:
                    a_mixer(l, g, TT)
                else:
                    b_mixer(l, g, TT)
                if STOP <= 2 + 3 * l:
                    break
                out_proj(l, PT)
                if STOP <= 3 + 3 * l:
                    break
                ffn(l, PT)
                if l == 1:
                    kv_proj(g, TT)
            top = A.top
            yraw = [A.f32(1024) for i in range(2)]
            for t in range(T // 128):
                i = t % 2
                r0 = g * T + t * 128
                for c in range(8):
                    b = c % 2
                    tr(PS[b][:, 0:128], h[:, c, t * 128:(t + 1) * 128], ident, ['h%d' % (t // 4), 'cst'], ['ps%d' % b])
                    cp('dve' if c % 2 else 'act', yraw[i][:, c * 128:(c + 1) * 128], PS[b][:, 0:128], ['ps%d' % b], ['yraw%d' % i])
                dma('sp', yp[r0:r0 + 128, :], yraw[i], ['yraw%d' % i], [])
            S.barrier()
            A.top = top
        S.emit()
    return nc


_NC = None


def make_params(g_norm, hg_lb, hg_norm, sb_bias, g_kv):
    params = np.zeros((128, 160), np.float32)
    params[:, 0:128] = g_norm.reshape(4, 4, 8, 128).transpose(3, 0, 1, 2).reshape(128, 128)
    params[:, 128:136] = hg_lb.reshape(2, 4, 128).transpose(2, 0, 1).reshape(128, 8)
    params[:, 136:138] = hg_norm.T
    params[:, 138:146] = np.broadcast_to(sb_bias.reshape(1, 8), (128, 8))
    params[:, 146:154] = g_kv.reshape(8, 128).T
    params[:, 154] = np.arange(128)
    return params


def kernel(x_prompt, x_sample, mem_prompt, cache_sb_k, cache_sb_v, cache_mem_k, cache_mem_v, state_hgrn,
           page_table, g_norm, w_in_a, hg_lb, hg_norm, w_in_b, sb_bias, g_kv, w_kv, w_mem_kv, w_o, w_gu, w_down):
    global _NC
    f = lambda a: np.ascontiguousarray(np.asarray(a, dtype=np.float32))
    if _NC is None:
        _NC = build()
    nc = _NC
    consts = host_consts()
    g_norm = f(g_norm); hg_lb = f(hg_lb); hg_norm = f(hg_norm); sb_bias = f(sb_bias); g_kv = f(g_kv)
    params = make_params(g_norm, hg_lb, hg_norm, sb_bias, g_kv)
    x_prompt = f(x_prompt); mem_prompt = f(mem_prompt); x_sample = f(x_sample)
    npool = 2560
    if cache_sb_k is None:
        pk = np.zeros((npool * 128, 512), np.float32); pv = pk
        cache_mem_k = np.zeros((4, 128, 256, 4, 128), np.float32); cache_mem_v = cache_mem_k
        state_hgrn = np.zeros((2, 128, 4, 128, 128), np.float32)
        page_table = np.zeros((128, 16), np.int32)
    else:
        pk = f(cache_sb_k).reshape(npool * 128, 512); pv = f(cache_sb_v).reshape(npool * 128, 512)
    cache_mem_k = f(cache_mem_k); cache_mem_v = f(cache_mem_v); state_hgrn = f(state_hgrn)
    page_table = np.ascontiguousarray(np.asarray(page_table, dtype=np.int32))
    shared = {"consts": consts, "params": params, "w_in_a": f(w_in_a), "w_in_b": f(w_in_b), "w_kv": f(w_kv),
              "w_mem": f(w_mem_kv), "w_o": f(w_o), "w_gu": f(w_gu), "w_down": f(w_down), "pool_k": pk, "pool_v": pv}
    in_maps = []
    for c in range(NCORES):
        b = c % 2
        sl = slice(c * NS, (c + 1) * NS)
        m = dict(shared)
        m["xp"] = x_prompt[b]
        m["xs"] = np.ascontiguousarray(x_sample[sl, 0, :])
        m["memp"] = mem_prompt[b]
        m["cmk"] = np.ascontiguousarray(cache_mem_k[:, sl]).reshape(4, NS, 256, 512)
        m["cmv"] = np.ascontiguousarray(cache_mem_v[:, sl]).reshape(4, NS, 256, 512)
        m["st0"] = np.ascontiguousarray(state_hgrn[:, sl])
        m["ptab"] = np.ascontiguousarray(page_table[sl]).reshape(1, NS * 16)
        in_maps.append(m)
    res = run_bass_kernel_spmd(nc, in_maps, core_ids=list(range(NCORES)))
    r = list(res.results)
    while len(r) < 8:
        r.append(r[0])
    y_prompt = np.stack([r[0]["yp"], r[1]["yp"]]).astype(np.float32)
    st_p = np.stack([r[0]["stp"], r[1]["stp"]], axis=1).astype(np.float32)
    sbk = np.stack([r[0]["sbk_p"], r[1]["sbk_p"]]).reshape(2, SEQ, 4, 128).astype(np.float32)
    sbv = np.stack([r[0]["sbv_p"], r[1]["sbv_p"]]).reshape(2, SEQ, 4, 128).astype(np.float32)
    mk = np.stack([r[0]["mk_p"], r[1]["mk_p"]], axis=1).reshape(4, 2, 256, 4, 128).astype(np.float32)
    mv = np.stack([r[0]["mv_p"], r[1]["mv_p"]], axis=1).reshape(4, 2, 256, 4, 128).astype(np.float32)
    y_sample = np.concatenate([r[c]["ys"] for c in range(8)], axis=0).reshape(128, 1, D).astype(np.float32)
    st_s = np.concatenate([r[c]["sts"] for c in range(8)], axis=1).astype(np.float32)
    sbk_s = np.concatenate([r[c]["sbk_s"] for c in range(8)], axis=0).reshape(128, 1, 4, 128).astype(np.float32)
    sbv_s = np.concatenate([r[c]["sbv_s"] for c in range(8)], axis=0).reshape(128, 1, 4, 128).astype(np.float32)
    return (y_prompt, y_sample, st_p, st_s, sbk, sbv, sbk_s, sbv_s, mk, mv)
```
